# Optimizing a Trainium2 kernel written in Bass

```python
import math
import jax, jax.numpy as jnp
from jax import lax
import numpy as np

D_MODEL = 2048
BATCH = 8
SEQ = 2048
DEPTH = 1

N_DIR = 2
N_BRANCH = 2
MLSTM_HEADS = 4
MLSTM_V_DIM = 256
MLSTM_QK_DIM = 128
MLSTM_W = MLSTM_HEADS * MLSTM_V_DIM
MLSTM_QK_W = MLSTM_HEADS * MLSTM_QK_DIM
MLSTM_CHUNK = 64
DELTA_HEADS = 8
DELTA_HEAD_DIM = 128
DELTA_W = DELTA_HEADS * DELTA_HEAD_DIM
DELTA_CONV = 5
DELTA_CHUNK = 64
D_FF = 4 * D_MODEL
RMS_EPS = 1e-6
L2_EPS = 1e-6
IN_SPLITS = (MLSTM_QK_W, MLSTM_QK_W, MLSTM_W, MLSTM_W,
             N_DIR * MLSTM_HEADS, N_DIR * MLSTM_HEADS,
             3 * DELTA_W, DELTA_W, N_DIR * DELTA_HEADS, N_DIR * DELTA_HEADS,
             N_BRANCH * D_MODEL)
D_IN = sum(IN_SPLITS)

kernel_name = "bidir_mlstm_gdn_hybrid_layer"


def _rms_norm(x, g):
    xf = x.astype(jnp.float32)
    y = xf * lax.rsqrt(jnp.mean(xf * xf, axis=-1, keepdims=True) + RMS_EPS)
    return (y * g.astype(jnp.float32)).astype(x.dtype)


def _heads(t, n):
    b, s, w = t.shape
    return t.reshape(b, s, n, w // n).transpose(0, 2, 1, 3)


def _merge_heads(t):
    b, n, s, d = t.shape
    return t.transpose(0, 2, 1, 3).reshape(b, s, n * d)


def _dir_heads(t, n):
    b, s, _ = t.shape
    return t.astype(jnp.float32).reshape(b, s, N_DIR, n).transpose(2, 0, 3, 1)


def _flip(t):
    return jnp.flip(t, axis=2)


def _to_chunks(t, size):
    s = t.shape[2]
    t = t.reshape(t.shape[:2] + (s // size, size) + t.shape[3:])
    return jnp.moveaxis(t, 2, 0)


def _from_chunks(t):
    t = jnp.moveaxis(t, 0, 2)
    return t.reshape(t.shape[:2] + (t.shape[2] * t.shape[3],) + t.shape[4:])


def _l2_normalize(t):
    return t * lax.rsqrt(jnp.sum(t * t, axis=-1, keepdims=True) + L2_EPS)


def _mlstm_chunk_scan(q, k, v, i_pre, logf):
    b_, h_, s_, dk = q.shape
    dv = v.shape[-1]
    L = MLSTM_CHUNK
    qc, kc, vc, ic, fc = (_to_chunks(t, L) for t in (q, k, v, i_pre, logf))
    causal = jnp.tril(jnp.ones((L, L), dtype=bool))

    def step(carry, inp):
        c_st, n_st, m_st = carry
        qj, kj, vj, ij, fj = inp
        bcum = jnp.cumsum(fj, axis=-1)
        log_intra = jnp.where(causal, bcum[..., :, None] - bcum[..., None, :] + ij[..., None, :], -jnp.inf)
        log_inter = bcum + m_st[..., None]
        m_row = jnp.maximum(log_inter, jnp.max(log_intra, axis=-1))
        w_intra = jnp.exp(log_intra - m_row[..., None])
        w_inter = jnp.exp(log_inter - m_row)
        scores = jnp.einsum('bhld,bhsd->bhls', qj, kj) * w_intra
        num = (jnp.einsum('bhls,bhsv->bhlv', scores, vj)
               + w_inter[..., None] * jnp.einsum('bhld,bhdv->bhlv', qj, c_st))
        den = jnp.sum(scores, axis=-1) + w_inter * jnp.einsum('bhld,bhd->bhl', qj, n_st)
        h_out = num / jnp.maximum(jnp.abs(den), jnp.exp(-m_row))[..., None]
        b_last = bcum[..., -1]
        log_state = b_last[..., None] - bcum + ij
        m_new = jnp.maximum(b_last + m_st, jnp.max(log_state, axis=-1))
        w_state = jnp.exp(log_state - m_new[..., None])
        decay = jnp.exp(b_last + m_st - m_new)
        c_new = decay[..., None, None] * c_st + jnp.einsum('bhs,bhsd,bhsv->bhdv', w_state, kj, vj)
        n_new = decay[..., None] * n_st + jnp.einsum('bhs,bhsd->bhd', w_state, kj)
        return (c_new, n_new, m_new), h_out

    carry0 = (jnp.zeros((b_, h_, dk, dv), q.dtype),
              jnp.zeros((b_, h_, dk), q.dtype),
              jnp.full((b_, h_), -jnp.inf, q.dtype))
    _, h_all = lax.scan(step, carry0, (qc, kc, vc, ic, fc))
    return _from_chunks(h_all)


def _gated_delta_chunk_scan(q, k, v, g, beta):
    b_, h_, s_, dk = q.shape
    dv = v.shape[-1]
    L = DELTA_CHUNK
    qc, kc, vc, gc, bc = (_to_chunks(t, L) for t in (q, k, v, g, beta))
    gc = jnp.cumsum(gc, axis=-1)
    lower = jnp.tril(jnp.ones((L, L), dtype=bool))
    strict = jnp.tril(jnp.ones((L, L), dtype=bool), -1)
    gamma = jnp.exp(jnp.where(lower, gc[..., :, None] - gc[..., None, :], -jnp.inf))
    kb = kc * bc[..., None]
    a_mat = jnp.where(strict, jnp.einsum('nbhid,nbhjd->nbhij', kb, kc) * gamma, 0.0) + jnp.eye(L, dtype=q.dtype)
    u = lax.linalg.triangular_solve(a_mat, vc * bc[..., None], left_side=True, lower=True, unit_diagonal=True)
    w = lax.linalg.triangular_solve(a_mat, kb * jnp.exp(gc)[..., None], left_side=True, lower=True, unit_diagonal=True)
    attn = jnp.einsum('nbhid,nbhjd->nbhij', qc, kc) * gamma
    qg = qc * jnp.exp(gc)[..., None]
    g_last = gc[..., -1]
    kd = kc * jnp.exp(g_last[..., None] - gc)[..., None]

    def step(state, inp):
        qg_c, kd_c, u_c, w_c, attn_c, gl = inp
        v_new = u_c - jnp.einsum('bhld,bhdv->bhlv', w_c, state)
        o = jnp.einsum('bhld,bhdv->bhlv', qg_c, state) + jnp.einsum('bhls,bhsv->bhlv', attn_c, v_new)
        state = jnp.exp(gl)[..., None, None] * state + jnp.einsum('bhld,bhlv->bhdv', kd_c, v_new)
        return state, o

    s0 = jnp.zeros((b_, h_, dk, dv), q.dtype)
    _, o_all = lax.scan(step, s0, (qg, kd, u, w, attn, g_last))
    return _from_chunks(o_all)


def _mlstm_mixer(q, k, v, o_pre, i_pre, f_pre, i_bias, f_bias, norm_g):
    f32 = jnp.float32
    qh = _heads(q, MLSTM_HEADS).astype(f32)
    kh = _heads(k, MLSTM_HEADS).astype(f32) * (MLSTM_QK_DIM ** -0.5)
    vh = _heads(v, MLSTM_HEADS).astype(f32)
    ig = _dir_heads(i_pre, MLSTM_HEADS) + i_bias.astype(f32)[:, None, :, None]
    lf = jax.nn.log_sigmoid(_dir_heads(f_pre, MLSTM_HEADS) + f_bias.astype(f32)[:, None, :, None])
    h_fwd = _mlstm_chunk_scan(qh, kh, vh, ig[0], lf[0])
    h_bwd = _flip(_mlstm_chunk_scan(_flip(qh), _flip(kh), _flip(vh), _flip(ig[1]), _flip(lf[1])))
    h = h_fwd + h_bwd
    h = h * lax.rsqrt(jnp.mean(h * h, axis=-1, keepdims=True) + RMS_EPS)
    h = h * norm_g.astype(f32).reshape(MLSTM_HEADS, 1, MLSTM_V_DIM)
    return (jax.nn.sigmoid(o_pre.astype(f32)) * _merge_heads(h)).astype(q.dtype)


def _gated_deltanet_mixer(qkv, z, a_pre, b_pre, conv_w, a_log, dt_bias, norm_g):
    f32 = jnp.float32
    ch = qkv.shape[-1]
    pad = DELTA_CONV // 2
    qkv = lax.conv_general_dilated(qkv, conv_w[:, None, :], window_strides=(1,), padding=[(pad, pad)],
                                   dimension_numbers=('NWC', 'WIO', 'NWC'), feature_group_count=ch)
    qkv = jax.nn.silu(qkv)
    q, k, v = jnp.split(qkv, 3, axis=-1)
    qh = _l2_normalize(_heads(q, DELTA_HEADS).astype(f32)) * (DELTA_HEAD_DIM ** -0.5)
    kh = _l2_normalize(_heads(k, DELTA_HEADS).astype(f32))
    vh = _heads(v, DELTA_HEADS).astype(f32)
    a = _dir_heads(a_pre, DELTA_HEADS) + dt_bias.astype(f32)[:, None, :, None]
    g = -jnp.exp(a_log.astype(f32))[:, None, :, None] * jax.nn.softplus(a)
    beta = jax.nn.sigmoid(_dir_heads(b_pre, DELTA_HEADS))
    o_fwd = _gated_delta_chunk_scan(qh, kh, vh, g[0], beta[0])
    o_bwd = _flip(_gated_delta_chunk_scan(_flip(qh), _flip(kh), _flip(vh), _flip(g[1]), _flip(beta[1])))
    o = o_fwd + o_bwd
    o = o * lax.rsqrt(jnp.mean(o * o, axis=-1, keepdims=True) + RMS_EPS) * norm_g.astype(f32)
    return (_merge_heads(o) * jax.nn.silu(z.astype(f32))).astype(z.dtype)


def setup_inputs(seed: int = 0) -> dict:
    key = jax.random.key(seed)
    ks = jax.random.split(key, 20)
    f32 = jnp.float32

    def dense(k, fan_in, fan_out):
        return jax.random.normal(k, (DEPTH, fan_in, fan_out), f32) * (fan_in ** -0.5)

    def gain(k, n):
        return 1.0 + 0.02 * jax.random.normal(k, (DEPTH, n), f32)

    dt = jnp.exp(jax.random.uniform(ks[8], (DEPTH, N_DIR, DELTA_HEADS), f32, math.log(1e-3), math.log(1e-1)))
    return {
        "x": jax.random.normal(ks[0], (BATCH, SEQ, D_MODEL), f32),
        "norm1_g": gain(ks[1], D_MODEL),
        "w_in": dense(ks[2], D_MODEL, D_IN),
        "mlstm_i_bias": 0.1 * jax.random.normal(ks[3], (DEPTH, N_DIR, MLSTM_HEADS), f32),
        "mlstm_f_bias": jax.random.uniform(ks[4], (DEPTH, N_DIR, MLSTM_HEADS), f32, 3.0, 6.0),
        "mlstm_norm_g": gain(ks[5], MLSTM_W),
        "delta_conv_w": jax.random.normal(ks[6], (DEPTH, DELTA_CONV, 3 * DELTA_W), f32) * (DELTA_CONV ** -0.5),
        "delta_a_log": jnp.log(jax.random.uniform(ks[7], (DEPTH, N_DIR, DELTA_HEADS), f32, 1.0, 16.0)),
        "delta_dt_bias": dt + jnp.log(-jnp.expm1(-dt)),
        "delta_norm_g": gain(ks[9], DELTA_HEAD_DIM),
        "w_branch_m": dense(ks[10], MLSTM_W, D_MODEL),
        "w_branch_d": dense(ks[11], DELTA_W, D_MODEL),
        "w_out": dense(ks[12], D_MODEL, D_MODEL),
        "norm2_g": gain(ks[13], D_MODEL),
        "w_ff1": dense(ks[14], D_MODEL, D_FF),
        "w_ff2": dense(ks[15], D_FF, D_MODEL),
        "norm_f_g": 1.0 + 0.02 * jax.random.normal(ks[16], (D_MODEL,), f32),
    }


def reference(x, norm1_g, w_in, mlstm_i_bias, mlstm_f_bias, mlstm_norm_g, delta_conv_w, delta_a_log,
              delta_dt_bias, delta_norm_g, w_branch_m, w_branch_d, w_out, norm2_g, w_ff1, w_ff2, norm_f_g):
    split_points = np.cumsum(IN_SPLITS)[:-1].tolist()
    for l in range(DEPTH):
        h = _rms_norm(x, norm1_g[l])
        proj = h @ w_in[l]
        (m_q, m_k, m_v, m_o, m_i, m_f, d_qkv, d_z, d_a, d_b, gates) = jnp.split(proj, split_points, axis=-1)
        y_m = _mlstm_mixer(m_q, m_k, m_v, m_o, m_i, m_f, mlstm_i_bias[l], mlstm_f_bias[l], mlstm_norm_g[l])
        y_d = _gated_deltanet_mixer(d_qkv, d_z, d_a, d_b, delta_conv_w[l], delta_a_log[l], delta_dt_bias[l],
                                    delta_norm_g[l])
        g_m, g_d = jnp.split(gates, 2, axis=-1)
        mixed = jax.nn.sigmoid(g_m) * (y_m @ w_branch_m[l]) + jax.nn.sigmoid(g_d) * (y_d @ w_branch_d[l])
        x = x + mixed @ w_out[l]
        h = _rms_norm(x, norm2_g[l])
        x = x + jnp.square(jax.nn.relu(h @ w_ff1[l])) @ w_ff2[l]
    return _rms_norm(x, norm_f_g)
```

```python
import contextlib
import numpy as np
import ml_dtypes
import concourse.bass as bass
import concourse.mybir as mybir
from concourse.bass_utils import run_bass_kernel_spmd

F32 = mybir.dt.float32
BF16 = mybir.dt.bfloat16
AF = mybir.ActivationFunctionType
ALU = mybir.AluOpType

CENG = ("pe", "act", "dve", "pool")
ALLENG = ("pe", "act", "dve", "pool", "sp")

S = 2048
D = 2048
NT = 16
DIN = 11312
DFF = 8192
BIG = 30000.0


class Buf:
    __slots__ = ("writers", "readers", "war")

    def __init__(self):
        self.writers = []
        self.readers = []
        self.war = []


class DSem:
    __slots__ = ("sem", "count")

    def __init__(self, sem):
        self.sem = sem
        self.count = 0


class Prog:
    def __init__(self):
        self.streams = {e: [] for e in ALLENG}
        self.dsems = []

    def _deps(self, eng, reads, writes, pwrites, is_dma=False):
        deps = []
        if is_dma:
            eng = None
        for b in reads:
            deps.extend(b.writers)
        for b in writes:
            for ev in b.writers + b.readers:
                if not (ev[0] == "c" and ev[1] == eng):
                    deps.append(ev)
        for b in pwrites:
            for ev in (b.readers if b.readers else b.war):
                if not (ev[0] == "c" and ev[1] == eng):
                    deps.append(ev)
        return deps

    @staticmethod
    def _reduce(evs):
        best = {}
        for ev in evs:
            key = (ev[0], ev[1] if ev[0] == "c" else id(ev[1]))
            if key not in best or best[key][2] < ev[2]:
                best[key] = ev
        return list(best.values())

    def _commit(self, ev, reads, writes, pwrites):
        for b in reads:
            b.readers.append(ev)
            if len(b.readers) > 8:
                b.readers = self._reduce(b.readers)
        for b in writes:
            b.writers = [ev]
            b.readers = []
            b.war = []
        for b in pwrites:
            if b.readers:
                b.war = [r for r in b.readers if r != ev]
                b.writers = [ev]
                b.readers = []
            else:
                b.writers.append(ev)
                b.writers = self._reduce(b.writers)

    def op(self, eng, fn, reads=(), writes=(), pwrites=()):
        deps = self._reduce(self._deps(eng, reads, writes, pwrites))
        st = self.streams[eng]
        idx = len(st)
        st.append({"fn": fn, "deps": deps, "sig": False, "dma": None})
        ev = ("c", eng, idx)
        self._commit(ev, reads, writes, pwrites)
        return ev

    def dma(self, queue, dsem, out_ap, in_ap, reads=(), writes=(), pwrites=()):
        deps = self._reduce(self._deps(queue, reads, writes, pwrites, True))
        st = self.streams[queue]
        dsem.count += 16
        ev = ("d", dsem, dsem.count)
        st.append({"fn": (lambda e, o=out_ap, i=in_ap: e.dma_start(out=o, in_=i)),
                   "deps": deps, "sig": False, "dma": dsem})
        self._commit(ev, reads, writes, pwrites)
        return ev

    def barrier(self):
        evs = []
        for e in CENG:
            st = self.streams[e]
            for i in range(len(st) - 1, -1, -1):
                if st[i]["dma"] is None and st[i]["fn"] is not None:
                    evs.append(("c", e, i))
                    break
        for d in self.dsems:
            if d.count:
                evs.append(("d", d, d.count))
        for e in ALLENG:
            self.streams[e].append({"fn": None, "deps": list(evs), "sig": False, "dma": None})

    def emit(self, nc, engsems):
        for e in ALLENG:
            for o in self.streams[e]:
                for ev in o["deps"]:
                    if ev[0] == "c":
                        self.streams[ev[1]][ev[2]]["sig"] = True
        sigidx = {}
        for e in ALLENG:
            c = 0
            arr = []
            for o in self.streams[e]:
                if o["sig"]:
                    c += 1
                arr.append(c)
            sigidx[e] = arr
        self.sigcounts = {e: (sigidx[e][-1] if sigidx[e] else 0) for e in ALLENG}
        streams = self.streams

        def replay(e):
            def run(eng):
                seen = {}
                for o in streams[e]:
                    need = {}
                    for ev in o["deps"]:
                        if ev[0] == "c":
                            key = ("c", ev[1])
                            val = sigidx[ev[1]][ev[2]]
                        else:
                            key = ("d", id(ev[1]))
                            val = ev[2]
                        if seen.get(key, 0) >= val:
                            continue
                        if need.get(key, (0, None))[0] < val:
                            need[key] = (val, ev)
                    for key, (val, ev) in need.items():
                        seen[key] = val
                        sem = engsems[ev[1]] if ev[0] == "c" else ev[1].sem
                        eng.wait_ge(sem, val)
                    if o["fn"] is None:
                        continue
                    ins = o["fn"](eng)
                    if o["dma"] is not None:
                        ins.then_inc(o["dma"].sem, 16)
                    elif o["sig"]:
                        ins.then_inc(engsems[e], 1)
            return run

        with nc.Block() as block:
            block.sync(replay("sp"))
            block.tensor(replay("pe"))
            block.scalar(replay("act"))
            block.vector(replay("dve"))
            block.gpsimd(replay("pool"))


class Arena:
    def __init__(self, ap, nwords):
        self.ap = ap
        self.n = nwords
        self.off = 0

    def reset(self):
        self.off = 0

    def f32(self, n):
        a = self.ap[:, self.off:self.off + n]
        self.off += n
        assert self.off <= self.n, ("arena overflow", self.off)
        return a

    def bf16(self, n):
        w = (n + 1) // 2
        a = self.ap[:, self.off:self.off + w].bitcast(BF16)
        self.off += w
        assert self.off <= self.n, ("arena overflow", self.off)
        return a


C_IDENT, C_ONES, C_TRIF, C_TRIB, C_BLK, C_HA, C_HB, C_MNF, C_MNB, C_MPF, C_MPB = [i * 128 for i in range(11)]
C_SEL8 = 11 * 128
C_SEL16 = C_SEL8 + 1024
NCST = C_SEL16 + 2048

P_G1, P_G2, P_IB, P_FB, P_AL, P_DT = 0, 16, 32, 40, 48, 64
P_CW = 80
P_NGM = 200
P_NGD = P_NGM + 1024
P_GF = P_NGD + 128
NPRM = P_GF + 2048


def _consts():
    p = np.arange(128)[:, None]
    j = np.arange(128)[None, :]
    same = (p // 64) == (j // 64)
    c = np.zeros((128, NCST), np.float32)
    c[:, C_IDENT:C_IDENT + 128] = np.eye(128)
    c[:, C_ONES:C_ONES + 128] = 1.0
    c[:, C_TRIF:C_TRIF + 128] = same & (p <= j)
    c[:, C_TRIB:C_TRIB + 128] = same & (p >= j)
    c[:, C_BLK:C_BLK + 128] = same
    c[:, C_HA:C_HA + 128] = (p < 64) & (j >= 0)
    c[:, C_HB:C_HB + 128] = (p >= 64) & (j >= 0)
    c[:, C_MNF:C_MNF + 128] = np.where(same & (p <= j), 0.0, -BIG)
    c[:, C_MNB:C_MNB + 128] = np.where(same & (p >= j), 0.0, -BIG)
    c[:, C_MPF:C_MPF + 128] = np.where(same & (j < p), 0.0, BIG)
    c[:, C_MPB:C_MPB + 128] = np.where(same & (j > p), 0.0, BIG)
    for k in range(8):
        c[k, C_SEL8 + k * 128:C_SEL8 + (k + 1) * 128] = 1.0
    for k in range(16):
        c[k, C_SEL16 + k * 128:C_SEL16 + (k + 1) * 128] = 1.0
    return c


def _params(inp):
    f = np.float32
    prm = np.zeros((128, NPRM), f)
    prm[:, P_G1:P_G1 + 16] = np.asarray(inp["norm1_g"], f).reshape(16, 128).T
    prm[:, P_G2:P_G2 + 16] = np.asarray(inp["norm2_g"], f).reshape(16, 128).T
    prm[:, P_IB:P_IB + 8] = np.asarray(inp["mlstm_i_bias"], f).reshape(1, 8)
    prm[:, P_FB:P_FB + 8] = np.asarray(inp["mlstm_f_bias"], f).reshape(1, 8)
    prm[:, P_AL:P_AL + 16] = np.asarray(inp["delta_a_log"], f).reshape(1, 16)
    prm[:, P_DT:P_DT + 16] = np.asarray(inp["delta_dt_bias"], f).reshape(1, 16)
    cw = np.asarray(inp["delta_conv_w"], f).reshape(5, 24, 128)
    prm[:, P_CW:P_CW + 120] = cw.transpose(2, 1, 0).reshape(128, 120)
    prm[:, P_NGM:P_NGM + 1024] = np.asarray(inp["mlstm_norm_g"], f).reshape(1, 1024)
    prm[:, P_NGD:P_NGD + 128] = np.asarray(inp["delta_norm_g"], f).reshape(1, 128)
    prm[:, P_GF:P_GF + 2048] = np.asarray(inp["norm_f_g"], f).reshape(1, 2048)
    return prm


class Kern:
    def __init__(self, nc, debug=False, upto=99):
        self.nc = nc
        self.debug = debug
        self.upto = upto
        self.P = Prog()
        self.dbg_names = []

    def ds(self):
        d = self.dspool[self.dsnext]
        self.dsnext += 1
        return d

    def scratch(self, name, shape, dt):
        kind = "ExternalOutput" if self.debug else "Internal"
        t = self.nc.dram_tensor(name, list(shape), dt, kind=kind).ap()
        if self.debug:
            self.dbg_names.append(name)
        return t

    def act(self, out, in_, func, reads, writes=(), pwrites=(), **kw):
        return self.P.op("act", lambda e: e.activation(out=out, in_=in_, func=func, **kw), reads, writes, pwrites)

    def tt(self, eng, out, in0, in1, op, reads, writes=(), pwrites=()):
        return self.P.op(eng, lambda e: e.tensor_tensor(out=out, in0=in0, in1=in1, op=op), reads, writes, pwrites)

    def stt(self, eng, out, in0, scalar, in1, op0, op1, reads, writes=(), pwrites=()):
        return self.P.op(eng, lambda e: e.scalar_tensor_tensor(out=out, in0=in0, scalar=scalar, in1=in1, op0=op0, op1=op1),
                         reads, writes, pwrites)

    def cp(self, eng, out, in_, reads, writes=(), pwrites=()):
        return self.P.op(eng, lambda e: e.tensor_copy(out=out, in_=in_), reads, writes, pwrites)

    def mm(self, out, lhsT, rhs, start, stop, reads, writes=(), pwrites=()):
        return self.P.op("pe", lambda e: e.matmul(out, lhsT=lhsT, rhs=rhs, start=start, stop=stop), reads, writes, pwrites)

    def tr(self, out, in_, ident, reads, writes=(), pwrites=()):
        return self.P.op("pe", lambda e: e.transpose(out=out, in_=in_, identity=ident), reads, writes, pwrites)

    def memset(self, eng, ap, val, writes=(), pwrites=()):
        return self.P.op(eng, lambda e: e.memset(ap, val), (), writes, pwrites)

    def load(self, dst, src, buf, dsem=None, queue="sp"):
        if dsem is None:
            dsem = self.ds()
        return self.P.dma(queue, dsem, dst, src, writes=[buf])

    def rsqrt_col(self, out, in_, scale, eps_ap, reads, buf):
        self.act(out, in_, AF.Sqrt, reads, writes=[buf], bias=eps_ap, scale=scale)
        self.P.op("dve", lambda e: e.reciprocal(out=out, in_=out), [buf], [buf])

    def wring_init(self, nbuf=3):
        A = self.A
        self.w_st = [A.f32(2048) for _ in range(nbuf)]
        self.w_bf = [A.bf16(2048) for _ in range(nbuf)]
        self.w_bst = [Buf() for _ in range(nbuf)]
        self.w_bbf = [Buf() for _ in range(nbuf)]
        self.w_ds = [self.ds() for _ in range(nbuf)]
        self.w_i = 0
        self.w_n = nbuf

    def wload(self, src, kc, n, fold=None, eng="dve"):
        i = self.w_i % self.w_n
        self.w_i += 1
        st = self.w_st[i][:, 0:kc * n].rearrange("p (c n) -> p c n", n=n)
        bf = self.w_bf[i][:, 0:kc * n].rearrange("p (c n) -> p c n", n=n)
        self.P.dma("sp", self.w_ds[i], st, src, writes=[self.w_bst[i]])
        if fold is None:
            self.cp(eng, bf, st, [self.w_bst[i]], [self.w_bbf[i]])
        else:
            self.tt(eng, bf, st, fold.unsqueeze(2).to_broadcast([128, kc, n]), ALU.mult, [self.w_bst[i]] + self.b_prm, [self.w_bbf[i]])
        return bf, self.w_bbf[i]

    def build(self):
        nc = self.nc
        with contextlib.ExitStack() as es:
            self.es = es
            self.io()
            NW = 48640
            arena = es.enter_context(nc.sbuf_tensor("arena", [128, NW], F32))
            self.A = Arena(arena, NW)
            self.psum = es.enter_context(nc.psum_tensor("psum", [128, 4096], F32))
            self.engsems = {e: es.enter_context(nc.semaphore("s_" + e)) for e in CENG}
            self.dspool = [DSem(es.enter_context(nc.semaphore("d%d" % i))) for i in range(20)]
            self.dsnext = 0
            self.P.dsems = self.dspool
            phases = [self.phase1_2, self.phase2b, self.phase3, self.phase4, self.phase5, self.phase67]
            for i, ph in enumerate(phases):
                if i >= self.upto:
                    break
                self.dsnext = 0
                ph()
                self.P.barrier()
            self.P.emit(nc, self.engsems)

    def io(self):
        nc = self.nc
        I = lambda n, s, dt=F32: nc.dram_tensor(n, list(s), dt, kind="ExternalInput").ap()
        self.x = I("x", [S, D])
        self.w_in = I("w_in", [D, DIN])
        self.w_bm = I("w_bm", [1024, D])
        self.w_bd = I("w_bd", [1024, D])
        self.w_out = I("w_out", [D, D])
        self.w_ff1 = I("w_ff1", [D, DFF])
        self.w_ff2 = I("w_ff2", [DFF, D])
        self.prm_d = I("prm", [128, NPRM])
        self.cst_d = I("cst", [128, NCST])
        self.cstb_d = I("cstb", [128, 128], BF16)
        self.y = nc.dram_tensor("y", [S, D], F32, kind="ExternalOutput").ap()
        sc = self.scratch
        self.S_qTm = sc("S_qTm", [4, 128, S], BF16)
        self.S_kTm = sc("S_kTm", [4, 128, S], BF16)
        self.S_km = sc("S_km", [4, 128, NT, 128], BF16)
        self.S_vm = sc("S_vm", [4, 128, NT, 256], BF16)
        self.S_som = sc("S_som", [4, 128, NT, 256], F32)
        self.S_gm = sc("S_gm", [128, NT, 16], F32)
        self.S_qkv = sc("S_qkv", [24, 128, S], F32)
        self.S_sz = sc("S_sz", [8, 128, NT, 128], F32)
        self.S_gd = sc("S_gd", [128, NT, 32], F32)
        self.S_sg = sc("S_sg", [32, 128, S], F32)
        self.S_qTd = sc("S_qTd", [8, 128, S], BF16)
        self.S_kTd = sc("S_kTd", [8, 128, S], BF16)
        self.S_kd = sc("S_kd", [8, 128, NT, 128], BF16)
        self.S_vd = sc("S_vd", [8, 128, NT, 128], BF16)
        self.S_ymT = sc("S_ymT", [8, 128, S], BF16)
        self.S_ydT = sc("S_ydT", [8, 128, S], BF16)
        self.S_mixT = sc("S_mixT", [16, 128, S], BF16)

    def common(self, need_prm=True):
        A = self.A
        A.reset()
        self.identb = A.bf16(128)
        self.b_identb = Buf()
        self.load(self.identb, self.cstb_d, self.b_identb)
        self.eps = A.f32(2)
        self.b_eps = Buf()
        self.memset("pool", self.eps[:, 0:1], 1e-6, pwrites=[self.b_eps])
        self.memset("pool", self.eps[:, 1:2], 1.0, pwrites=[self.b_eps])
        self.eps6 = self.eps[:, 0:1]
        self.one1 = self.eps[:, 1:2]

    def load_prm(self, off, n):
        t = self.A.f32(n)
        b = Buf()
        self.load(t, self.prm_d[:, off:off + n], b)
        return t, b

    def load_cst(self, off, n, rows=128):
        t = self.A.f32(n)
        b = Buf()
        self.load(t[0:rows, :], self.cst_d[0:rows, off:off + n], b)
        return t, b

    def phase1_2(self):
        P, A, ps = self.P, self.A, self.psum
        self.common()
        g1, b_g1 = self.load_prm(P_G1, 16)
        self.b_prm = [b_g1]
        xT = A.bf16(16 * S)
        xT3 = xT.rearrange("p (c t) -> p c t", t=S)
        b_xT = Buf()
        mark = A.off
        xin = [A.f32(D) for _ in range(2)]
        b_xin = [Buf(), Buf()]
        ds_x = [self.ds(), self.ds()]
        junk = A.f32(D)
        b_junk = Buf()
        xn = [A.bf16(D) for _ in range(2)]
        b_xn = [Buf(), Buf()]
        ssq = A.f32(16)
        rstd = A.f32(16)
        b_st = [Buf() for _ in range(16)]
        pT = [ps[:, 0:1024].bitcast(BF16), ps[:, 1024:2048].bitcast(BF16)]
        b_pT = [Buf(), Buf()]
        for t in range(NT):
            i = t % 2
            P.dma("sp", ds_x[i], xin[i], self.x[t * 128:(t + 1) * 128, :], writes=[b_xin[i]])
            self.act(junk, xin[i], AF.Square, [b_xin[i]], writes=[b_junk, b_st[t]], accum_out=ssq[:, t:t + 1])
            self.rsqrt_col(rstd[:, t:t + 1], ssq[:, t:t + 1], 1.0 / D, self.eps6, [b_st[t], self.b_eps], b_st[t])
            self.act(xn[i], xin[i], AF.Copy, [b_xin[i], b_st[t]], writes=[b_xn[i]], scale=rstd[:, t:t + 1])
            for dc in range(16):
                self.tr(pT[i][:, dc * 128:(dc + 1) * 128], xn[i][:, dc * 128:(dc + 1) * 128], self.identb,
                        [b_xn[i], self.b_identb], pwrites=[b_pT[i]])
            self.cp("dve", xT3[:, :, t * 128:(t + 1) * 128], pT[i].rearrange("p (c t) -> p c t", t=128),
                    [b_pT[i]], pwrites=[b_xT])
        P.barrier()
        A.off = mark
        self.wring_init(3)
        NS = 4
        stage = [A.f32(1024) for _ in range(NS)]
        b_stage = [Buf() for _ in range(NS)]
        ds_stage = [self.ds() for _ in range(NS)]
        preg = [ps[:, i * 1024:(i + 1) * 1024] for i in range(4)]
        b_preg = [Buf() for _ in range(4)]
        units = []
        qs = 128 ** -0.5
        for h in range(4):
            units.append(("fm", h * 128, 128, AF.Copy, 1.0, BF16, lambda hf, h=h: self.S_qTm[h][:, hf * 1024:(hf + 1) * 1024]))
        for h in range(4):
            units.append(("fm", 512 + h * 128, 128, AF.Copy, qs, BF16, lambda hf, h=h: self.S_kTm[h][:, hf * 1024:(hf + 1) * 1024]))
        for h in range(4):
            units.append(("tm", 512 + h * 128, 128, AF.Copy, qs, BF16, lambda hf, h=h: self.S_km[h][:, hf * 8:(hf + 1) * 8, :]))
        for u in range(8):
            units.append(("tm", 1024 + u * 128, 128, AF.Copy, 1.0, BF16,
                          lambda hf, u=u: self.S_vm[u // 2][:, hf * 8:(hf + 1) * 8, (u % 2) * 128:(u % 2) * 128 + 128]))
        for u in range(8):
            units.append(("tm", 2048 + u * 128, 128, AF.Sigmoid, 1.0, F32,
                          lambda hf, u=u: self.S_som[u // 2][:, hf * 8:(hf + 1) * 8, (u % 2) * 128:(u % 2) * 128 + 128]))
        units.append(("tm", 3072, 16, AF.Copy, 1.0, F32, lambda hf: self.S_gm[:, hf * 8:(hf + 1) * 8, :]))
        for u in range(24):
            units.append(("fm", 3088 + u * 128, 128, AF.Copy, 1.0, F32, lambda hf, u=u: self.S_qkv[u][:, hf * 1024:(hf + 1) * 1024]))
        for h in range(8):
            units.append(("tm", 6160 + h * 128, 128, AF.Silu, 1.0, F32, lambda hf, h=h: self.S_sz[h][:, hf * 8:(hf + 1) * 8, :]))
        units.append(("tm", 7184, 32, AF.Copy, 1.0, F32, lambda hf: self.S_gd[:, hf * 8:(hf + 1) * 8, :]))
        for u in range(32):
            units.append(("fm", 7216 + u * 128, 128, AF.Sigmoid, 1.0, F32, lambda hf, u=u: self.S_sg[u][:, hf * 1024:(hf + 1) * 1024]))
        w3 = self.w_in.rearrange("(c p) n -> p c n", p=128)
        ri = 0
        for (kind, c0, n, func, scale, odt, dst) in units:
            wbf, b_w = self.wload(w3[:, :, c0:c0 + n], 16, n, fold=g1)
            for hf in range(2):
                r = ri % 4
                s_ = ri % NS
                ri += 1
                pr = preg[r]
                if kind == "fm":
                    for tb in range(2):
                        t0 = hf * 1024 + tb * 512
                        for dc in range(16):
                            self.mm(pr[0:n, tb * 512:(tb + 1) * 512], wbf[:, dc, :], xT3[:, dc, t0:t0 + 512],
                                    dc == 0, dc == 15, [b_w, b_xT], pwrites=[b_preg[r]])
                    width = 1024
                else:
                    for t in range(8):
                        tg = hf * 8 + t
                        for dc in range(16):
                            self.mm(pr[:, t * n:(t + 1) * n], xT3[:, dc, tg * 128:(tg + 1) * 128], wbf[:, dc, :],
                                    dc == 0, dc == 15, [b_w, b_xT], pwrites=[b_preg[r]])
                    width = 8 * n
                if odt == BF16:
                    sv = stage[s_].bitcast(BF16)[:, 0:width]
                else:
                    sv = stage[s_][:, 0:width]
                self.act(sv, pr[:, 0:width], func, [b_preg[r]], writes=[b_stage[s_]], scale=scale)
                if kind == "fm":
                    P.dma("pool", ds_stage[s_], dst(hf), sv, reads=[b_stage[s_]])
                else:
                    P.dma("pool", ds_stage[s_], dst(hf), sv.rearrange("p (t n) -> p t n", n=n), reads=[b_stage[s_]])

    def phase2b(self):
        P, A, ps = self.P, self.A, self.psum
        self.common()
        cw, b_cw = self.load_prm(P_CW, 120)
        ones, b_ones = self.load_cst(C_ONES, 128)
        NB = 2
        xp = [A.f32(S + 4) for _ in range(NB)]
        b_xp = [Buf() for _ in range(NB)]
        ds_xp = [self.ds() for _ in range(NB)]
        for i in range(NB):
            self.memset("pool", xp[i][:, 0:2], 0.0, pwrites=[b_xp[i]])
            self.memset("pool", xp[i][:, S + 2:S + 4], 0.0, pwrites=[b_xp[i]])
        P.barrier()
        acc = [A.f32(S) for _ in range(NB)]
        b_acc = [Buf() for _ in range(NB)]
        sl = [A.f32(S) for _ in range(NB)]
        b_sl = [Buf() for _ in range(NB)]
        sq = [A.f32(S) for _ in range(NB)]
        b_sq = [Buf() for _ in range(NB)]
        rn = [A.f32(S) for _ in range(NB)]
        b_rn = [Buf() for _ in range(NB)]
        fb = [A.bf16(S) for _ in range(NB)]
        b_fb = [Buf() for _ in range(NB)]
        ds_fb = [self.ds() for _ in range(NB)]
        tk = [A.bf16(S) for _ in range(NB)]
        b_tk = [Buf() for _ in range(NB)]
        ds_tk = [self.ds() for _ in range(NB)]
        pss = ps[:, 0:2048]
        b_pss = Buf()
        ptr = ps[:, 2048:3072].bitcast(BF16)
        b_ptr = Buf()
        for u in range(24):
            i = u % NB
            h = u % 8
            P.dma("sp", ds_xp[i], xp[i][:, 2:S + 2], self.S_qkv[u], pwrites=[b_xp[i]])
            P.op("dve", lambda e, i=i, u=u: e.tensor_scalar_mul(out=acc[i], in0=xp[i][:, 0:S], scalar1=cw[:, u * 5:u * 5 + 1]),
                 [b_xp[i], b_cw], [b_acc[i]])
            for k in range(1, 5):
                self.stt("dve", acc[i], xp[i][:, k:k + S], cw[:, u * 5 + k:u * 5 + k + 1], acc[i], ALU.mult, ALU.add,
                         [b_xp[i], b_cw, b_acc[i]], [b_acc[i]])
            if u < 16:
                self.act(sl[i], acc[i], AF.Silu, [b_acc[i]], [b_sl[i]])
                self.tt("pool", sq[i], sl[i], sl[i], ALU.mult, [b_sl[i]], [b_sq[i]])
                for tb in range(4):
                    self.mm(pss[:, tb * 512:(tb + 1) * 512], ones, sq[i][:, tb * 512:(tb + 1) * 512], True, True,
                            [b_ones, b_sq[i]], pwrites=[b_pss])
                self.act(rn[i], pss, AF.Sqrt, [b_pss, self.b_eps], [b_rn[i]], bias=self.eps6, scale=1.0)
                P.op("dve", lambda e, i=i: e.reciprocal(out=rn[i], in_=rn[i]), [b_rn[i]], [b_rn[i]])
                sc_ = (128 ** -0.5) if u < 8 else 1.0
                self.stt("dve", fb[i], sl[i], sc_, rn[i], ALU.mult, ALU.mult, [b_sl[i], b_rn[i]], [b_fb[i]])
                dstT = self.S_qTd[h] if u < 8 else self.S_kTd[h]
                P.dma("pool", ds_fb[i], dstT, fb[i], reads=[b_fb[i]])
            else:
                self.act(fb[i], acc[i], AF.Silu, [b_acc[i]], [b_fb[i]])
            if u >= 8:
                for t in range(NT):
                    self.tr(ptr[:, t * 128:(t + 1) * 128], fb[i][:, t * 128:(t + 1) * 128], self.identb,
                            [b_fb[i], self.b_identb], pwrites=[b_ptr])
                self.cp("dve", tk[i], ptr, [b_ptr], [b_tk[i]])
                dst = self.S_kd[h] if u < 16 else self.S_vd[h]
                P.dma("pool", ds_tk[i], dst, tk[i].rearrange("p (t c) -> p t c", c=128), reads=[b_tk[i]])

    def gate_prep(self, lg, b_lg, ncol, cstt, b_cst, identf, b_identf):
        P, A, ps = self.P, self.A, self.psum
        half = ncol // 2
        n = NT * ncol
        lg3 = lg.rearrange("p (t c) -> p t c", c=ncol)
        cum = A.f32(n)
        tot = A.f32(n)
        decA = A.f32(n)
        decB = A.f32(n)
        cumT = A.f32(S)
        b = {k: Buf() for k in ("cum", "tot", "decA", "decB", "cumT", "p0", "p1", "p2", "p3", "pT")}
        p0 = ps[:, 0:n].rearrange("p (t c) -> p t c", c=ncol)
        cum3 = cum.rearrange("p (t c) -> p t c", c=ncol)
        c_ = lambda off: cstt[:, off:off + 128]
        p0b = ps[:, 256:256 + n].rearrange("p (t c) -> p t c", c=ncol)
        self.mm(ps[:, 0:n], c_(C_TRIF), lg, True, True, [b_cst, b_lg], pwrites=[b["p0"]])
        self.mm(ps[:, 256:256 + n], c_(C_TRIB), lg, True, True, [b_cst, b_lg], pwrites=[b["p0"]])
        self.cp("dve", cum3[:, :, 0:half], p0[:, :, 0:half], [b["p0"]], pwrites=[b["cum"]])
        self.cp("dve", cum3[:, :, half:ncol], p0b[:, :, half:ncol], [b["p0"]], pwrites=[b["cum"]])
        self.mm(ps[:, 512:512 + n], c_(C_BLK), lg, True, True, [b_cst, b_lg], writes=[b["p1"]])
        self.cp("dve", tot, ps[:, 512:512 + n], [b["p1"]], [b["tot"]])
        self.mm(ps[:, 1024:1024 + n], c_(C_HA), lg, True, True, [b_cst, b_lg], writes=[b["p2"]])
        self.act(decA, ps[:, 1024:1024 + n], AF.Exp, [b["p2"]], [b["decA"]])
        self.mm(ps[:, 1536:1536 + n], c_(C_HB), lg, True, True, [b_cst, b_lg], writes=[b["p3"]])
        self.act(decB, ps[:, 1536:1536 + n], AF.Exp, [b["p3"]], [b["decB"]])
        pT = ps[:, 2048:4096]
        for t in range(NT):
            self.tr(pT[0:ncol, t * 128:(t + 1) * 128], cum3[:, t, :], identf, [b["cum"], b_identf], pwrites=[b["pT"]])
        self.cp("dve", cumT[0:ncol, :], pT[0:ncol, :], [b["pT"]], [b["cumT"]])
        return dict(cum=cum, cum3=cum3, tot=tot, decA=decA, decB=decB, cumT=cumT, b=b)

    def phase3(self):
        P, A, ps = self.P, self.A, self.psum
        self.common()
        cstt, b_cst = self.load_cst(0, C_SEL8)
        sel, b_sel = self.load_cst(C_SEL8, 1024, rows=8)
        identf = cstt[:, C_IDENT:C_IDENT + 128]
        ibfb, b_ibfb = self.load_prm(P_IB, 16)
        ngm, b_ngm = self.load_prm(P_NGM, 1024)
        gm = A.f32(NT * 16)
        b_gm = Buf()
        self.load(gm, self.S_gm.rearrange("p t c -> p (t c)"), b_gm)
        gm3 = gm.rearrange("p (t c) -> p t c", c=16)
        n8 = NT * 8
        ig = A.f32(n8)
        lf = A.f32(n8)
        t1 = A.f32(n8)
        t2 = A.f32(n8)
        wst = A.f32(n8)
        b_ig, b_lf, b_t1, b_t2, b_wst = Buf(), Buf(), Buf(), Buf(), Buf()
        v3 = lambda a: a.rearrange("p (t c) -> p t c", c=8)
        bc8 = lambda a: a.unsqueeze(1).to_broadcast([128, NT, 8])
        self.tt("dve", v3(ig), gm3[:, :, 0:8], bc8(ibfb[:, 0:8]), ALU.add, [b_gm, b_ibfb], [b_ig])
        self.tt("dve", v3(t1), gm3[:, :, 8:16], bc8(ibfb[:, 8:16]), ALU.add, [b_gm, b_ibfb], [b_t1])
        self.act(t2, t1, AF.Abs, [b_t1], [b_t2])
        self.act(t2, t2, AF.Exp, [b_t2], [b_t2], scale=-1.0)
        self.act(t2, t2, AF.Ln, [b_t2, self.b_eps], [b_t2], bias=self.one1, scale=1.0)
        P.op("dve", lambda e: e.tensor_scalar_min(out=t1, in0=t1, scalar1=0.0), [b_t1], [b_t1])
        self.tt("dve", lf, t1, t2, ALU.subtract, [b_t1, b_t2], [b_lf])
        G = self.gate_prep(lf, b_lf, 8, cstt, b_cst, identf, b_cst)
        gb = G["b"]
        self.tt("dve", t1, G["tot"], G["cum"], ALU.subtract, [gb["tot"], gb["cum"]], [b_t1])
        self.tt("dve", t1, t1, ig, ALU.add, [b_t1, b_ig], [b_t1])
        self.act(wst, t1, AF.Exp, [b_t1], [b_wst])
        wst3, ig3 = v3(wst), v3(ig)
        decA3, decB3 = v3(G["decA"]), v3(G["decB"])
        cum3 = G["cum3"]
        P.barrier()
        qT = A.bf16(S); kT = A.bf16(S); ktok = A.bf16(S); vext = A.bf16(NT * 264)
        so = A.f32(NT * 256); hsum = A.f32(NT * 256); ymT = A.bf16(2 * S)
        E = A.f32(S); qg = A.bf16(S); WT = A.f32(S); Dm = A.f32(S); kw = A.bf16(S)
        Cf = A.f32(264)
        Cb = [A.bf16(264) for _ in range(3)]
        PT = [A.bf16(128) for _ in range(2)]
        den = A.f32(4)
        ytmp = [A.f32(256) for _ in range(2)]
        ybf = [A.bf16(256) for _ in range(2)]
        fin = A.f32(8)
        vext3 = vext.rearrange("p (t c) -> p t c", c=264)
        so3 = so.rearrange("p (t c) -> p t c", c=256)
        hs3 = hsum.rearrange("p (t c) -> p t c", c=256)
        ktok3 = ktok.rearrange("p (t c) -> p t c", c=128)
        kw3 = kw.rearrange("p (t c) -> p t c", c=128)
        WT3 = WT.rearrange("p (t c) -> p t c", c=128)
        Dm3 = Dm.rearrange("p (t c) -> p t c", c=128)
        ymT3 = ymT.rearrange("p (h t) -> p h t", t=S)
        B = {k: Buf() for k in ("qT", "kT", "ktok", "vext", "so", "ymT", "E", "qg", "WT", "Dm", "kw", "Cf", "bc",
                                "ones")}
        b_hs = [Buf() for _ in range(NT)]
        b_Cb = [Buf() for _ in range(3)]
        b_PT = [Buf(), Buf()]
        b_den = [Buf(), Buf()]
        b_yt = [Buf(), Buf()]
        b_yb = [Buf(), Buf()]
        b_fin = [Buf(), Buf()]
        dsl = {k: self.ds() for k in ("qT", "kT", "ktok", "vext", "so", "ymT")}
        pbc = ps[:, 0:2048]
        pST = [ps[:, 2048:2176], ps[:, 2176:2304]]
        b_pST = [Buf(), Buf()]
        ptr = ps[:, 2304:2432].bitcast(BF16)
        b_ptr = Buf()
        pnum = [ps[:, 2560:2560 + 257], ps[:, 3584:3584 + 257]]
        b_pnum = [Buf(), Buf()]
        pC = ps[:, 3072:3072 + 257]
        b_pC = Buf()
        self.memset("pool", vext3[:, :, 256:257], 1.0, pwrites=[B["ones"]])
        P.barrier()
        for h in range(4):
            P.dma("sp", dsl["qT"], qT, self.S_qTm[h], writes=[B["qT"]])
            P.dma("sp", dsl["kT"], kT, self.S_kTm[h], writes=[B["kT"]])
            P.dma("sp", dsl["ktok"], ktok3, self.S_km[h], writes=[B["ktok"]])
            P.dma("sp", dsl["vext"], vext3[:, :, 0:256], self.S_vm[h], writes=[B["vext"]])
            P.dma("sp", dsl["so"], so3, self.S_som[h], writes=[B["so"]])
            for dr in range(2):
                c = dr * 4 + h
                mn = cstt[:, C_MNF:C_MNF + 128] if dr == 0 else cstt[:, C_MNB:C_MNB + 128]
                for tb in range(4):
                    self.mm(pbc[:, tb * 512:(tb + 1) * 512], sel[0:8, c * 128:(c + 1) * 128],
                            G["cumT"][0:8, tb * 512:(tb + 1) * 512], True, True, [b_sel, gb["cumT"]], pwrites=[B["bc"]])
                self.act(E, pbc, AF.Exp, [B["bc"]], [B["E"]])
                self.tt("dve", qg, qT, E, ALU.mult, [B["qT"], B["E"]], [B["qg"]])
                self.tt("dve", Dm3, pbc.rearrange("p (t c) -> p t c", c=128),
                        cum3[:, :, c:c + 1].to_broadcast([128, NT, 128]), ALU.subtract, [B["bc"], gb["cum"]], [B["Dm"]])
                self.tt("dve", Dm3, Dm3, mn.unsqueeze(1).to_broadcast([128, NT, 128]), ALU.min, [B["Dm"], b_cst], [B["Dm"]])
                self.tt("pool", Dm3, Dm3, ig3[:, :, c:c + 1].to_broadcast([128, NT, 128]), ALU.add, [B["Dm"], b_ig], [B["Dm"]])
                self.act(WT, Dm, AF.Exp, [B["Dm"]], [B["WT"]])
                self.tt("dve", kw3, ktok3, wst3[:, :, c:c + 1].to_broadcast([128, NT, 128]), ALU.mult,
                        [B["ktok"], b_wst], [B["kw"]])
                self.memset("pool", Cf, 0.0, writes=[B["Cf"]])
                ci = 0
                self.memset("pool", Cb[0], 0.0, writes=[b_Cb[0]])
                tiles = range(NT) if dr == 0 else range(NT - 1, -1, -1)
                for it, t in enumerate(tiles):
                    j = it % 2
                    tok = slice(t * 128, (t + 1) * 128)
                    self.mm(pST[j], kT[:, tok], qT[:, tok], True, True, [B["kT"], B["qT"]], writes=[b_pST[j]])
                    self.tt("dve", PT[j], pST[j], WT3[:, t, :], ALU.mult, [b_pST[j], B["WT"]], [b_PT[j]])
                    chunks = (0, 1) if dr == 0 else (1, 0)
                    for ck in chunks:
                        rows = slice(ck * 64, ck * 64 + 64)
                        ctok = slice(t * 128 + ck * 64, t * 128 + ck * 64 + 64)
                        self.mm(pnum[j][rows, :], qg[:, ctok], Cb[ci][:, 0:257], True, False,
                                [B["qg"], b_Cb[ci]], pwrites=[b_pnum[j]])
                        self.mm(pC, kw3[rows, t, :], vext3[rows, t, 0:257], True, True,
                                [B["kw"], B["vext"], B["ones"]], writes=[b_pC])
                        dec = (decA3 if ck == 0 else decB3)[:, t, c:c + 1]
                        bdec = gb["decA"] if ck == 0 else gb["decB"]
                        cn = (ci + 1) % 3
                        self.stt("dve", Cb[cn][:, 0:257], Cf[:, 0:257], dec, pC, ALU.mult, ALU.add,
                                 [B["Cf"], bdec, b_pC], [b_Cb[cn]])
                        self.stt("dve", Cf[:, 0:257], Cf[:, 0:257], dec, pC, ALU.mult, ALU.add,
                                 [B["Cf"], bdec, b_pC], [B["Cf"]])
                        ci = cn
                    self.mm(pnum[j], PT[j], vext3[:, t, 0:257], False, True, [b_PT[j], B["vext"], B["ones"]],
                            pwrites=[b_pnum[j]])
                    dn = den[:, 2 * j:2 * j + 1]
                    rc = den[:, 2 * j + 1:2 * j + 2]
                    self.act(dn, pnum[j][:, 256:257], AF.Abs, [b_pnum[j]], [b_den[j]])
                    P.op("dve", lambda e, dn=dn: e.tensor_scalar_max(out=dn, in0=dn, scalar1=1.0), [b_den[j]], [b_den[j]])
                    P.op("dve", lambda e, dn=dn, rc=rc: e.reciprocal(out=rc, in_=dn), [b_den[j]], [b_den[j]])
                    if dr == 0:
                        self.act(hs3[:, t, :], pnum[j][:, 0:256], AF.Copy, [b_pnum[j], b_den[j]], [b_hs[t]], scale=rc)
                    else:
                        self.stt("dve", hs3[:, t, :], pnum[j][:, 0:256], rc, hs3[:, t, :], ALU.mult, ALU.add,
                                 [b_pnum[j], b_den[j], b_hs[t]], [b_hs[t]])
                        fs = fin[:, 2 * j:2 * j + 1]
                        fr = fin[:, 2 * j + 1:2 * j + 2]
                        self.act(ytmp[j], hs3[:, t, :], AF.Square, [b_hs[t]], [b_yt[j], b_fin[j]], accum_out=fs)
                        self.rsqrt_col(fr, fs, 1.0 / 256, self.eps6, [b_fin[j], self.b_eps], b_fin[j])
                        self.stt("dve", ytmp[j], hs3[:, t, :], fr, ngm[:, h * 256:(h + 1) * 256], ALU.mult, ALU.mult,
                                 [b_hs[t], b_fin[j], b_ngm], [b_yt[j]])
                        self.tt("pool", ybf[j], ytmp[j], so3[:, t, :], ALU.mult, [b_yt[j], B["so"]], [b_yb[j]])
                        for hh in range(2):
                            self.tr(ptr[:, hh * 128:(hh + 1) * 128], ybf[j][:, hh * 128:(hh + 1) * 128], self.identb,
                                    [b_yb[j], self.b_identb], pwrites=[b_ptr])
                        self.cp("dve", ymT3[:, :, tok], ptr.rearrange("p (h t) -> p h t", t=128), [b_ptr], pwrites=[B["ymT"]])
            for hh in range(2):
                P.dma("pool", dsl["ymT"], self.S_ymT[h * 2 + hh], ymT3[:, hh, :], reads=[B["ymT"]])

    def phase4(self):
        P, A, ps = self.P, self.A, self.psum
        self.common()
        cstt, b_cst = self.load_cst(0, C_SEL8)
        sel, b_sel = self.load_cst(C_SEL16, 2048, rows=16)
        identf = cstt[:, C_IDENT:C_IDENT + 128]
        adt, b_adt = self.load_prm(P_AL, 32)
        ngd, b_ngd = self.load_prm(P_NGD, 128)
        gd = A.f32(NT * 32)
        b_gd = Buf()
        self.load(gd, self.S_gd.rearrange("p t c -> p (t c)"), b_gd)
        gd3 = gd.rearrange("p (t c) -> p t c", c=32)
        n16 = NT * 16
        t1 = A.f32(n16); t2 = A.f32(n16); g = A.f32(n16); beta = A.f32(n16); nbeta = A.f32(n16)
        bg = A.f32(n16); ekd = A.f32(n16); eal = A.f32(16)
        b_t1, b_t2, b_g, b_beta, b_nb, b_bg, b_ekd, b_eal = [Buf() for _ in range(8)]
        v3 = lambda a: a.rearrange("p (t c) -> p t c", c=16)
        bc16 = lambda a: a.unsqueeze(1).to_broadcast([128, NT, 16])
        self.tt("dve", v3(t1), gd3[:, :, 0:16], bc16(adt[:, 16:32]), ALU.add, [b_gd, b_adt], [b_t1])
        self.act(t2, t1, AF.Abs, [b_t1], [b_t2])
        self.act(t2, t2, AF.Exp, [b_t2], [b_t2], scale=-1.0)
        self.act(t2, t2, AF.Ln, [b_t2, self.b_eps], [b_t2], bias=self.one1, scale=1.0)
        P.op("dve", lambda e: e.tensor_scalar_max(out=t1, in0=t1, scalar1=0.0), [b_t1], [b_t1])
        self.tt("dve", t1, t1, t2, ALU.add, [b_t1, b_t2], [b_t1])
        self.act(eal, adt[:, 0:16], AF.Exp, [b_adt], [b_eal])
        self.stt("dve", v3(g), v3(t1), -1.0, bc16(eal), ALU.mult, ALU.mult, [b_t1, b_eal], [b_g])
        self.act(v3(beta), gd3[:, :, 16:32], AF.Sigmoid, [b_gd], [b_beta])
        P.op("dve", lambda e: e.tensor_scalar_mul(out=nbeta, in0=beta, scalar1=-1.0), [b_beta], [b_nb])
        G = self.gate_prep(g, b_g, 16, cstt, b_cst, identf, b_cst)
        gb = G["b"]
        self.act(t2, G["cum"], AF.Exp, [gb["cum"]], [b_t2])
        self.tt("dve", bg, beta, t2, ALU.mult, [b_beta, b_t2], [b_bg])
        self.tt("dve", t1, G["tot"], G["cum"], ALU.subtract, [gb["tot"], gb["cum"]], [b_t1])
        self.act(ekd, t1, AF.Exp, [b_t1], [b_ekd])
        beta3, nbeta3, bg3, ekd3 = v3(beta), v3(nbeta), v3(bg), v3(ekd)
        decA3, decB3 = v3(G["decA"]), v3(G["decB"])
        cum3 = G["cum3"]
        P.barrier()
        qT = A.bf16(S); kT = A.bf16(S); ktok = A.bf16(S); vtok = A.bf16(S)
        sz = A.f32(S); osum = A.f32(S); ydT = A.bf16(S)
        qg = A.bf16(S); X = A.f32(S); Gam = A.f32(S); GamT = A.f32(S)
        kbg = A.bf16(S); vb = A.bf16(S); kd = A.bf16(S)
        u_all = A.f32(S); wT = A.bf16(S); attnT = A.bf16(S)
        GN = 2
        Xs = [[A.f32(128) for _ in range(2)] for _ in range(GN)]
        Ys = [[A.f32(128) for _ in range(2)] for _ in range(GN)]
        TTs = [[A.f32(128) for _ in range(2)] for _ in range(GN)]
        TTb = [A.bf16(128) for _ in range(GN)]
        Sf = A.f32(128)
        Sb = [A.bf16(128) for _ in range(3)]
        vnew = [A.bf16(128) for _ in range(2)]
        ytmp = [A.f32(128) for _ in range(2)]
        ybf = [A.bf16(128) for _ in range(2)]
        fin = A.f32(4)
        r3 = lambda a: a.rearrange("p (t c) -> p t c", c=128)
        ktok3, vtok3, sz3, os3 = r3(ktok), r3(vtok), r3(sz), r3(osum)
        X3, Gam3, GamT3, kbg3, vb3, kd3, u3, at3 = r3(X), r3(Gam), r3(GamT), r3(kbg), r3(vb), r3(kd), r3(u_all), r3(attnT)
        B = {k: Buf() for k in ("qT", "kT", "ktok", "vtok", "sz", "ydT", "qg", "X", "Gam", "GamT", "kbg", "vb", "kd",
                                "u", "wT", "attnT", "Sf", "bc")}
        b_os = [Buf() for _ in range(NT)]
        b_Xs = [[Buf(), Buf()] for _ in range(GN)]
        b_Ys = [[Buf(), Buf()] for _ in range(GN)]
        b_TTs = [[Buf(), Buf()] for _ in range(GN)]
        b_TTb = [Buf() for _ in range(GN)]
        b_Sb = [Buf() for _ in range(3)]
        b_vn = [Buf(), Buf()]
        b_yt = [Buf(), Buf()]
        b_yb = [Buf(), Buf()]
        b_fin = [Buf(), Buf()]
        dsl = {k: self.ds() for k in ("qT", "kT", "ktok", "vtok", "sz", "ydT")}
        pbc = ps[:, 0:2048]
        pX = [ps[:, (gi * 3) * 512:(gi * 3) * 512 + 128] for gi in range(GN)]
        pY = [ps[:, (gi * 3 + 1) * 512:(gi * 3 + 1) * 512 + 128] for gi in range(GN)]
        pTT = [ps[:, (gi * 3 + 2) * 512:(gi * 3 + 2) * 512 + 128] for gi in range(GN)]
        b_pX = [Buf() for _ in range(GN)]
        b_pY = [Buf() for _ in range(GN)]
        b_pTT = [Buf() for _ in range(GN)]
        pA = [ps[:, (6 + gi) * 512:(6 + gi) * 512 + 128] for gi in range(GN)]
        b_pA = [Buf() for _ in range(GN)]
        pB = [ps[:, (6 + gi) * 512 + 128:(6 + gi) * 512 + 256] for gi in range(GN)]
        b_pB = [Buf() for _ in range(GN)]
        pCq = [ps[:, (6 + gi) * 512 + 256:(6 + gi) * 512 + 384] for gi in range(GN)]
        b_pCq = [Buf() for _ in range(GN)]
        pvn = ps[:, 0:128]; b_pvn = Buf()
        po = [ps[:, 512:640], ps[:, 1024:1152]]; b_po = [Buf(), Buf()]
        pS = ps[:, 1536:1664]; b_pS = Buf()
        ptr = ps[:, 2048:2112].bitcast(BF16); b_ptr = Buf()
        import os
        K4H = int(os.environ.get("K4H", "8")); K4S = os.environ.get("K4S", "C")
        for h in range(K4H):
            P.dma("sp", dsl["qT"], qT, self.S_qTd[h], writes=[B["qT"]])
            P.dma("sp", dsl["kT"], kT, self.S_kTd[h], writes=[B["kT"]])
            P.dma("sp", dsl["ktok"], ktok3, self.S_kd[h], writes=[B["ktok"]])
            P.dma("sp", dsl["vtok"], vtok3, self.S_vd[h], writes=[B["vtok"]])
            P.dma("sp", dsl["sz"], sz3, self.S_sz[h], writes=[B["sz"]])
            for dr in range(2):
                c = dr * 8 + h
                mn = cstt[:, C_MNF:C_MNF + 128] if dr == 0 else cstt[:, C_MNB:C_MNB + 128]
                mp = cstt[:, C_MPF:C_MPF + 128] if dr == 0 else cstt[:, C_MPB:C_MPB + 128]
                for tb in range(4):
                    self.mm(pbc[:, tb * 512:(tb + 1) * 512], sel[0:16, c * 128:(c + 1) * 128],
                            G["cumT"][0:16, tb * 512:(tb + 1) * 512], True, True, [b_sel, gb["cumT"]], pwrites=[B["bc"]])
                self.act(Gam, pbc, AF.Exp, [B["bc"]], [B["Gam"]])
                self.tt("dve", qg, qT, Gam, ALU.mult, [B["qT"], B["Gam"]], [B["qg"]])
                self.tt("dve", X3, pbc.rearrange("p (t c) -> p t c", c=128),
                        cum3[:, :, c:c + 1].to_broadcast([128, NT, 128]), ALU.subtract, [B["bc"], gb["cum"]], [B["X"]])
                self.tt("dve", GamT3, X3, mn.unsqueeze(1).to_broadcast([128, NT, 128]), ALU.min, [B["X"], b_cst], [B["GamT"]])
                self.act(GamT, GamT, AF.Exp, [B["GamT"]], [B["GamT"]])
                self.tt("dve", Gam3, X3, mp.unsqueeze(1).to_broadcast([128, NT, 128]), ALU.max, [B["X"], b_cst, B["qg"]], [B["Gam"]])
                self.act(Gam, Gam, AF.Exp, [B["Gam"]], [B["Gam"]], scale=-1.0)
                bcc = lambda a3: a3[:, :, c:c + 1].to_broadcast([128, NT, 128])
                self.tt("dve", kbg3, ktok3, bcc(bg3), ALU.mult, [B["ktok"], b_bg], [B["kbg"]])
                self.tt("dve", vb3, vtok3, bcc(beta3), ALU.mult, [B["vtok"], b_beta], [B["vb"]])
                self.tt("dve", kd3, ktok3, bcc(ekd3), ALU.mult, [B["ktok"], b_ekd], [B["kd"]])
                P.barrier()
                if K4S == "A":
                    continue
                K4B = int(os.environ.get("K4B", "9"))
                for g0 in range(0, NT, GN):
                    cur = [0] * GN
                    for gi in range(GN):
                        t = g0 + gi
                        tok = slice(t * 128, (t + 1) * 128)
                        self.mm(pX[gi], kT[:, tok], kT[:, tok], True, True, [B["kT"]], writes=[b_pX[gi]])
                        self.stt("dve", Xs[gi][0], pX[gi], nbeta3[:, t, c:c + 1], Gam3[:, t, :], ALU.mult, ALU.mult,
                                 [b_pX[gi], b_nb, B["Gam"]], [b_Xs[gi][0]])
                    if K4B < 2:
                        continue
                    for gi in range(GN):
                        self.tr(pY[gi], Xs[gi][0], identf, [b_Xs[gi][0], b_cst], writes=[b_pY[gi]])
                    for gi in range(GN):
                        self.cp("dve", Ys[gi][0], pY[gi], [b_pY[gi]], [b_Ys[gi][0]])
                        self.tt("dve", TTs[gi][0], pY[gi], identf, ALU.add, [b_pY[gi], b_cst], [b_TTs[gi][0]])
                    if K4B < 3:
                        continue
                    for lv in range(1, 6):
                        a, b_ = (lv - 1) % 2, lv % 2
                        for gi in range(GN):
                            self.mm(pX[gi], Ys[gi][a], Xs[gi][a], True, True, [b_Ys[gi][a], b_Xs[gi][a]], writes=[b_pX[gi]])
                            if lv < 5:
                                self.mm(pY[gi], Xs[gi][a], Ys[gi][a], True, True, [b_Ys[gi][a], b_Xs[gi][a]], writes=[b_pY[gi]])
                        for gi in range(GN):
                            self.act(Xs[gi][b_], pX[gi], AF.Copy, [b_pX[gi]], [b_Xs[gi][b_]])
                            if lv < 5:
                                self.cp("dve", Ys[gi][b_], pY[gi], [b_pY[gi]], [b_Ys[gi][b_]])
                        for gi in range(GN):
                            self.mm(pTT[gi], Xs[gi][b_], TTs[gi][a], True, True, [b_Xs[gi][b_], b_TTs[gi][a]], writes=[b_pTT[gi]])
                        for gi in range(GN):
                            if lv < 5:
                                self.tt("dve", TTs[gi][b_], pTT[gi], TTs[gi][a], ALU.add, [b_pTT[gi], b_TTs[gi][a]], [b_TTs[gi][b_]])
                            else:
                                self.tt("dve", TTb[gi], pTT[gi], TTs[gi][a], ALU.add, [b_pTT[gi], b_TTs[gi][a]], [b_TTb[gi]])
                    if K4B < 4:
                        continue
                    for gi in range(GN):
                        t = g0 + gi
                        tok = slice(t * 128, (t + 1) * 128)
                        self.mm(pA[gi], TTb[gi], vb3[:, t, :], True, True, [b_TTb[gi], B["vb"]], writes=[b_pA[gi]])
                        self.act(u3[:, t, :], pA[gi], AF.Copy, [b_pA[gi]], pwrites=[B["u"]])
                        if K4B < 5:
                            continue
                        self.mm(pB[gi], kbg3[:, t, :], TTb[gi], True, True, [b_TTb[gi], B["kbg"]], writes=[b_pB[gi]])
                        self.act(wT[:, tok], pB[gi], AF.Copy, [b_pB[gi]], pwrites=[B["wT"]])
                P.barrier()
                if K4B >= 6:
                    for t in range(NT):
                        tok = slice(t * 128, (t + 1) * 128)
                        gi = t % 2
                        pq = ps[:, gi * 512:gi * 512 + 128]
                        self.mm(pq, kT[:, tok], qT[:, tok], True, True, [B["kT"], B["qT"]], writes=[b_pCq[gi]])
                        self.tt("dve", at3[:, t, :], pq, GamT3[:, t, :], ALU.mult, [b_pCq[gi], B["GamT"]], pwrites=[B["attnT"]])
                P.barrier()
                if K4S == "B":
                    continue
                self.memset("pool", Sf, 0.0, writes=[B["Sf"]])
                self.memset("pool", Sb[0], 0.0, writes=[b_Sb[0]])
                si = 0
                tiles = range(NT) if dr == 0 else range(NT - 1, -1, -1)
                for it, t in enumerate(tiles):
                    j = it % 2
                    tok = slice(t * 128, (t + 1) * 128)
                    chunks = (0, 1) if dr == 0 else (1, 0)
                    for ck in chunks:
                        rows = slice(ck * 64, ck * 64 + 64)
                        ctok = slice(t * 128 + ck * 64, t * 128 + ck * 64 + 64)
                        self.mm(pvn[rows, :], wT[:, ctok], Sb[si], True, True, [B["wT"], b_Sb[si]], pwrites=[b_pvn])
                        self.tt("dve", vnew[j][rows, :], u3[rows, t, :], pvn[rows, :], ALU.subtract, [B["u"], b_pvn], pwrites=[b_vn[j]])
                        self.mm(po[j][rows, :], qg[:, ctok], Sb[si], True, False, [B["qg"], b_Sb[si]], pwrites=[b_po[j]])
                        self.mm(pS, kd3[rows, t, :], vnew[j][rows, :], True, True, [B["kd"], b_vn[j]], writes=[b_pS])
                        dec = (decA3 if ck == 0 else decB3)[:, t, c:c + 1]
                        bdec = gb["decA"] if ck == 0 else gb["decB"]
                        sn = (si + 1) % 3
                        self.stt("dve", Sb[sn], Sf, dec, pS, ALU.mult, ALU.add, [B["Sf"], bdec, b_pS], [b_Sb[sn]])
                        self.stt("dve", Sf, Sf, dec, pS, ALU.mult, ALU.add, [B["Sf"], bdec, b_pS], [B["Sf"]])
                        si = sn
                    self.mm(po[j], at3[:, t, :], vnew[j], False, True, [B["attnT"], b_vn[j]], pwrites=[b_po[j]])
                    if dr == 0:
                        self.act(os3[:, t, :], po[j], AF.Copy, [b_po[j]], [b_os[t]])
                    else:
                        self.tt("dve", os3[:, t, :], po[j], os3[:, t, :], ALU.add, [b_po[j], b_os[t]], [b_os[t]])
                        fs = fin[:, 2 * j:2 * j + 1]
                        fr = fin[:, 2 * j + 1:2 * j + 2]
                        self.act(ytmp[j], os3[:, t, :], AF.Square, [b_os[t]], [b_yt[j], b_fin[j]], accum_out=fs)
                        self.rsqrt_col(fr, fs, 1.0 / 128, self.eps6, [b_fin[j], self.b_eps], b_fin[j])
                        self.stt("dve", ytmp[j], os3[:, t, :], fr, ngd, ALU.mult, ALU.mult, [b_os[t], b_fin[j], b_ngd], [b_yt[j]])
                        self.tt("pool", ybf[j], ytmp[j], sz3[:, t, :], ALU.mult, [b_yt[j], B["sz"]], [b_yb[j]])
                        self.tr(ptr, ybf[j], self.identb, [b_yb[j], self.b_identb], writes=[b_ptr])
                        self.cp("dve", ydT[:, tok], ptr, [b_ptr], pwrites=[B["ydT"]])
                P.barrier()
            P.dma("pool", dsl["ydT"], self.S_ydT[h], ydT, reads=[B["ydT"]])
            P.barrier()

    def phase5(self):
        P, A, ps = self.P, self.A, self.psum
        self.common()
        self.b_prm = []
        ym = A.bf16(8 * S); yd = A.bf16(8 * S)
        b_ym, b_yd = Buf(), Buf()
        ym3 = ym.rearrange("p (c t) -> p c t", t=S)
        yd3 = yd.rearrange("p (c t) -> p c t", t=S)
        self.load(ym3, self.S_ymT.rearrange("c p t -> p c t"), b_ym)
        self.load(yd3, self.S_ydT.rearrange("c p t -> p c t"), b_yd)
        self.wring_init(4)
        sg = [[A.f32(S) for _ in range(2)] for _ in range(2)]
        b_sg = [[Buf(), Buf()] for _ in range(2)]
        ds_sg = [[self.ds(), self.ds()] for _ in range(2)]
        tmp = [A.f32(S) for _ in range(2)]
        b_tmp = [Buf(), Buf()]
        mixb = [A.bf16(S) for _ in range(2)]
        b_mix = [Buf(), Buf()]
        ds_mix = [self.ds(), self.ds()]
        pm = ps[:, 0:2048]; pd = ps[:, 2048:4096]
        b_pm, b_pd = Buf(), Buf()
        wm3 = self.w_bm.rearrange("(c p) n -> p c n", p=128)
        wd3 = self.w_bd.rearrange("(c p) n -> p c n", p=128)
        for cg in range(16):
            i = cg % 2
            wm, b_wm = self.wload(wm3[:, :, cg * 128:(cg + 1) * 128], 8, 128)
            wd, b_wd = self.wload(wd3[:, :, cg * 128:(cg + 1) * 128], 8, 128, eng="pool")
            P.dma("sp", ds_sg[i][0], sg[i][0], self.S_sg[cg], writes=[b_sg[i][0]])
            P.dma("sp", ds_sg[i][1], sg[i][1], self.S_sg[16 + cg], writes=[b_sg[i][1]])
            for tb in range(4):
                for kc in range(8):
                    self.mm(pm[:, tb * 512:(tb + 1) * 512], wm[:, kc, :], ym3[:, kc, tb * 512:(tb + 1) * 512],
                            kc == 0, kc == 7, [b_wm, b_ym], pwrites=[b_pm])
            for tb in range(4):
                for kc in range(8):
                    self.mm(pd[:, tb * 512:(tb + 1) * 512], wd[:, kc, :], yd3[:, kc, tb * 512:(tb + 1) * 512],
                            kc == 0, kc == 7, [b_wd, b_yd], pwrites=[b_pd])
            self.tt("dve", tmp[i], pm, sg[i][0], ALU.mult, [b_pm, b_sg[i][0]], [b_tmp[i]])
            self.tt("dve", sg[i][1], pd, sg[i][1], ALU.mult, [b_pd, b_sg[i][1]], [b_sg[i][1]])
            self.tt("pool", mixb[i], tmp[i], sg[i][1], ALU.add, [b_tmp[i], b_sg[i][1]], [b_mix[i]])
            P.dma("pool", ds_mix[i], self.S_mixT[cg], mixb[i], reads=[b_mix[i]])

    def phase67(self):
        P, A, ps = self.P, self.A, self.psum
        self.common()
        g2, b_g2 = self.load_prm(P_G2, 16)
        self.b_prm = [b_g2]
        gF, b_gF = self.load_prm(P_GF, 2048)
        self.wring_init(3)
        mix = A.bf16(16 * 512); b_mixb = Buf(); ds_mixl = self.ds()
        mix3 = mix.rearrange("p (c t) -> p c t", t=512)
        x2 = A.f32(4 * D); b_x2 = [Buf() for _ in range(4)]; ds_x2 = [self.ds() for _ in range(4)]
        x23 = x2.rearrange("p (t c) -> p t c", c=D)
        h2T = A.bf16(16 * 512); b_h2T = Buf()
        h2T3 = h2T.rearrange("p (c t) -> p c t", t=512)
        actT = A.bf16(64 * 512); b_actT = Buf()
        actT3 = actT.rearrange("p (c t) -> p c t", t=512)
        junk = A.bf16(D); b_junk = Buf()
        xn1 = A.bf16(D); xn = [xn1, xn1]; b_xn1 = Buf(); b_xn = [b_xn1, b_xn1]
        st = A.f32(16); b_st = [Buf() for _ in range(4)]
        rl = [A.f32(512) for _ in range(2)]; b_rl = [Buf(), Buf()]
        ds_yo = [self.ds() for _ in range(4)]
        pacc = [ps[:, 0:512], ps[:, 512:1024]]; b_pacc = [Buf(), Buf()]
        pff = [ps[:, 1024:1536], ps[:, 1536:2048]]; b_pff = [Buf(), Buf()]
        ptrF = ps[:, 2048:2560]; b_ptrF = Buf()
        pT1 = ps[:, 3072:4096].bitcast(BF16); b_pT1 = Buf()
        pT = [pT1, pT1]; b_pT = [b_pT1, b_pT1]
        identf, b_idf = self.load_cst(C_IDENT, 128)
        wo3 = self.w_out.rearrange("(c p) n -> p c n", p=128)
        w13 = self.w_ff1.rearrange("(c p) n -> p c n", p=128)
        w23 = self.w_ff2.rearrange("(c p) n -> p c n", p=128)
        ai = 0
        for tb in range(4):
            P.dma("sp", ds_mixl, mix3, self.S_mixT.rearrange("c p t -> p c t")[:, :, tb * 512:(tb + 1) * 512], writes=[b_mixb])
            for tt_ in range(4):
                tg = tb * 4 + tt_
                P.dma("sp", ds_x2[tt_], x23[:, tt_, :], self.x[tg * 128:(tg + 1) * 128, :], writes=[b_x2[tt_]])
            for cu in range(16):
                wbf, b_w = self.wload(wo3[:, :, cu * 128:(cu + 1) * 128], 16, 128, eng=("dve" if cu % 2 == 0 else "pool"))
                a = ai % 2
                ai += 1
                for tt_ in range(4):
                    for kc in range(16):
                        self.mm(pacc[a][:, tt_ * 128:(tt_ + 1) * 128], mix3[:, kc, tt_ * 128:(tt_ + 1) * 128], wbf[:, kc, :],
                                kc == 0, kc == 15, [b_mixb, b_w], pwrites=[b_pacc[a]])
                self.tt("dve", x23[:, :, cu * 128:(cu + 1) * 128], pacc[a].rearrange("p (t c) -> p t c", c=128),
                        x23[:, :, cu * 128:(cu + 1) * 128], ALU.add, [b_pacc[a]] + b_x2, pwrites=b_x2)
            for tt_ in range(4):
                i = tt_ % 2
                self.act(junk, x23[:, tt_, :], AF.Square, [b_x2[tt_]], [b_junk, b_st[tt_]], accum_out=st[:, tt_:tt_ + 1])
                self.rsqrt_col(st[:, 4 + tt_:5 + tt_], st[:, tt_:tt_ + 1], 1.0 / D, self.eps6, [b_st[tt_], self.b_eps], b_st[tt_])
                self.act(xn[i], x23[:, tt_, :], AF.Copy, [b_x2[tt_], b_st[tt_]], [b_xn[i]], scale=st[:, 4 + tt_:5 + tt_])
                for dc in range(16):
                    self.tr(pT[i][:, dc * 128:(dc + 1) * 128], xn[i][:, dc * 128:(dc + 1) * 128], self.identb,
                            [b_xn[i], self.b_identb], pwrites=[b_pT[i]])
                self.cp("dve", h2T3[:, :, tt_ * 128:(tt_ + 1) * 128], pT[i].rearrange("p (c t) -> p c t", t=128),
                        [b_pT[i]], pwrites=[b_h2T])
            for fu in range(64):
                wbf, b_w = self.wload(w13[:, :, fu * 128:(fu + 1) * 128], 16, 128, fold=g2, eng=("dve" if fu % 2 == 0 else "pool"))
                a = fu % 2
                for dc in range(16):
                    self.mm(pff[a], wbf[:, dc, :], h2T3[:, dc, :], dc == 0, dc == 15, [b_w, b_h2T], pwrites=[b_pff[a]])
                self.act(rl[a], pff[a], AF.Relu, [b_pff[a]], [b_rl[a]])
                self.tt("dve" if fu % 2 == 1 else "pool", actT3[:, fu, :], rl[a], rl[a], ALU.mult, [b_rl[a]], pwrites=[b_actT])
            for cu in range(16):
                a = ai % 2
                ai += 1
                for kq in range(4):
                    wbf, b_w = self.wload(w23[:, kq * 16:(kq + 1) * 16, cu * 128:(cu + 1) * 128], 16, 128,
                                          eng=("dve" if kq % 2 == 0 else "pool"))
                    for kl in range(16):
                        kc = kq * 16 + kl
                        self.mm(pacc[a], wbf[:, kl, :], actT3[:, kc, :], kc == 0, kc == 63, [b_actT, b_w], pwrites=[b_pacc[a]])
                self.act(rl[a], pacc[a], AF.Copy, [b_pacc[a]], [b_rl[a]])
                for tt_ in range(4):
                    self.tr(ptrF[:, tt_ * 128:(tt_ + 1) * 128], rl[a][:, tt_ * 128:(tt_ + 1) * 128], identf,
                            [b_rl[a], b_idf], pwrites=[b_ptrF])
                self.tt("dve", x23[:, :, cu * 128:(cu + 1) * 128], ptrF.rearrange("p (t c) -> p t c", c=128),
                        x23[:, :, cu * 128:(cu + 1) * 128], ALU.add, [b_ptrF] + b_x2, pwrites=b_x2)
            for tt_ in range(4):
                i = tt_ % 2
                tg = tb * 4 + tt_
                self.act(junk, x23[:, tt_, :], AF.Square, [b_x2[tt_]], [b_junk, b_st[tt_]], accum_out=st[:, 8 + tt_:9 + tt_])
                self.rsqrt_col(st[:, 12 + tt_:13 + tt_], st[:, 8 + tt_:9 + tt_], 1.0 / D, self.eps6, [b_st[tt_], self.b_eps], b_st[tt_])
                self.stt("dve", x23[:, tt_, :], x23[:, tt_, :], st[:, 12 + tt_:13 + tt_], gF, ALU.mult, ALU.mult,
                         [b_x2[tt_], b_st[tt_], b_gF], [b_x2[tt_]])
                P.dma("pool", ds_yo[tt_], self.y[tg * 128:(tg + 1) * 128, :], x23[:, tt_, :], reads=[b_x2[tt_]])


_CACHE = {}


def _build(debug=False, upto=99):
    key = (debug, upto)
    if key not in _CACHE:
        nc = bass.Bass("TRN2", target_bir_lowering=False)
        k = Kern(nc, debug=debug, upto=upto)
        k.build()
        _CACHE[key] = (nc, k)
    return _CACHE[key]


def _in_maps(inp):
    f = np.float32
    x = np.asarray(inp["x"], f)
    shared = {
        "w_in": np.ascontiguousarray(np.asarray(inp["w_in"], f)[0]),
        "w_bm": np.ascontiguousarray(np.asarray(inp["w_branch_m"], f)[0]),
        "w_bd": np.ascontiguousarray(np.asarray(inp["w_branch_d"], f)[0]),
        "w_out": np.ascontiguousarray(np.asarray(inp["w_out"], f)[0]),
        "w_ff1": np.ascontiguousarray(np.asarray(inp["w_ff1"], f)[0]),
        "w_ff2": np.ascontiguousarray(np.asarray(inp["w_ff2"], f)[0]),
        "prm": _params(inp),
        "cst": _consts(),
        "cstb": np.eye(128).astype(ml_dtypes.bfloat16),
    }
    maps = []
    for b in range(8):
        m = dict(shared)
        m["x"] = np.ascontiguousarray(x[b])
        maps.append(m)
    return maps


def kernel(**inputs):
    nc, _ = _build()
    res = run_bass_kernel_spmd(nc, _in_maps(inputs), core_ids=list(range(8)))
    return np.stack([np.asarray(r["y"], np.float32) for r in res.results], axis=0)
```

```python
import contextlib
import numpy as np
import ml_dtypes
import concourse.bass as bass
import concourse.mybir as mybir
from concourse.bass_utils import run_bass_kernel_spmd

F32 = mybir.dt.float32
BF16 = mybir.dt.bfloat16
AF = mybir.ActivationFunctionType
ALU = mybir.AluOpType

CENG = ("pe", "act", "dve", "pool")
ALLENG = ("pe", "act", "dve", "pool", "sp")

S = 2048
D = 2048
NT = 16
DIN = 11312
DFF = 8192
BIG = 30000.0


class Buf:
    __slots__ = ("writers", "readers", "war")

    def __init__(self):
        self.writers = []
        self.readers = []
        self.war = []


class DSem:
    __slots__ = ("sem", "count")

    def __init__(self, sem):
        self.sem = sem
        self.count = 0


class Prog:
    def __init__(self):
        self.streams = {e: [] for e in ALLENG}
        self.dsems = []

    def _deps(self, eng, reads, writes, pwrites, is_dma=False):
        deps = []
        if is_dma:
            eng = None
        for b in reads:
            deps.extend(b.writers)
        for b in writes:
            for ev in b.writers + b.readers:
                if not (ev[0] == "c" and ev[1] == eng):
                    deps.append(ev)
        for b in pwrites:
            for ev in (b.readers if b.readers else b.war):
                if not (ev[0] == "c" and ev[1] == eng):
                    deps.append(ev)
        return deps

    @staticmethod
    def _reduce(evs):
        best = {}
        for ev in evs:
            key = (ev[0], ev[1] if ev[0] == "c" else id(ev[1]))
            if key not in best or best[key][2] < ev[2]:
                best[key] = ev
        return list(best.values())

    def _commit(self, ev, reads, writes, pwrites):
        for b in reads:
            b.readers.append(ev)
            if len(b.readers) > 8:
                b.readers = self._reduce(b.readers)
        for b in writes:
            b.writers = [ev]
            b.readers = []
            b.war = []
        for b in pwrites:
            if b.readers:
                b.war = [r for r in b.readers if r != ev]
                b.writers = [ev]
                b.readers = []
            else:
                b.writers.append(ev)
                b.writers = self._reduce(b.writers)

    def op(self, eng, fn, reads=(), writes=(), pwrites=()):
        deps = self._reduce(self._deps(eng, reads, writes, pwrites))
        st = self.streams[eng]
        idx = len(st)
        st.append({"fn": fn, "deps": deps, "sig": False, "dma": None})
        ev = ("c", eng, idx)
        self._commit(ev, reads, writes, pwrites)
        return ev

    def dma(self, queue, dsem, out_ap, in_ap, reads=(), writes=(), pwrites=()):
        deps = self._reduce(self._deps(queue, reads, writes, pwrites, True))
        st = self.streams[queue]
        dsem.count += 16
        ev = ("d", dsem, dsem.count)
        st.append({"fn": (lambda e, o=out_ap, i=in_ap: e.dma_start(out=o, in_=i)),
                   "deps": deps, "sig": False, "dma": dsem})
        self._commit(ev, reads, writes, pwrites)
        return ev

    def barrier(self):
        evs = []
        for e in CENG:
            st = self.streams[e]
            for i in range(len(st) - 1, -1, -1):
                if st[i]["dma"] is None and st[i]["fn"] is not None:
                    evs.append(("c", e, i))
                    break
        for d in self.dsems:
            if d.count:
                evs.append(("d", d, d.count))
        for e in ALLENG:
            self.streams[e].append({"fn": None, "deps": list(evs), "sig": False, "dma": None})

    def emit(self, nc, engsems):
        for e in ALLENG:
            for o in self.streams[e]:
                for ev in o["deps"]:
                    if ev[0] == "c":
                        self.streams[ev[1]][ev[2]]["sig"] = True
        sigidx = {}
        for e in ALLENG:
            c = 0
            arr = []
            for o in self.streams[e]:
                if o["sig"]:
                    c += 1
                arr.append(c)
            sigidx[e] = arr
        self.sigcounts = {e: (sigidx[e][-1] if sigidx[e] else 0) for e in ALLENG}
        streams = self.streams

        def replay(e):
            def run(eng):
                seen = {}
                for o in streams[e]:
                    need = {}
                    for ev in o["deps"]:
                        if ev[0] == "c":
                            key = ("c", ev[1])
                            val = sigidx[ev[1]][ev[2]]
                        else:
                            key = ("d", id(ev[1]))
                            val = ev[2]
                        if seen.get(key, 0) >= val:
                            continue
                        if need.get(key, (0, None))[0] < val:
                            need[key] = (val, ev)
                    for key, (val, ev) in need.items():
                        seen[key] = val
                        sem = engsems[ev[1]] if ev[0] == "c" else ev[1].sem
                        eng.wait_ge(sem, val)
                    if o["fn"] is None:
                        continue
                    ins = o["fn"](eng)
                    if o["dma"] is not None:
                        ins.then_inc(o["dma"].sem, 16)
                    elif o["sig"]:
                        ins.then_inc(engsems[e], 1)
            return run

        with nc.Block() as block:
            block.sync(replay("sp"))
            block.tensor(replay("pe"))
            block.scalar(replay("act"))
            block.vector(replay("dve"))
            block.gpsimd(replay("pool"))


class Arena:
    def __init__(self, ap, nwords):
        self.ap = ap
        self.n = nwords
        self.off = 0

    def reset(self):
        self.off = 0

    def f32(self, n):
        a = self.ap[:, self.off:self.off + n]
        self.off += n
        assert self.off <= self.n, ("arena overflow", self.off)
        return a

    def bf16(self, n):
        w = (n + 1) // 2
        a = self.ap[:, self.off:self.off + w].bitcast(BF16)
        self.off += w
        assert self.off <= self.n, ("arena overflow", self.off)
        return a


C_IDENT, C_ONES, C_TRIF, C_TRIB, C_BLK, C_HA, C_HB, C_MNF, C_MNB, C_MPF, C_MPB = [i * 128 for i in range(11)]
C_SEL8 = 11 * 128
C_SEL16 = C_SEL8 + 1024
NCST = C_SEL16 + 2048

P_G1, P_G2, P_IB, P_FB, P_AL, P_DT = 0, 16, 32, 40, 48, 64
P_CW = 80
P_NGM = 200
P_NGD = P_NGM + 1024
P_GF = P_NGD + 128
NPRM = P_GF + 2048


def _consts():
    p = np.arange(128)[:, None]
    j = np.arange(128)[None, :]
    same = (p // 64) == (j // 64)
    c = np.zeros((128, NCST), np.float32)
    c[:, C_IDENT:C_IDENT + 128] = np.eye(128)
    c[:, C_ONES:C_ONES + 128] = 1.0
    c[:, C_TRIF:C_TRIF + 128] = same & (p <= j)
    c[:, C_TRIB:C_TRIB + 128] = same & (p >= j)
    c[:, C_BLK:C_BLK + 128] = same
    c[:, C_HA:C_HA + 128] = (p < 64) & (j >= 0)
    c[:, C_HB:C_HB + 128] = (p >= 64) & (j >= 0)
    c[:, C_MNF:C_MNF + 128] = np.where(same & (p <= j), 0.0, -BIG)
    c[:, C_MNB:C_MNB + 128] = np.where(same & (p >= j), 0.0, -BIG)
    c[:, C_MPF:C_MPF + 128] = np.where(same & (j < p), 0.0, BIG)
    c[:, C_MPB:C_MPB + 128] = np.where(same & (j > p), 0.0, BIG)
    for k in range(8):
        c[k, C_SEL8 + k * 128:C_SEL8 + (k + 1) * 128] = 1.0
    for k in range(16):
        c[k, C_SEL16 + k * 128:C_SEL16 + (k + 1) * 128] = 1.0
    return c


def _params(inp):
    f = np.float32
    prm = np.zeros((128, NPRM), f)
    prm[:, P_G1:P_G1 + 16] = np.asarray(inp["norm1_g"], f).reshape(16, 128).T
    prm[:, P_G2:P_G2 + 16] = np.asarray(inp["norm2_g"], f).reshape(16, 128).T
    prm[:, P_IB:P_IB + 8] = np.asarray(inp["mlstm_i_bias"], f).reshape(1, 8)
    prm[:, P_FB:P_FB + 8] = np.asarray(inp["mlstm_f_bias"], f).reshape(1, 8)
    prm[:, P_AL:P_AL + 16] = np.asarray(inp["delta_a_log"], f).reshape(1, 16)
    prm[:, P_DT:P_DT + 16] = np.asarray(inp["delta_dt_bias"], f).reshape(1, 16)
    cw = np.asarray(inp["delta_conv_w"], f).reshape(5, 24, 128)
    prm[:, P_CW:P_CW + 120] = cw.transpose(2, 1, 0).reshape(128, 120)
    prm[:, P_NGM:P_NGM + 1024] = np.asarray(inp["mlstm_norm_g"], f).reshape(1, 1024)
    prm[:, P_NGD:P_NGD + 128] = np.asarray(inp["delta_norm_g"], f).reshape(1, 128)
    prm[:, P_GF:P_GF + 2048] = np.asarray(inp["norm_f_g"], f).reshape(1, 2048)
    return prm


class Kern:
    def __init__(self, nc, debug=False, upto=99):
        self.nc = nc
        self.debug = debug
        self.upto = upto
        self.P = Prog()
        self.dbg_names = []

    def ds(self):
        d = self.dspool[self.dsnext]
        self.dsnext += 1
        return d

    def scratch(self, name, shape, dt):
        kind = "ExternalOutput" if self.debug else "Internal"
        t = self.nc.dram_tensor(name, list(shape), dt, kind=kind).ap()
        if self.debug:
            self.dbg_names.append(name)
        return t

    def act(self, out, in_, func, reads, writes=(), pwrites=(), **kw):
        return self.P.op("act", lambda e: e.activation(out=out, in_=in_, func=func, **kw), reads, writes, pwrites)

    def tt(self, eng, out, in0, in1, op, reads, writes=(), pwrites=()):
        return self.P.op(eng, lambda e: e.tensor_tensor(out=out, in0=in0, in1=in1, op=op), reads, writes, pwrites)

    def stt(self, eng, out, in0, scalar, in1, op0, op1, reads, writes=(), pwrites=()):
        return self.P.op(eng, lambda e: e.scalar_tensor_tensor(out=out, in0=in0, scalar=scalar, in1=in1, op0=op0, op1=op1),
                         reads, writes, pwrites)

    def cp(self, eng, out, in_, reads, writes=(), pwrites=()):
        return self.P.op(eng, lambda e: e.tensor_copy(out=out, in_=in_), reads, writes, pwrites)

    def mm(self, out, lhsT, rhs, start, stop, reads, writes=(), pwrites=()):
        return self.P.op("pe", lambda e: e.matmul(out, lhsT=lhsT, rhs=rhs, start=start, stop=stop), reads, writes, pwrites)

    def tr(self, out, in_, ident, reads, writes=(), pwrites=()):
        return self.P.op("pe", lambda e: e.transpose(out=out, in_=in_, identity=ident), reads, writes, pwrites)

    def memset(self, eng, ap, val, writes=(), pwrites=()):
        return self.P.op(eng, lambda e: e.memset(ap, val), (), writes, pwrites)

    def load(self, dst, src, buf, dsem=None, queue="sp"):
        if dsem is None:
            dsem = self.ds()
        return self.P.dma(queue, dsem, dst, src, writes=[buf])

    def rsqrt_col(self, out, in_, scale, eps_ap, reads, buf):
        self.act(out, in_, AF.Sqrt, reads, writes=[buf], bias=eps_ap, scale=scale)
        self.P.op("dve", lambda e: e.reciprocal(out=out, in_=out), [buf], [buf])

    def wring_init(self, nbuf=3):
        A = self.A
        self.w_st = [A.f32(2048) for _ in range(nbuf)]
        self.w_bf = [A.bf16(2048) for _ in range(nbuf)]
        self.w_bst = [Buf() for _ in range(nbuf)]
        self.w_bbf = [Buf() for _ in range(nbuf)]
        self.w_ds = [self.ds() for _ in range(nbuf)]
        self.w_i = 0
        self.w_n = nbuf

    def wload(self, src, kc, n, fold=None, eng="dve"):
        i = self.w_i % self.w_n
        self.w_i += 1
        st = self.w_st[i][:, 0:kc * n].rearrange("p (c n) -> p c n", n=n)
        bf = self.w_bf[i][:, 0:kc * n].rearrange("p (c n) -> p c n", n=n)
        self.P.dma("sp", self.w_ds[i], st, src, writes=[self.w_bst[i]])
        if fold is None:
            self.cp(eng, bf, st, [self.w_bst[i]], [self.w_bbf[i]])
        else:
            self.tt(eng, bf, st, fold.unsqueeze(2).to_broadcast([128, kc, n]), ALU.mult, [self.w_bst[i]] + self.b_prm, [self.w_bbf[i]])
        return bf, self.w_bbf[i]

    def build(self):
        nc = self.nc
        with contextlib.ExitStack() as es:
            self.es = es
            self.io()
            NW = 48640
            arena = es.enter_context(nc.sbuf_tensor("arena", [128, NW], F32))
            self.A = Arena(arena, NW)
            self.psum = es.enter_context(nc.psum_tensor("psum", [128, 4096], F32))
            self.engsems = {e: es.enter_context(nc.semaphore("s_" + e)) for e in CENG}
            self.dspool = [DSem(es.enter_context(nc.semaphore("d%d" % i))) for i in range(20)]
            self.dsnext = 0
            self.P.dsems = self.dspool
            phases = [self.phase1_2, self.phase2b, self.phase3, self.phase4, self.phase5, self.phase67]
            for i, ph in enumerate(phases):
                if i >= self.upto:
                    break
                self.dsnext = 0
                ph()
                self.P.barrier()
            self.P.emit(nc, self.engsems)

    def io(self):
        nc = self.nc
        I = lambda n, s, dt=F32: nc.dram_tensor(n, list(s), dt, kind="ExternalInput").ap()
        self.x = I("x", [S, D])
        self.w_in = I("w_in", [D, DIN])
        self.w_bm = I("w_bm", [1024, D])
        self.w_bd = I("w_bd", [1024, D])
        self.w_out = I("w_out", [D, D])
        self.w_ff1 = I("w_ff1", [D, DFF])
        self.w_ff2 = I("w_ff2", [DFF, D])
        self.prm_d = I("prm", [128, NPRM])
        self.cst_d = I("cst", [128, NCST])
        self.cstb_d = I("cstb", [128, 128], BF16)
        self.y = nc.dram_tensor("y", [S, D], F32, kind="ExternalOutput").ap()
        sc = self.scratch
        self.S_qTm = sc("S_qTm", [4, 128, S], BF16)
        self.S_kTm = sc("S_kTm", [4, 128, S], BF16)
        self.S_km = sc("S_km", [4, 128, NT, 128], BF16)
        self.S_vm = sc("S_vm", [4, 128, NT, 256], BF16)
        self.S_som = sc("S_som", [4, 128, NT, 256], F32)
        self.S_gm = sc("S_gm", [128, NT, 16], F32)
        self.S_qkv = sc("S_qkv", [24, 128, S], F32)
        self.S_sz = sc("S_sz", [8, 128, NT, 128], F32)
        self.S_gd = sc("S_gd", [128, NT, 32], F32)
        self.S_sg = sc("S_sg", [32, 128, S], F32)
        self.S_qTd = sc("S_qTd", [8, 128, S], BF16)
        self.S_kTd = sc("S_kTd", [8, 128, S], BF16)
        self.S_kd = sc("S_kd", [8, 128, NT, 128], BF16)
        self.S_vd = sc("S_vd", [8, 128, NT, 128], BF16)
        self.S_ymT = sc("S_ymT", [8, 128, S], BF16)
        self.S_ydT = sc("S_ydT", [8, 128, S], BF16)
        self.S_mixT = sc("S_mixT", [16, 128, S], BF16)

    def common(self, need_prm=True):
        A = self.A
        A.reset()
        self.identb = A.bf16(128)
        self.b_identb = Buf()
        self.load(self.identb, self.cstb_d, self.b_identb)
        self.eps = A.f32(2)
        self.b_eps = Buf()
        self.memset("pool", self.eps[:, 0:1], 1e-6, pwrites=[self.b_eps])
        self.memset("pool", self.eps[:, 1:2], 1.0, pwrites=[self.b_eps])
        self.eps6 = self.eps[:, 0:1]
        self.one1 = self.eps[:, 1:2]

    def load_prm(self, off, n):
        t = self.A.f32(n)
        b = Buf()
        self.load(t, self.prm_d[:, off:off + n], b)
        return t, b

    def load_cst(self, off, n, rows=128):
        t = self.A.f32(n)
        b = Buf()
        self.load(t[0:rows, :], self.cst_d[0:rows, off:off + n], b)
        return t, b

    def phase1_2(self):
        P, A, ps = self.P, self.A, self.psum
        self.common()
        g1, b_g1 = self.load_prm(P_G1, 16)
        self.b_prm = [b_g1]
        xT = A.bf16(16 * S)
        xT3 = xT.rearrange("p (c t) -> p c t", t=S)
        b_xT = Buf()
        mark = A.off
        xin = [A.f32(D) for _ in range(2)]
        b_xin = [Buf(), Buf()]
        ds_x = [self.ds(), self.ds()]
        junk = A.f32(D)
        b_junk = Buf()
        xn = [A.bf16(D) for _ in range(2)]
        b_xn = [Buf(), Buf()]
        ssq = A.f32(16)
        rstd = A.f32(16)
        b_st = [Buf() for _ in range(16)]
        pT = [ps[:, 0:1024].bitcast(BF16), ps[:, 1024:2048].bitcast(BF16)]
        b_pT = [Buf(), Buf()]
        for t in range(NT):
            i = t % 2
            P.dma("sp", ds_x[i], xin[i], self.x[t * 128:(t + 1) * 128, :], writes=[b_xin[i]])
            self.act(junk, xin[i], AF.Square, [b_xin[i]], writes=[b_junk, b_st[t]], accum_out=ssq[:, t:t + 1])
            self.rsqrt_col(rstd[:, t:t + 1], ssq[:, t:t + 1], 1.0 / D, self.eps6, [b_st[t], self.b_eps], b_st[t])
            self.act(xn[i], xin[i], AF.Copy, [b_xin[i], b_st[t]], writes=[b_xn[i]], scale=rstd[:, t:t + 1])
            for dc in range(16):
                self.tr(pT[i][:, dc * 128:(dc + 1) * 128], xn[i][:, dc * 128:(dc + 1) * 128], self.identb,
                        [b_xn[i], self.b_identb], pwrites=[b_pT[i]])
            self.cp("dve", xT3[:, :, t * 128:(t + 1) * 128], pT[i].rearrange("p (c t) -> p c t", t=128),
                    [b_pT[i]], pwrites=[b_xT])
        P.barrier()
        A.off = mark
        self.wring_init(3)
        NS = 4
        stage = [A.f32(1024) for _ in range(NS)]
        b_stage = [Buf() for _ in range(NS)]
        ds_stage = [self.ds() for _ in range(NS)]
        preg = [ps[:, i * 1024:(i + 1) * 1024] for i in range(4)]
        b_preg = [Buf() for _ in range(4)]
        units = []
        qs = 128 ** -0.5
        for h in range(4):
            units.append(("fm", h * 128, 128, AF.Copy, 1.0, BF16, lambda hf, h=h: self.S_qTm[h][:, hf * 1024:(hf + 1) * 1024]))
        for h in range(4):
            units.append(("fm", 512 + h * 128, 128, AF.Copy, qs, BF16, lambda hf, h=h: self.S_kTm[h][:, hf * 1024:(hf + 1) * 1024]))
        for h in range(4):
            units.append(("tm", 512 + h * 128, 128, AF.Copy, qs, BF16, lambda hf, h=h: self.S_km[h][:, hf * 8:(hf + 1) * 8, :]))
        for u in range(8):
            units.append(("tm", 1024 + u * 128, 128, AF.Copy, 1.0, BF16,
                          lambda hf, u=u: self.S_vm[u // 2][:, hf * 8:(hf + 1) * 8, (u % 2) * 128:(u % 2) * 128 + 128]))
        for u in range(8):
            units.append(("tm", 2048 + u * 128, 128, AF.Sigmoid, 1.0, F32,
                          lambda hf, u=u: self.S_som[u // 2][:, hf * 8:(hf + 1) * 8, (u % 2) * 128:(u % 2) * 128 + 128]))
        units.append(("tm", 3072, 16, AF.Copy, 1.0, F32, lambda hf: self.S_gm[:, hf * 8:(hf + 1) * 8, :]))
        for u in range(24):
            units.append(("fm", 3088 + u * 128, 128, AF.Copy, 1.0, F32, lambda hf, u=u: self.S_qkv[u][:, hf * 1024:(hf + 1) * 1024]))
        for h in range(8):
            units.append(("tm", 6160 + h * 128, 128, AF.Silu, 1.0, F32, lambda hf, h=h: self.S_sz[h][:, hf * 8:(hf + 1) * 8, :]))
        units.append(("tm", 7184, 32, AF.Copy, 1.0, F32, lambda hf: self.S_gd[:, hf * 8:(hf + 1) * 8, :]))
        for u in range(32):
            units.append(("fm", 7216 + u * 128, 128, AF.Sigmoid, 1.0, F32, lambda hf, u=u: self.S_sg[u][:, hf * 1024:(hf + 1) * 1024]))
        w3 = self.w_in.rearrange("(c p) n -> p c n", p=128)
        ri = 0
        for (kind, c0, n, func, scale, odt, dst) in units:
            wbf, b_w = self.wload(w3[:, :, c0:c0 + n], 16, n, fold=g1)
            for hf in range(2):
                r = ri % 4
                s_ = ri % NS
                ri += 1
                pr = preg[r]
                if kind == "fm":
                    for tb in range(2):
                        t0 = hf * 1024 + tb * 512
                        for dc in range(16):
                            self.mm(pr[0:n, tb * 512:(tb + 1) * 512], wbf[:, dc, :], xT3[:, dc, t0:t0 + 512],
                                    dc == 0, dc == 15, [b_w, b_xT], pwrites=[b_preg[r]])
                    width = 1024
                else:
                    for t in range(8):
                        tg = hf * 8 + t
                        for dc in range(16):
                            self.mm(pr[:, t * n:(t + 1) * n], xT3[:, dc, tg * 128:(tg + 1) * 128], wbf[:, dc, :],
                                    dc == 0, dc == 15, [b_w, b_xT], pwrites=[b_preg[r]])
                    width = 8 * n
                if odt == BF16:
                    sv = stage[s_].bitcast(BF16)[:, 0:width]
                else:
                    sv = stage[s_][:, 0:width]
                self.act(sv, pr[:, 0:width], func, [b_preg[r]], writes=[b_stage[s_]], scale=scale)
                if kind == "fm":
                    P.dma("pool", ds_stage[s_], dst(hf), sv, reads=[b_stage[s_]])
                else:
                    P.dma("pool", ds_stage[s_], dst(hf), sv.rearrange("p (t n) -> p t n", n=n), reads=[b_stage[s_]])

    def phase2b(self):
        P, A, ps = self.P, self.A, self.psum
        self.common()
        cw, b_cw = self.load_prm(P_CW, 120)
        ones, b_ones = self.load_cst(C_ONES, 128)
        NB = 2
        xp = [A.f32(S + 4) for _ in range(NB)]
        b_xp = [Buf() for _ in range(NB)]
        ds_xp = [self.ds() for _ in range(NB)]
        for i in range(NB):
            self.memset("pool", xp[i][:, 0:2], 0.0, pwrites=[b_xp[i]])
            self.memset("pool", xp[i][:, S + 2:S + 4], 0.0, pwrites=[b_xp[i]])
        P.barrier()
        acc = [A.f32(S) for _ in range(NB)]
        b_acc = [Buf() for _ in range(NB)]
        sl = [A.f32(S) for _ in range(NB)]
        b_sl = [Buf() for _ in range(NB)]
        sq = [A.f32(S) for _ in range(NB)]
        b_sq = [Buf() for _ in range(NB)]
        rn = [A.f32(S) for _ in range(NB)]
        b_rn = [Buf() for _ in range(NB)]
        fb = [A.bf16(S) for _ in range(NB)]
        b_fb = [Buf() for _ in range(NB)]
        ds_fb = [self.ds() for _ in range(NB)]
        tk = [A.bf16(S) for _ in range(NB)]
        b_tk = [Buf() for _ in range(NB)]
        ds_tk = [self.ds() for _ in range(NB)]
        pss = ps[:, 0:2048]
        b_pss = Buf()
        ptr = ps[:, 2048:3072].bitcast(BF16)
        b_ptr = Buf()
        for u in range(24):
            i = u % NB
            h = u % 8
            P.dma("sp", ds_xp[i], xp[i][:, 2:S + 2], self.S_qkv[u], pwrites=[b_xp[i]])
            P.op("dve", lambda e, i=i, u=u: e.tensor_scalar_mul(out=acc[i], in0=xp[i][:, 0:S], scalar1=cw[:, u * 5:u * 5 + 1]),
                 [b_xp[i], b_cw], [b_acc[i]])
            for k in range(1, 5):
                self.stt("dve", acc[i], xp[i][:, k:k + S], cw[:, u * 5 + k:u * 5 + k + 1], acc[i], ALU.mult, ALU.add,
                         [b_xp[i], b_cw, b_acc[i]], [b_acc[i]])
            if u < 16:
                self.act(sl[i], acc[i], AF.Silu, [b_acc[i]], [b_sl[i]])
                self.tt("pool", sq[i], sl[i], sl[i], ALU.mult, [b_sl[i]], [b_sq[i]])
                for tb in range(4):
                    self.mm(pss[:, tb * 512:(tb + 1) * 512], ones, sq[i][:, tb * 512:(tb + 1) * 512], True, True,
                            [b_ones, b_sq[i]], pwrites=[b_pss])
                self.act(rn[i], pss, AF.Sqrt, [b_pss, self.b_eps], [b_rn[i]], bias=self.eps6, scale=1.0)
                P.op("dve", lambda e, i=i: e.reciprocal(out=rn[i], in_=rn[i]), [b_rn[i]], [b_rn[i]])
                sc_ = (128 ** -0.5) if u < 8 else 1.0
                self.stt("dve", fb[i], sl[i], sc_, rn[i], ALU.mult, ALU.mult, [b_sl[i], b_rn[i]], [b_fb[i]])
                dstT = self.S_qTd[h] if u < 8 else self.S_kTd[h]
                P.dma("pool", ds_fb[i], dstT, fb[i], reads=[b_fb[i]])
            else:
                self.act(fb[i], acc[i], AF.Silu, [b_acc[i]], [b_fb[i]])
            if u >= 8:
                for t in range(NT):
                    self.tr(ptr[:, t * 128:(t + 1) * 128], fb[i][:, t * 128:(t + 1) * 128], self.identb,
                            [b_fb[i], self.b_identb], pwrites=[b_ptr])
                self.cp("dve", tk[i], ptr, [b_ptr], [b_tk[i]])
                dst = self.S_kd[h] if u < 16 else self.S_vd[h]
                P.dma("pool", ds_tk[i], dst, tk[i].rearrange("p (t c) -> p t c", c=128), reads=[b_tk[i]])

    def gate_prep(self, lg, b_lg, ncol, cstt, b_cst, identf, b_identf):
        P, A, ps = self.P, self.A, self.psum
        half = ncol // 2
        n = NT * ncol
        lg3 = lg.rearrange("p (t c) -> p t c", c=ncol)
        cum = A.f32(n)
        tot = A.f32(n)
        decA = A.f32(n)
        decB = A.f32(n)
        cumT = A.f32(S)
        b = {k: Buf() for k in ("cum", "tot", "decA", "decB", "cumT", "p0", "p1", "p2", "p3", "pT")}
        p0 = ps[:, 0:n].rearrange("p (t c) -> p t c", c=ncol)
        cum3 = cum.rearrange("p (t c) -> p t c", c=ncol)
        c_ = lambda off: cstt[:, off:off + 128]
        p0b = ps[:, 256:256 + n].rearrange("p (t c) -> p t c", c=ncol)
        self.mm(ps[:, 0:n], c_(C_TRIF), lg, True, True, [b_cst, b_lg], pwrites=[b["p0"]])
        self.mm(ps[:, 256:256 + n], c_(C_TRIB), lg, True, True, [b_cst, b_lg], pwrites=[b["p0"]])
        self.cp("dve", cum3[:, :, 0:half], p0[:, :, 0:half], [b["p0"]], pwrites=[b["cum"]])
        self.cp("dve", cum3[:, :, half:ncol], p0b[:, :, half:ncol], [b["p0"]], pwrites=[b["cum"]])
        self.mm(ps[:, 512:512 + n], c_(C_BLK), lg, True, True, [b_cst, b_lg], writes=[b["p1"]])
        self.cp("dve", tot, ps[:, 512:512 + n], [b["p1"]], [b["tot"]])
        self.mm(ps[:, 1024:1024 + n], c_(C_HA), lg, True, True, [b_cst, b_lg], writes=[b["p2"]])
        self.act(decA, ps[:, 1024:1024 + n], AF.Exp, [b["p2"]], [b["decA"]])
        self.mm(ps[:, 1536:1536 + n], c_(C_HB), lg, True, True, [b_cst, b_lg], writes=[b["p3"]])
        self.act(decB, ps[:, 1536:1536 + n], AF.Exp, [b["p3"]], [b["decB"]])
        pT = ps[:, 2048:4096]
        for t in range(NT):
            self.tr(pT[0:ncol, t * 128:(t + 1) * 128], cum3[:, t, :], identf, [b["cum"], b_identf], pwrites=[b["pT"]])
        self.cp("dve", cumT[0:ncol, :], pT[0:ncol, :], [b["pT"]], [b["cumT"]])
        return dict(cum=cum, cum3=cum3, tot=tot, decA=decA, decB=decB, cumT=cumT, b=b)

    def phase3(self):
        P, A, ps = self.P, self.A, self.psum
        self.common()
        cstt, b_cst = self.load_cst(0, C_SEL8)
        sel, b_sel = self.load_cst(C_SEL8, 1024, rows=8)
        identf = cstt[:, C_IDENT:C_IDENT + 128]
        ibfb, b_ibfb = self.load_prm(P_IB, 16)
        ngm, b_ngm = self.load_prm(P_NGM, 1024)
        gm = A.f32(NT * 16)
        b_gm = Buf()
        self.load(gm, self.S_gm.rearrange("p t c -> p (t c)"), b_gm)
        gm3 = gm.rearrange("p (t c) -> p t c", c=16)
        n8 = NT * 8
        ig = A.f32(n8)
        lf = A.f32(n8)
        t1 = A.f32(n8)
        t2 = A.f32(n8)
        wst = A.f32(n8)
        b_ig, b_lf, b_t1, b_t2, b_wst = Buf(), Buf(), Buf(), Buf(), Buf()
        v3 = lambda a: a.rearrange("p (t c) -> p t c", c=8)
        bc8 = lambda a: a.unsqueeze(1).to_broadcast([128, NT, 8])
        self.tt("dve", v3(ig), gm3[:, :, 0:8], bc8(ibfb[:, 0:8]), ALU.add, [b_gm, b_ibfb], [b_ig])
        self.tt("dve", v3(t1), gm3[:, :, 8:16], bc8(ibfb[:, 8:16]), ALU.add, [b_gm, b_ibfb], [b_t1])
        self.act(t2, t1, AF.Abs, [b_t1], [b_t2])
        self.act(t2, t2, AF.Exp, [b_t2], [b_t2], scale=-1.0)
        self.act(t2, t2, AF.Ln, [b_t2, self.b_eps], [b_t2], bias=self.one1, scale=1.0)
        P.op("dve", lambda e: e.tensor_scalar_min(out=t1, in0=t1, scalar1=0.0), [b_t1], [b_t1])
        self.tt("dve", lf, t1, t2, ALU.subtract, [b_t1, b_t2], [b_lf])
        G = self.gate_prep(lf, b_lf, 8, cstt, b_cst, identf, b_cst)
        gb = G["b"]
        self.tt("dve", t1, G["tot"], G["cum"], ALU.subtract, [gb["tot"], gb["cum"]], [b_t1])
        self.tt("dve", t1, t1, ig, ALU.add, [b_t1, b_ig], [b_t1])
        self.act(wst, t1, AF.Exp, [b_t1], [b_wst])
        wst3, ig3 = v3(wst), v3(ig)
        decA3, decB3 = v3(G["decA"]), v3(G["decB"])
        cum3 = G["cum3"]
        P.barrier()
        qT = A.bf16(S); kT = A.bf16(S); ktok = A.bf16(S); vext = A.bf16(NT * 264)
        so = A.f32(NT * 256); hsum = A.f32(NT * 256); ymT = A.bf16(2 * S)
        E = A.f32(S); qg = A.bf16(S); WT = A.f32(S); Dm = A.f32(S); kw = A.bf16(S)
        Cf = A.f32(264)
        Cb = [A.bf16(264) for _ in range(3)]
        PT = [A.bf16(128) for _ in range(2)]
        den = A.f32(4)
        ytmp = [A.f32(256) for _ in range(2)]
        ybf = [A.bf16(256) for _ in range(2)]
        fin = A.f32(8)
        vext3 = vext.rearrange("p (t c) -> p t c", c=264)
        so3 = so.rearrange("p (t c) -> p t c", c=256)
        hs3 = hsum.rearrange("p (t c) -> p t c", c=256)
        ktok3 = ktok.rearrange("p (t c) -> p t c", c=128)
        kw3 = kw.rearrange("p (t c) -> p t c", c=128)
        WT3 = WT.rearrange("p (t c) -> p t c", c=128)
        Dm3 = Dm.rearrange("p (t c) -> p t c", c=128)
        ymT3 = ymT.rearrange("p (h t) -> p h t", t=S)
        B = {k: Buf() for k in ("qT", "kT", "ktok", "vext", "so", "ymT", "E", "qg", "WT", "Dm", "kw", "Cf", "bc",
                                "ones")}
        b_hs = [Buf() for _ in range(NT)]
        b_Cb = [Buf() for _ in range(3)]
        b_PT = [Buf(), Buf()]
        b_den = [Buf(), Buf()]
        b_yt = [Buf(), Buf()]
        b_yb = [Buf(), Buf()]
        b_fin = [Buf(), Buf()]
        dsl = {k: self.ds() for k in ("qT", "kT", "ktok", "vext", "so", "ymT")}
        pbc = ps[:, 0:2048]
        pST = [ps[:, 2048:2176], ps[:, 2176:2304]]
        b_pST = [Buf(), Buf()]
        ptr = ps[:, 2304:2432].bitcast(BF16)
        b_ptr = Buf()
        pnum = [ps[:, 2560:2560 + 257], ps[:, 3584:3584 + 257]]
        b_pnum = [Buf(), Buf()]
        pC = ps[:, 3072:3072 + 257]
        b_pC = Buf()
        self.memset("pool", vext3[:, :, 256:257], 1.0, pwrites=[B["ones"]])
        P.barrier()
        for h in range(4):
            P.dma("sp", dsl["qT"], qT, self.S_qTm[h], writes=[B["qT"]])
            P.dma("sp", dsl["kT"], kT, self.S_kTm[h], writes=[B["kT"]])
            P.dma("sp", dsl["ktok"], ktok3, self.S_km[h], writes=[B["ktok"]])
            P.dma("sp", dsl["vext"], vext3[:, :, 0:256], self.S_vm[h], writes=[B["vext"]])
            P.dma("sp", dsl["so"], so3, self.S_som[h], writes=[B["so"]])
            for dr in range(2):
                c = dr * 4 + h
                mn = cstt[:, C_MNF:C_MNF + 128] if dr == 0 else cstt[:, C_MNB:C_MNB + 128]
                for tb in range(4):
                    self.mm(pbc[:, tb * 512:(tb + 1) * 512], sel[0:8, c * 128:(c + 1) * 128],
                            G["cumT"][0:8, tb * 512:(tb + 1) * 512], True, True, [b_sel, gb["cumT"]], pwrites=[B["bc"]])
                self.act(E, pbc, AF.Exp, [B["bc"]], [B["E"]])
                self.tt("dve", qg, qT, E, ALU.mult, [B["qT"], B["E"]], [B["qg"]])
                self.tt("dve", Dm3, pbc.rearrange("p (t c) -> p t c", c=128),
                        cum3[:, :, c:c + 1].to_broadcast([128, NT, 128]), ALU.subtract, [B["bc"], gb["cum"]], [B["Dm"]])
                self.tt("dve", Dm3, Dm3, mn.unsqueeze(1).to_broadcast([128, NT, 128]), ALU.min, [B["Dm"], b_cst], [B["Dm"]])
                self.tt("pool", Dm3, Dm3, ig3[:, :, c:c + 1].to_broadcast([128, NT, 128]), ALU.add, [B["Dm"], b_ig], [B["Dm"]])
                self.act(WT, Dm, AF.Exp, [B["Dm"]], [B["WT"]])
                self.tt("dve", kw3, ktok3, wst3[:, :, c:c + 1].to_broadcast([128, NT, 128]), ALU.mult,
                        [B["ktok"], b_wst], [B["kw"]])
                self.memset("pool", Cf, 0.0, writes=[B["Cf"]])
                ci = 0
                self.memset("pool", Cb[0], 0.0, writes=[b_Cb[0]])
                tiles = range(NT) if dr == 0 else range(NT - 1, -1, -1)
                for it, t in enumerate(tiles):
                    j = it % 2
                    tok = slice(t * 128, (t + 1) * 128)
                    self.mm(pST[j], kT[:, tok], qT[:, tok], True, True, [B["kT"], B["qT"]], writes=[b_pST[j]])
                    self.tt("dve", PT[j], pST[j], WT3[:, t, :], ALU.mult, [b_pST[j], B["WT"]], [b_PT[j]])
                    chunks = (0, 1) if dr == 0 else (1, 0)
                    for ck in chunks:
                        rows = slice(ck * 64, ck * 64 + 64)
                        ctok = slice(t * 128 + ck * 64, t * 128 + ck * 64 + 64)
                        self.mm(pnum[j][rows, :], qg[:, ctok], Cb[ci][:, 0:257], True, False,
                                [B["qg"], b_Cb[ci]], pwrites=[b_pnum[j]])
                        self.mm(pC, kw3[rows, t, :], vext3[rows, t, 0:257], True, True,
                                [B["kw"], B["vext"], B["ones"]], writes=[b_pC])
                        dec = (decA3 if ck == 0 else decB3)[:, t, c:c + 1]
                        bdec = gb["decA"] if ck == 0 else gb["decB"]
                        cn = (ci + 1) % 3
                        self.stt("dve", Cb[cn][:, 0:257], Cf[:, 0:257], dec, pC, ALU.mult, ALU.add,
                                 [B["Cf"], bdec, b_pC], [b_Cb[cn]])
                        self.stt("dve", Cf[:, 0:257], Cf[:, 0:257], dec, pC, ALU.mult, ALU.add,
                                 [B["Cf"], bdec, b_pC], [B["Cf"]])
                        ci = cn
                    self.mm(pnum[j], PT[j], vext3[:, t, 0:257], False, True, [b_PT[j], B["vext"], B["ones"]],
                            pwrites=[b_pnum[j]])
                    dn = den[:, 2 * j:2 * j + 1]
                    rc = den[:, 2 * j + 1:2 * j + 2]
                    self.act(dn, pnum[j][:, 256:257], AF.Abs, [b_pnum[j]], [b_den[j]])
                    P.op("dve", lambda e, dn=dn: e.tensor_scalar_max(out=dn, in0=dn, scalar1=1.0), [b_den[j]], [b_den[j]])
                    P.op("dve", lambda e, dn=dn, rc=rc: e.reciprocal(out=rc, in_=dn), [b_den[j]], [b_den[j]])
                    if dr == 0:
                        self.act(hs3[:, t, :], pnum[j][:, 0:256], AF.Copy, [b_pnum[j], b_den[j]], [b_hs[t]], scale=rc)
                    else:
                        self.stt("dve", hs3[:, t, :], pnum[j][:, 0:256], rc, hs3[:, t, :], ALU.mult, ALU.add,
                                 [b_pnum[j], b_den[j], b_hs[t]], [b_hs[t]])
                        fs = fin[:, 2 * j:2 * j + 1]
                        fr = fin[:, 2 * j + 1:2 * j + 2]
                        self.act(ytmp[j], hs3[:, t, :], AF.Square, [b_hs[t]], [b_yt[j], b_fin[j]], accum_out=fs)
                        self.rsqrt_col(fr, fs, 1.0 / 256, self.eps6, [b_fin[j], self.b_eps], b_fin[j])
                        self.stt("dve", ytmp[j], hs3[:, t, :], fr, ngm[:, h * 256:(h + 1) * 256], ALU.mult, ALU.mult,
                                 [b_hs[t], b_fin[j], b_ngm], [b_yt[j]])
                        self.tt("pool", ybf[j], ytmp[j], so3[:, t, :], ALU.mult, [b_yt[j], B["so"]], [b_yb[j]])
                        for hh in range(2):
                            self.tr(ptr[:, hh * 128:(hh + 1) * 128], ybf[j][:, hh * 128:(hh + 1) * 128], self.identb,
                                    [b_yb[j], self.b_identb], pwrites=[b_ptr])
                        self.cp("dve", ymT3[:, :, tok], ptr.rearrange("p (h t) -> p h t", t=128), [b_ptr], pwrites=[B["ymT"]])
            for hh in range(2):
                P.dma("pool", dsl["ymT"], self.S_ymT[h * 2 + hh], ymT3[:, hh, :], reads=[B["ymT"]])

    def phase4(self):
        P, A, ps = self.P, self.A, self.psum
        self.common()
        cstt, b_cst = self.load_cst(0, C_SEL8)
        sel, b_sel = self.load_cst(C_SEL16, 2048, rows=16)
        identf = cstt[:, C_IDENT:C_IDENT + 128]
        adt, b_adt = self.load_prm(P_AL, 32)
        ngd, b_ngd = self.load_prm(P_NGD, 128)
        gd = A.f32(NT * 32)
        b_gd = Buf()
        self.load(gd, self.S_gd.rearrange("p t c -> p (t c)"), b_gd)
        gd3 = gd.rearrange("p (t c) -> p t c", c=32)
        n16 = NT * 16
        t1 = A.f32(n16); t2 = A.f32(n16); g = A.f32(n16); beta = A.f32(n16); nbeta = A.f32(n16)
        bg = A.f32(n16); ekd = A.f32(n16); eal = A.f32(16)
        b_t1, b_t2, b_g, b_beta, b_nb, b_bg, b_ekd, b_eal = [Buf() for _ in range(8)]
        v3 = lambda a: a.rearrange("p (t c) -> p t c", c=16)
        bc16 = lambda a: a.unsqueeze(1).to_broadcast([128, NT, 16])
        self.tt("dve", v3(t1), gd3[:, :, 0:16], bc16(adt[:, 16:32]), ALU.add, [b_gd, b_adt], [b_t1])
        self.act(t2, t1, AF.Abs, [b_t1], [b_t2])
        self.act(t2, t2, AF.Exp, [b_t2], [b_t2], scale=-1.0)
        self.act(t2, t2, AF.Ln, [b_t2, self.b_eps], [b_t2], bias=self.one1, scale=1.0)
        P.op("dve", lambda e: e.tensor_scalar_max(out=t1, in0=t1, scalar1=0.0), [b_t1], [b_t1])
        self.tt("dve", t1, t1, t2, ALU.add, [b_t1, b_t2], [b_t1])
        self.act(eal, adt[:, 0:16], AF.Exp, [b_adt], [b_eal])
        self.stt("dve", v3(g), v3(t1), -1.0, bc16(eal), ALU.mult, ALU.mult, [b_t1, b_eal], [b_g])
        self.act(v3(beta), gd3[:, :, 16:32], AF.Sigmoid, [b_gd], [b_beta])
        P.op("dve", lambda e: e.tensor_scalar_mul(out=nbeta, in0=beta, scalar1=-1.0), [b_beta], [b_nb])
        G = self.gate_prep(g, b_g, 16, cstt, b_cst, identf, b_cst)
        gb = G["b"]
        self.act(t2, G["cum"], AF.Exp, [gb["cum"]], [b_t2])
        self.tt("dve", bg, beta, t2, ALU.mult, [b_beta, b_t2], [b_bg])
        self.tt("dve", t1, G["tot"], G["cum"], ALU.subtract, [gb["tot"], gb["cum"]], [b_t1])
        self.act(ekd, t1, AF.Exp, [b_t1], [b_ekd])
        beta3, nbeta3, bg3, ekd3 = v3(beta), v3(nbeta), v3(bg), v3(ekd)
        decA3, decB3 = v3(G["decA"]), v3(G["decB"])
        cum3 = G["cum3"]
        P.barrier()
        qT = A.bf16(S); kT = A.bf16(S); ktok = A.bf16(S); vtok = A.bf16(S)
        sz = A.f32(S); osum = A.f32(S); ydT = A.bf16(S)
        qg = A.bf16(S); X = A.f32(S); Gam = A.f32(S); GamT = A.f32(S)
        kbg = A.bf16(S); vb = A.bf16(S); kd = A.bf16(S)
        u_all = A.f32(S); wT = A.bf16(S); attnT = A.bf16(S)
        GN = 2
        Xs = [[A.f32(128) for _ in range(2)] for _ in range(GN)]
        Ys = [[A.f32(128) for _ in range(2)] for _ in range(GN)]
        TTs = [[A.f32(128) for _ in range(2)] for _ in range(GN)]
        TTb = [A.bf16(128) for _ in range(GN)]
        Sf = A.f32(128)
        Sb = [A.bf16(128) for _ in range(3)]
        vnew = [A.bf16(128) for _ in range(2)]
        ytmp = [A.f32(128) for _ in range(2)]
        ybf = [A.bf16(128) for _ in range(2)]
        fin = A.f32(4)
        r3 = lambda a: a.rearrange("p (t c) -> p t c", c=128)
        ktok3, vtok3, sz3, os3 = r3(ktok), r3(vtok), r3(sz), r3(osum)
        X3, Gam3, GamT3, kbg3, vb3, kd3, u3, at3 = r3(X), r3(Gam), r3(GamT), r3(kbg), r3(vb), r3(kd), r3(u_all), r3(attnT)
        B = {k: Buf() for k in ("qT", "kT", "ktok", "vtok", "sz", "ydT", "qg", "X", "Gam", "GamT", "kbg", "vb", "kd",
                                "u", "wT", "attnT", "Sf", "bc")}
        b_os = [Buf() for _ in range(NT)]
        b_Xs = [[Buf(), Buf()] for _ in range(GN)]
        b_Ys = [[Buf(), Buf()] for _ in range(GN)]
        b_TTs = [[Buf(), Buf()] for _ in range(GN)]
        b_TTb = [Buf() for _ in range(GN)]
        b_Sb = [Buf() for _ in range(3)]
        b_vn = [Buf(), Buf()]
        b_yt = [Buf(), Buf()]
        b_yb = [Buf(), Buf()]
        b_fin = [Buf(), Buf()]
        dsl = {k: self.ds() for k in ("qT", "kT", "ktok", "vtok", "sz", "ydT")}
        pbc = ps[:, 0:2048]
        pX = [ps[:, (gi * 3) * 512:(gi * 3) * 512 + 128] for gi in range(GN)]
        pY = [ps[:, (gi * 3 + 1) * 512:(gi * 3 + 1) * 512 + 128] for gi in range(GN)]
        pTT = [ps[:, (gi * 3 + 2) * 512:(gi * 3 + 2) * 512 + 128] for gi in range(GN)]
        b_pX = [Buf() for _ in range(GN)]
        b_pY = [Buf() for _ in range(GN)]
        b_pTT = [Buf() for _ in range(GN)]
        pA = [ps[:, (6 + gi) * 512:(6 + gi) * 512 + 128] for gi in range(GN)]
        b_pA = [Buf() for _ in range(GN)]
        pB = [ps[:, (6 + gi) * 512 + 128:(6 + gi) * 512 + 256] for gi in range(GN)]
        b_pB = [Buf() for _ in range(GN)]
        pCq = [ps[:, (6 + gi) * 512 + 256:(6 + gi) * 512 + 384] for gi in range(GN)]
        b_pCq = [Buf() for _ in range(GN)]
        pvn = ps[:, 0:128]; b_pvn = Buf()
        po = [ps[:, 512:640], ps[:, 1024:1152]]; b_po = [Buf(), Buf()]
        pS = ps[:, 1536:1664]; b_pS = Buf()
        ptr = ps[:, 2048:2112].bitcast(BF16); b_ptr = Buf()
        import os
        K4H = int(os.environ.get("K4H", "8")); K4S = os.environ.get("K4S", "C")
        for h in range(K4H):
            P.dma("sp", dsl["qT"], qT, self.S_qTd[h], writes=[B["qT"]])
            P.dma("sp", dsl["kT"], kT, self.S_kTd[h], writes=[B["kT"]])
            P.dma("sp", dsl["ktok"], ktok3, self.S_kd[h], writes=[B["ktok"]])
            P.dma("sp", dsl["vtok"], vtok3, self.S_vd[h], writes=[B["vtok"]])
            P.dma("sp", dsl["sz"], sz3, self.S_sz[h], writes=[B["sz"]])
            for dr in range(2):
                c = dr * 8 + h
                mn = cstt[:, C_MNF:C_MNF + 128] if dr == 0 else cstt[:, C_MNB:C_MNB + 128]
                mp = cstt[:, C_MPF:C_MPF + 128] if dr == 0 else cstt[:, C_MPB:C_MPB + 128]
                for tb in range(4):
                    self.mm(pbc[:, tb * 512:(tb + 1) * 512], sel[0:16, c * 128:(c + 1) * 128],
                            G["cumT"][0:16, tb * 512:(tb + 1) * 512], True, True, [b_sel, gb["cumT"]], pwrites=[B["bc"]])
                self.act(Gam, pbc, AF.Exp, [B["bc"]], [B["Gam"]])
                self.tt("dve", qg, qT, Gam, ALU.mult, [B["qT"], B["Gam"]], [B["qg"]])
                self.tt("dve", X3, pbc.rearrange("p (t c) -> p t c", c=128),
                        cum3[:, :, c:c + 1].to_broadcast([128, NT, 128]), ALU.subtract, [B["bc"], gb["cum"]], [B["X"]])
                self.tt("dve", GamT3, X3, mn.unsqueeze(1).to_broadcast([128, NT, 128]), ALU.min, [B["X"], b_cst], [B["GamT"]])
                self.act(GamT, GamT, AF.Exp, [B["GamT"]], [B["GamT"]])
                self.tt("dve", Gam3, X3, mp.unsqueeze(1).to_broadcast([128, NT, 128]), ALU.max, [B["X"], b_cst, B["qg"]], [B["Gam"]])
                self.act(Gam, Gam, AF.Exp, [B["Gam"]], [B["Gam"]], scale=-1.0)
                bcc = lambda a3: a3[:, :, c:c + 1].to_broadcast([128, NT, 128])
                self.tt("dve", kbg3, ktok3, bcc(bg3), ALU.mult, [B["ktok"], b_bg], [B["kbg"]])
                self.tt("dve", vb3, vtok3, bcc(beta3), ALU.mult, [B["vtok"], b_beta], [B["vb"]])
                self.tt("dve", kd3, ktok3, bcc(ekd3), ALU.mult, [B["ktok"], b_ekd], [B["kd"]])
                P.barrier()
                if K4S == "A":
                    continue
                K4B = int(os.environ.get("K4B", "9"))
                for g0 in range(0, NT, GN):
                    cur = [0] * GN
                    for gi in range(GN):
                        t = g0 + gi
                        tok = slice(t * 128, (t + 1) * 128)
                        self.mm(pX[gi], kT[:, tok], kT[:, tok], True, True, [B["kT"]], writes=[b_pX[gi]])
                        self.stt("dve", Xs[gi][0], pX[gi], nbeta3[:, t, c:c + 1], Gam3[:, t, :], ALU.mult, ALU.mult,
                                 [b_pX[gi], b_nb, B["Gam"]], [b_Xs[gi][0]])
                    if K4B < 2:
                        continue
                    for gi in range(GN):
                        self.tr(pY[gi], Xs[gi][0], identf, [b_Xs[gi][0], b_cst], writes=[b_pY[gi]])
                    for gi in range(GN):
                        self.cp("dve", Ys[gi][0], pY[gi], [b_pY[gi]], [b_Ys[gi][0]])
                        self.tt("dve", TTs[gi][0], pY[gi], identf, ALU.add, [b_pY[gi], b_cst], [b_TTs[gi][0]])
                    if K4B < 3:
                        continue
                    for lv in range(1, 6):
                        a, b_ = (lv - 1) % 2, lv % 2
                        for gi in range(GN):
                            self.mm(pX[gi], Ys[gi][a], Xs[gi][a], True, True, [b_Ys[gi][a], b_Xs[gi][a]], writes=[b_pX[gi]])
                            if lv < 5:
                                self.mm(pY[gi], Xs[gi][a], Ys[gi][a], True, True, [b_Ys[gi][a], b_Xs[gi][a]], writes=[b_pY[gi]])
                        for gi in range(GN):
                            self.act(Xs[gi][b_], pX[gi], AF.Copy, [b_pX[gi]], [b_Xs[gi][b_]])
                            if lv < 5:
                                self.cp("dve", Ys[gi][b_], pY[gi], [b_pY[gi]], [b_Ys[gi][b_]])
                        for gi in range(GN):
                            self.mm(pTT[gi], Xs[gi][b_], TTs[gi][a], True, True, [b_Xs[gi][b_], b_TTs[gi][a]], writes=[b_pTT[gi]])
                        for gi in range(GN):
                            if lv < 5:
                                self.tt("dve", TTs[gi][b_], pTT[gi], TTs[gi][a], ALU.add, [b_pTT[gi], b_TTs[gi][a]], [b_TTs[gi][b_]])
                            else:
                                self.tt("dve", TTb[gi], pTT[gi], TTs[gi][a], ALU.add, [b_pTT[gi], b_TTs[gi][a]], [b_TTb[gi]])
                    if K4B < 4:
                        continue
                    for gi in range(GN):
                        t = g0 + gi
                        tok = slice(t * 128, (t + 1) * 128)
                        self.mm(pA[gi], TTb[gi], vb3[:, t, :], True, True, [b_TTb[gi], B["vb"]], writes=[b_pA[gi]])
                        self.act(u3[:, t, :], pA[gi], AF.Copy, [b_pA[gi]], pwrites=[B["u"]])
                        if K4B < 5:
                            continue
                        self.mm(pB[gi], kbg3[:, t, :], TTb[gi], True, True, [b_TTb[gi], B["kbg"]], writes=[b_pB[gi]])
                        self.act(wT[:, tok], pB[gi], AF.Copy, [b_pB[gi]], pwrites=[B["wT"]])
                P.barrier()
                if K4B >= 6:
                    for t in range(NT):
                        tok = slice(t * 128, (t + 1) * 128)
                        gi = t % 2
                        pq = ps[:, gi * 512:gi * 512 + 128]
                        self.mm(pq, kT[:, tok], qT[:, tok], True, True, [B["kT"], B["qT"]], writes=[b_pCq[gi]])
                        self.tt("dve", at3[:, t, :], pq, GamT3[:, t, :], ALU.mult, [b_pCq[gi], B["GamT"]], pwrites=[B["attnT"]])
                P.barrier()
                if K4S == "B":
                    continue
                self.memset("pool", Sf, 0.0, writes=[B["Sf"]])
                self.memset("pool", Sb[0], 0.0, writes=[b_Sb[0]])
                si = 0
                tiles = range(NT) if dr == 0 else range(NT - 1, -1, -1)
                for it, t in enumerate(tiles):
                    j = it % 2
                    tok = slice(t * 128, (t + 1) * 128)
                    chunks = (0, 1) if dr == 0 else (1, 0)
                    for ck in chunks:
                        rows = slice(ck * 64, ck * 64 + 64)
                        ctok = slice(t * 128 + ck * 64, t * 128 + ck * 64 + 64)
                        self.mm(pvn[rows, :], wT[:, ctok], Sb[si], True, True, [B["wT"], b_Sb[si]], pwrites=[b_pvn])
                        self.tt("dve", vnew[j][rows, :], u3[rows, t, :], pvn[rows, :], ALU.subtract, [B["u"], b_pvn], pwrites=[b_vn[j]])
                        self.mm(po[j][rows, :], qg[:, ctok], Sb[si], True, False, [B["qg"], b_Sb[si]], pwrites=[b_po[j]])
                        self.mm(pS, kd3[rows, t, :], vnew[j][rows, :], True, True, [B["kd"], b_vn[j]], writes=[b_pS])
                        dec = (decA3 if ck == 0 else decB3)[:, t, c:c + 1]
                        bdec = gb["decA"] if ck == 0 else gb["decB"]
                        sn = (si + 1) % 3
                        self.stt("dve", Sb[sn], Sf, dec, pS, ALU.mult, ALU.add, [B["Sf"], bdec, b_pS], [b_Sb[sn]])
                        self.stt("dve", Sf, Sf, dec, pS, ALU.mult, ALU.add, [B["Sf"], bdec, b_pS], [B["Sf"]])
                        si = sn
                    self.mm(po[j], at3[:, t, :], vnew[j], False, True, [B["attnT"], b_vn[j]], pwrites=[b_po[j]])
                    if dr == 0:
                        self.act(os3[:, t, :], po[j], AF.Copy, [b_po[j]], [b_os[t]])
                    else:
                        self.tt("dve", os3[:, t, :], po[j], os3[:, t, :], ALU.add, [b_po[j], b_os[t]], [b_os[t]])
                        fs = fin[:, 2 * j:2 * j + 1]
                        fr = fin[:, 2 * j + 1:2 * j + 2]
                        self.act(ytmp[j], os3[:, t, :], AF.Square, [b_os[t]], [b_yt[j], b_fin[j]], accum_out=fs)
                        self.rsqrt_col(fr, fs, 1.0 / 128, self.eps6, [b_fin[j], self.b_eps], b_fin[j])
                        self.stt("dve", ytmp[j], os3[:, t, :], fr, ngd, ALU.mult, ALU.mult, [b_os[t], b_fin[j], b_ngd], [b_yt[j]])
                        self.tt("pool", ybf[j], ytmp[j], sz3[:, t, :], ALU.mult, [b_yt[j], B["sz"]], [b_yb[j]])
                        self.tr(ptr, ybf[j], self.identb, [b_yb[j], self.b_identb], writes=[b_ptr])
                        self.cp("dve", ydT[:, tok], ptr, [b_ptr], pwrites=[B["ydT"]])
                P.barrier()
            P.dma("pool", dsl["ydT"], self.S_ydT[h], ydT, reads=[B["ydT"]])
            P.barrier()

    def phase5(self):
        P, A, ps = self.P, self.A, self.psum
        self.common()
        self.b_prm = []
        ym = A.bf16(8 * S); yd = A.bf16(8 * S)
        b_ym, b_yd = Buf(), Buf()
        ym3 = ym.rearrange("p (c t) -> p c t", t=S)
        yd3 = yd.rearrange("p (c t) -> p c t", t=S)
        self.load(ym3, self.S_ymT.rearrange("c p t -> p c t"), b_ym)
        self.load(yd3, self.S_ydT.rearrange("c p t -> p c t"), b_yd)
        self.wring_init(4)
        sg = [[A.f32(S) for _ in range(2)] for _ in range(2)]
        b_sg = [[Buf(), Buf()] for _ in range(2)]
        ds_sg = [[self.ds(), self.ds()] for _ in range(2)]
        tmp = [A.f32(S) for _ in range(2)]
        b_tmp = [Buf(), Buf()]
        mixb = [A.bf16(S) for _ in range(2)]
        b_mix = [Buf(), Buf()]
        ds_mix = [self.ds(), self.ds()]
        pm = ps[:, 0:2048]; pd = ps[:, 2048:4096]
        b_pm, b_pd = Buf(), Buf()
        wm3 = self.w_bm.rearrange("(c p) n -> p c n", p=128)
        wd3 = self.w_bd.rearrange("(c p) n -> p c n", p=128)
        for cg in range(16):
            i = cg % 2
            wm, b_wm = self.wload(wm3[:, :, cg * 128:(cg + 1) * 128], 8, 128)
            wd, b_wd = self.wload(wd3[:, :, cg * 128:(cg + 1) * 128], 8, 128, eng="pool")
            P.dma("sp", ds_sg[i][0], sg[i][0], self.S_sg[cg], writes=[b_sg[i][0]])
            P.dma("sp", ds_sg[i][1], sg[i][1], self.S_sg[16 + cg], writes=[b_sg[i][1]])
            for tb in range(4):
                for kc in range(8):
                    self.mm(pm[:, tb * 512:(tb + 1) * 512], wm[:, kc, :], ym3[:, kc, tb * 512:(tb + 1) * 512],
                            kc == 0, kc == 7, [b_wm, b_ym], pwrites=[b_pm])
            for tb in range(4):
                for kc in range(8):
                    self.mm(pd[:, tb * 512:(tb + 1) * 512], wd[:, kc, :], yd3[:, kc, tb * 512:(tb + 1) * 512],
                            kc == 0, kc == 7, [b_wd, b_yd], pwrites=[b_pd])
            self.tt("dve", tmp[i], pm, sg[i][0], ALU.mult, [b_pm, b_sg[i][0]], [b_tmp[i]])
            self.tt("dve", sg[i][1], pd, sg[i][1], ALU.mult, [b_pd, b_sg[i][1]], [b_sg[i][1]])
            self.tt("pool", mixb[i], tmp[i], sg[i][1], ALU.add, [b_tmp[i], b_sg[i][1]], [b_mix[i]])
            P.dma("pool", ds_mix[i], self.S_mixT[cg], mixb[i], reads=[b_mix[i]])

    def phase67(self):
        P, A, ps = self.P, self.A, self.psum
        self.common()
        g2, b_g2 = self.load_prm(P_G2, 16)
        self.b_prm = [b_g2]
        gF, b_gF = self.load_prm(P_GF, 2048)
        identf, b_idf = self.load_cst(C_IDENT, 128)
        self.wring_init(4)
        mix = A.bf16(16 * 512); b_mixb = Buf(); ds_mixl = self.ds()
        mix3 = mix.rearrange("p (c t) -> p c t", t=512)
        h2T3 = mix3
        b_h2T = b_mixb
        x2 = A.f32(4 * D); b_x2 = [Buf() for _ in range(4)]; ds_x2 = [self.ds() for _ in range(4)]
        x23 = x2.rearrange("p (t c) -> p t c", c=D)
        actT = A.bf16(64 * 512); b_actT = Buf()
        actT3 = actT.rearrange("p (c t) -> p c t", t=512)
        junk = A.bf16(D); b_junk = Buf()
        xn1 = A.bf16(D); b_xn1 = Buf()
        st = A.f32(16); b_st = [Buf() for _ in range(4)]
        rl = [A.f32(512) for _ in range(2)]; b_rl = [Buf(), Buf()]
        ds_yo = [self.ds() for _ in range(4)]
        pf = [ps[:, m * 512:(m + 1) * 512] for m in range(4)]; b_pf = [Buf() for _ in range(4)]
        pacc = [pf[0], pf[1]]; b_pacc = [b_pf[0], b_pf[1]]
        ptrF = ps[:, 2048:2560]; b_ptrF = Buf()
        pT1 = ps[:, 2560:3584].bitcast(BF16); b_pT1 = Buf()
        wo3 = self.w_out.rearrange("(c p) n -> p c n", p=128)
        w13 = self.w_ff1.rearrange("(c p) n -> p c n", p=128)
        w23 = self.w_ff2.rearrange("(c p) n -> p c n", p=128)
        ai = 0
        wi = 0
        for tb in range(4):
            P.dma("sp", ds_mixl, mix3, self.S_mixT.rearrange("c p t -> p c t")[:, :, tb * 512:(tb + 1) * 512], writes=[b_mixb])
            for tt_ in range(4):
                tg = tb * 4 + tt_
                P.dma("sp", ds_x2[tt_], x23[:, tt_, :], self.x[tg * 128:(tg + 1) * 128, :], writes=[b_x2[tt_]])
            for cu in range(16):
                wbf, b_w = self.wload(wo3[:, :, cu * 128:(cu + 1) * 128], 16, 128, eng=("dve" if cu % 2 == 0 else "pool"))
                a = ai % 2
                ai += 1
                for tt_ in range(4):
                    for kc in range(16):
                        self.mm(pacc[a][:, tt_ * 128:(tt_ + 1) * 128], mix3[:, kc, tt_ * 128:(tt_ + 1) * 128], wbf[:, kc, :],
                                kc == 0, kc == 15, [b_mixb, b_w], pwrites=[b_pacc[a]])
                self.tt("dve", x23[:, :, cu * 128:(cu + 1) * 128], pacc[a][:, 0:512].rearrange("p (t c) -> p t c", c=128),
                        x23[:, :, cu * 128:(cu + 1) * 128], ALU.add, [b_pacc[a]] + b_x2, pwrites=b_x2)
            for tt_ in range(4):
                self.act(junk, x23[:, tt_, :], AF.Square, [b_x2[tt_]], [b_junk, b_st[tt_]], accum_out=st[:, tt_:tt_ + 1])
                self.rsqrt_col(st[:, 4 + tt_:5 + tt_], st[:, tt_:tt_ + 1], 1.0 / D, self.eps6, [b_st[tt_], self.b_eps], b_st[tt_])
                self.act(xn1, x23[:, tt_, :], AF.Copy, [b_x2[tt_], b_st[tt_]], [b_xn1], scale=st[:, 4 + tt_:5 + tt_])
                for dc in range(16):
                    self.tr(pT1[:, dc * 128:(dc + 1) * 128], xn1[:, dc * 128:(dc + 1) * 128], self.identb,
                            [b_xn1, self.b_identb], pwrites=[b_pT1])
                if tt_ == 0:
                    self.cp("dve", h2T3[:, :, 0:128], pT1.rearrange("p (c t) -> p c t", t=128), [b_pT1], writes=[b_h2T])
                else:
                    self.cp("dve", h2T3[:, :, tt_ * 128:(tt_ + 1) * 128], pT1.rearrange("p (c t) -> p c t", t=128),
                            [b_pT1], pwrites=[b_h2T])
            for fg in range(16):
                for kp in range(4):
                    wi += 1
                    wbf, b_w = self.wload(w13[:, kp * 4:(kp + 1) * 4, fg * 512:(fg + 1) * 512], 4, 512,
                                          fold=g2[:, kp * 4:(kp + 1) * 4], eng=("dve" if wi % 2 == 0 else "pool"))
                    for m in range(4):
                        for dl in range(4):
                            dc = kp * 4 + dl
                            self.mm(pf[m], wbf[:, dl, m * 128:(m + 1) * 128], h2T3[:, dc, :], dc == 0, dc == 15,
                                    [b_w, b_h2T], pwrites=[b_pf[m]])
                for m in range(4):
                    fu = fg * 4 + m
                    r = m % 2
                    self.act(rl[r], pf[m], AF.Relu, [b_pf[m]], [b_rl[r]])
                    self.tt("dve" if m % 2 == 1 else "pool", actT3[:, fu, :], rl[r], rl[r], ALU.mult, [b_rl[r]], pwrites=[b_actT])
            for cg in range(4):
                for piece in range(16):
                    wi += 1
                    wbf, b_w = self.wload(w23[:, piece * 4:(piece + 1) * 4, cg * 512:(cg + 1) * 512], 4, 512,
                                          eng=("dve" if wi % 2 == 0 else "pool"))
                    for m in range(4):
                        for kl in range(4):
                            kc = piece * 4 + kl
                            self.mm(pf[m], wbf[:, kl, m * 128:(m + 1) * 128], actT3[:, kc, :], kc == 0, kc == 63,
                                    [b_actT, b_w], pwrites=[b_pf[m]])
                for m in range(4):
                    cu = cg * 4 + m
                    r = m % 2
                    self.act(rl[r], pf[m], AF.Copy, [b_pf[m]], [b_rl[r]])
                    for tt_ in range(4):
                        self.tr(ptrF[:, tt_ * 128:(tt_ + 1) * 128], rl[r][:, tt_ * 128:(tt_ + 1) * 128], identf,
                                [b_rl[r], b_idf], pwrites=[b_ptrF])
                    self.tt("dve", x23[:, :, cu * 128:(cu + 1) * 128], ptrF.rearrange("p (t c) -> p t c", c=128),
                            x23[:, :, cu * 128:(cu + 1) * 128], ALU.add, [b_ptrF] + b_x2, pwrites=b_x2)
            for tt_ in range(4):
                tg = tb * 4 + tt_
                self.act(junk, x23[:, tt_, :], AF.Square, [b_x2[tt_]], [b_junk, b_st[tt_]], accum_out=st[:, 8 + tt_:9 + tt_])
                self.rsqrt_col(st[:, 12 + tt_:13 + tt_], st[:, 8 + tt_:9 + tt_], 1.0 / D, self.eps6, [b_st[tt_], self.b_eps], b_st[tt_])
                self.stt("dve", x23[:, tt_, :], x23[:, tt_, :], st[:, 12 + tt_:13 + tt_], gF, ALU.mult, ALU.mult,
                         [b_x2[tt_], b_st[tt_], b_gF], [b_x2[tt_]])
                P.dma("pool", ds_yo[tt_], self.y[tg * 128:(tg + 1) * 128, :], x23[:, tt_, :], reads=[b_x2[tt_]])


_CACHE = {}


def _build(debug=False, upto=99):
    key = (debug, upto)
    if key not in _CACHE:
        nc = bass.Bass("TRN2", target_bir_lowering=False)
        k = Kern(nc, debug=debug, upto=upto)
        k.build()
        _CACHE[key] = (nc, k)
    return _CACHE[key]


def _in_maps(inp):
    f = np.float32
    x = np.asarray(inp["x"], f)
    shared = {
        "w_in": np.ascontiguousarray(np.asarray(inp["w_in"], f)[0]),
        "w_bm": np.ascontiguousarray(np.asarray(inp["w_branch_m"], f)[0]),
        "w_bd": np.ascontiguousarray(np.asarray(inp["w_branch_d"], f)[0]),
        "w_out": np.ascontiguousarray(np.asarray(inp["w_out"], f)[0]),
        "w_ff1": np.ascontiguousarray(np.asarray(inp["w_ff1"], f)[0]),
        "w_ff2": np.ascontiguousarray(np.asarray(inp["w_ff2"], f)[0]),
        "prm": _params(inp),
        "cst": _consts(),
        "cstb": np.eye(128).astype(ml_dtypes.bfloat16),
    }
    maps = []
    for b in range(8):
        m = dict(shared)
        m["x"] = np.ascontiguousarray(x[b])
        maps.append(m)
    return maps


def kernel(**inputs):
    nc, _ = _build()
    res = run_bass_kernel_spmd(nc, _in_maps(inputs), core_ids=list(range(8)))
    return np.stack([np.asarray(r["y"], np.float32) for r in res.results], axis=0)
```

```python
import contextlib
import numpy as np
import ml_dtypes
import concourse.bass as bass
import concourse.mybir as mybir
from concourse.bass_utils import run_bass_kernel_spmd

F32 = mybir.dt.float32
BF16 = mybir.dt.bfloat16
AF = mybir.ActivationFunctionType
ALU = mybir.AluOpType

CENG = ("pe", "act", "dve", "pool")
ALLENG = ("pe", "act", "dve", "pool", "sp")

S = 2048
D = 2048
NT = 16
DIN = 11312
DFF = 8192
BIG = 30000.0


class Buf:
    __slots__ = ("writers", "readers", "war")

    def __init__(self):
        self.writers = []
        self.readers = []
        self.war = []


class DSem:
    __slots__ = ("sem", "count")

    def __init__(self, sem):
        self.sem = sem
        self.count = 0


class Prog:
    def __init__(self):
        self.streams = {e: [] for e in ALLENG}
        self.dsems = []

    def _deps(self, eng, reads, writes, pwrites, is_dma=False):
        deps = []
        if is_dma:
            eng = None
        for b in reads:
            deps.extend(b.writers)
        for b in writes:
            for ev in b.writers + b.readers:
                if not (ev[0] == "c" and ev[1] == eng):
                    deps.append(ev)
        for b in pwrites:
            for ev in (b.readers if b.readers else b.war):
                if not (ev[0] == "c" and ev[1] == eng):
                    deps.append(ev)
        return deps

    @staticmethod
    def _reduce(evs):
        best = {}
        for ev in evs:
            key = (ev[0], ev[1] if ev[0] == "c" else id(ev[1]))
            if key not in best or best[key][2] < ev[2]:
                best[key] = ev
        return list(best.values())

    def _commit(self, ev, reads, writes, pwrites):
        for b in reads:
            b.readers.append(ev)
            if len(b.readers) > 8:
                b.readers = self._reduce(b.readers)
        for b in writes:
            b.writers = [ev]
            b.readers = []
            b.war = []
        for b in pwrites:
            if b.readers:
                b.war = [r for r in b.readers if r != ev]
                b.writers = [ev]
                b.readers = []
            else:
                b.writers.append(ev)
                b.writers = self._reduce(b.writers)

    def op(self, eng, fn, reads=(), writes=(), pwrites=()):
        deps = self._reduce(self._deps(eng, reads, writes, pwrites))
        st = self.streams[eng]
        idx = len(st)
        st.append({"fn": fn, "deps": deps, "sig": False, "dma": None})
        ev = ("c", eng, idx)
        self._commit(ev, reads, writes, pwrites)
        return ev

    def dma(self, queue, dsem, out_ap, in_ap, reads=(), writes=(), pwrites=()):
        deps = self._reduce(self._deps(queue, reads, writes, pwrites, True))
        st = self.streams[queue]
        dsem.count += 16
        ev = ("d", dsem, dsem.count)
        st.append({"fn": (lambda e, o=out_ap, i=in_ap: e.dma_start(out=o, in_=i)),
                   "deps": deps, "sig": False, "dma": dsem})
        self._commit(ev, reads, writes, pwrites)
        return ev

    def barrier(self):
        evs = []
        for e in CENG:
            st = self.streams[e]
            for i in range(len(st) - 1, -1, -1):
                if st[i]["dma"] is None and st[i]["fn"] is not None:
                    evs.append(("c", e, i))
                    break
        for d in self.dsems:
            if d.count:
                evs.append(("d", d, d.count))
        for e in ALLENG:
            self.streams[e].append({"fn": None, "deps": list(evs), "sig": False, "dma": None})

    def emit(self, nc, engsems):
        for e in ALLENG:
            for o in self.streams[e]:
                for ev in o["deps"]:
                    if ev[0] == "c":
                        self.streams[ev[1]][ev[2]]["sig"] = True
        sigidx = {}
        for e in ALLENG:
            c = 0
            arr = []
            for o in self.streams[e]:
                if o["sig"]:
                    c += 1
                arr.append(c)
            sigidx[e] = arr
        self.sigcounts = {e: (sigidx[e][-1] if sigidx[e] else 0) for e in ALLENG}
        streams = self.streams

        def replay(e):
            def run(eng):
                seen = {}
                for o in streams[e]:
                    need = {}
                    for ev in o["deps"]:
                        if ev[0] == "c":
                            key = ("c", ev[1])
                            val = sigidx[ev[1]][ev[2]]
                        else:
                            key = ("d", id(ev[1]))
                            val = ev[2]
                        if seen.get(key, 0) >= val:
                            continue
                        if need.get(key, (0, None))[0] < val:
                            need[key] = (val, ev)
                    for key, (val, ev) in need.items():
                        seen[key] = val
                        sem = engsems[ev[1]] if ev[0] == "c" else ev[1].sem
                        eng.wait_ge(sem, val)
                    if o["fn"] is None:
                        continue
                    ins = o["fn"](eng)
                    if o["dma"] is not None:
                        ins.then_inc(o["dma"].sem, 16)
                    elif o["sig"]:
                        ins.then_inc(engsems[e], 1)
            return run

        with nc.Block() as block:
            block.sync(replay("sp"))
            block.tensor(replay("pe"))
            block.scalar(replay("act"))
            block.vector(replay("dve"))
            block.gpsimd(replay("pool"))


class Arena:
    def __init__(self, ap, nwords):
        self.ap = ap
        self.n = nwords
        self.off = 0

    def reset(self):
        self.off = 0

    def f32(self, n):
        a = self.ap[:, self.off:self.off + n]
        self.off += n
        assert self.off <= self.n, ("arena overflow", self.off)
        return a

    def bf16(self, n):
        w = (n + 1) // 2
        a = self.ap[:, self.off:self.off + w].bitcast(BF16)
        self.off += w
        assert self.off <= self.n, ("arena overflow", self.off)
        return a


C_IDENT, C_ONES, C_TRIF, C_TRIB, C_BLK, C_HA, C_HB, C_MNF, C_MNB, C_MPF, C_MPB = [i * 128 for i in range(11)]
C_SEL8 = 11 * 128
C_SEL16 = C_SEL8 + 1024
NCST = C_SEL16 + 2048

P_G1, P_G2, P_IB, P_FB, P_AL, P_DT = 0, 16, 32, 40, 48, 64
P_CW = 80
P_NGM = 200
P_NGD = P_NGM + 1024
P_GF = P_NGD + 128
NPRM = P_GF + 2048


def _consts():
    p = np.arange(128)[:, None]
    j = np.arange(128)[None, :]
    same = (p // 64) == (j // 64)
    c = np.zeros((128, NCST), np.float32)
    c[:, C_IDENT:C_IDENT + 128] = np.eye(128)
    c[:, C_ONES:C_ONES + 128] = 1.0
    c[:, C_TRIF:C_TRIF + 128] = same & (p <= j)
    c[:, C_TRIB:C_TRIB + 128] = same & (p >= j)
    c[:, C_BLK:C_BLK + 128] = same
    c[:, C_HA:C_HA + 128] = (p < 64) & (j >= 0)
    c[:, C_HB:C_HB + 128] = (p >= 64) & (j >= 0)
    c[:, C_MNF:C_MNF + 128] = np.where(same & (p <= j), 0.0, -BIG)
    c[:, C_MNB:C_MNB + 128] = np.where(same & (p >= j), 0.0, -BIG)
    c[:, C_MPF:C_MPF + 128] = np.where(same & (j < p), 0.0, BIG)
    c[:, C_MPB:C_MPB + 128] = np.where(same & (j > p), 0.0, BIG)
    for k in range(8):
        c[k, C_SEL8 + k * 128:C_SEL8 + (k + 1) * 128] = 1.0
    for k in range(16):
        c[k, C_SEL16 + k * 128:C_SEL16 + (k + 1) * 128] = 1.0
    return c


def _params(inp):
    f = np.float32
    prm = np.zeros((128, NPRM), f)
    prm[:, P_G1:P_G1 + 16] = np.asarray(inp["norm1_g"], f).reshape(16, 128).T
    prm[:, P_G2:P_G2 + 16] = np.asarray(inp["norm2_g"], f).reshape(16, 128).T
    prm[:, P_IB:P_IB + 8] = np.asarray(inp["mlstm_i_bias"], f).reshape(1, 8)
    prm[:, P_FB:P_FB + 8] = np.asarray(inp["mlstm_f_bias"], f).reshape(1, 8)
    prm[:, P_AL:P_AL + 16] = np.asarray(inp["delta_a_log"], f).reshape(1, 16)
    prm[:, P_DT:P_DT + 16] = np.asarray(inp["delta_dt_bias"], f).reshape(1, 16)
    cw = np.asarray(inp["delta_conv_w"], f).reshape(5, 24, 128)
    prm[:, P_CW:P_CW + 120] = cw.transpose(2, 1, 0).reshape(128, 120)
    prm[:, P_NGM:P_NGM + 1024] = np.asarray(inp["mlstm_norm_g"], f).reshape(1, 1024)
    prm[:, P_NGD:P_NGD + 128] = np.asarray(inp["delta_norm_g"], f).reshape(1, 128)
    prm[:, P_GF:P_GF + 2048] = np.asarray(inp["norm_f_g"], f).reshape(1, 2048)
    return prm


class Kern:
    def __init__(self, nc, debug=False, upto=99):
        self.nc = nc
        self.debug = debug
        self.upto = upto
        self.P = Prog()
        self.dbg_names = []

    def ds(self, sw=False):
        if sw:
            d = self.dspool_sw[self.dsnext_sw]
            self.dsnext_sw += 1
        else:
            d = self.dspool[self.dsnext]
            self.dsnext += 1
        return d

    def scratch(self, name, shape, dt):
        kind = "ExternalOutput" if self.debug else "Internal"
        t = self.nc.dram_tensor(name, list(shape), dt, kind=kind).ap()
        if self.debug:
            self.dbg_names.append(name)
        return t

    def act(self, out, in_, func, reads, writes=(), pwrites=(), **kw):
        return self.P.op("act", lambda e: e.activation(out=out, in_=in_, func=func, **kw), reads, writes, pwrites)

    def tt(self, eng, out, in0, in1, op, reads, writes=(), pwrites=()):
        return self.P.op(eng, lambda e: e.tensor_tensor(out=out, in0=in0, in1=in1, op=op), reads, writes, pwrites)

    def stt(self, eng, out, in0, scalar, in1, op0, op1, reads, writes=(), pwrites=()):
        return self.P.op(eng, lambda e: e.scalar_tensor_tensor(out=out, in0=in0, scalar=scalar, in1=in1, op0=op0, op1=op1),
                         reads, writes, pwrites)

    def cp(self, eng, out, in_, reads, writes=(), pwrites=()):
        return self.P.op(eng, lambda e: e.tensor_copy(out=out, in_=in_), reads, writes, pwrites)

    def mm(self, out, lhsT, rhs, start, stop, reads, writes=(), pwrites=()):
        return self.P.op("pe", lambda e: e.matmul(out, lhsT=lhsT, rhs=rhs, start=start, stop=stop), reads, writes, pwrites)

    def tr(self, out, in_, ident, reads, writes=(), pwrites=()):
        return self.P.op("pe", lambda e: e.transpose(out=out, in_=in_, identity=ident), reads, writes, pwrites)

    def memset(self, eng, ap, val, writes=(), pwrites=()):
        return self.P.op(eng, lambda e: e.memset(ap, val), (), writes, pwrites)

    def load(self, dst, src, buf, dsem=None, queue="sp"):
        if dsem is None:
            dsem = self.ds()
        return self.P.dma(queue, dsem, dst, src, writes=[buf])

    def rsqrt_col(self, out, in_, scale, eps_ap, reads, buf):
        self.act(out, in_, AF.Sqrt, reads, writes=[buf], bias=eps_ap, scale=scale)
        self.P.op("dve", lambda e: e.reciprocal(out=out, in_=out), [buf], [buf])

    def wring_init(self, nbuf=3):
        A = self.A
        self.w_st = [A.f32(2048) for _ in range(nbuf)]
        self.w_bf = [A.bf16(2048) for _ in range(nbuf)]
        self.w_bst = [Buf() for _ in range(nbuf)]
        self.w_bbf = [Buf() for _ in range(nbuf)]
        self.w_ds = [self.ds() for _ in range(nbuf)]
        self.w_i = 0
        self.w_n = nbuf

    def wload(self, src, kc, n, fold=None, eng="dve"):
        i = self.w_i % self.w_n
        self.w_i += 1
        st = self.w_st[i][:, 0:kc * n].rearrange("p (c n) -> p c n", n=n)
        bf = self.w_bf[i][:, 0:kc * n].rearrange("p (c n) -> p c n", n=n)
        self.P.dma("sp", self.w_ds[i], st, src, writes=[self.w_bst[i]])
        if fold is None:
            self.cp(eng, bf, st, [self.w_bst[i]], [self.w_bbf[i]])
        else:
            self.tt(eng, bf, st, fold.unsqueeze(2).to_broadcast([128, kc, n]), ALU.mult, [self.w_bst[i]] + self.b_prm, [self.w_bbf[i]])
        return bf, self.w_bbf[i]

    def build(self):
        nc = self.nc
        with contextlib.ExitStack() as es:
            self.es = es
            self.io()
            NW = 48640
            arena = es.enter_context(nc.sbuf_tensor("arena", [128, NW], F32))
            self.A = Arena(arena, NW)
            self.psum = es.enter_context(nc.psum_tensor("psum", [128, 4096], F32))
            self.engsems = {e: es.enter_context(nc.semaphore("s_" + e)) for e in CENG}
            self.dspool = [DSem(es.enter_context(nc.semaphore("d%d" % i))) for i in range(15)]
            self.dspool_sw = [DSem(es.enter_context(nc.semaphore("w%d" % i))) for i in range(5)]
            self.dsnext = 0
            self.dsnext_sw = 0
            self.P.dsems = self.dspool + self.dspool_sw
            phases = [self.phase1_2, self.phase2b, self.phase3, self.phase4, self.phase5, self.phase67]
            for i, ph in enumerate(phases):
                if i >= self.upto:
                    break
                self.dsnext = 0
                self.dsnext_sw = 0
                ph()
                self.P.barrier()
            self.P.emit(nc, self.engsems)

    def io(self):
        nc = self.nc
        I = lambda n, s, dt=F32: nc.dram_tensor(n, list(s), dt, kind="ExternalInput").ap()
        self.x = I("x", [S, D])
        self.w_in = I("w_in", [D, DIN])
        self.w_bm = I("w_bm", [1024, D])
        self.w_bd = I("w_bd", [1024, D])
        self.w_out = I("w_out", [D, D])
        self.w_ff1 = I("w_ff1", [D, DFF])
        self.w_ff2 = I("w_ff2", [DFF, D])
        self.prm_d = I("prm", [128, NPRM])
        self.cst_d = I("cst", [128, NCST])
        self.cstb_d = I("cstb", [128, 128], BF16)
        self.y = nc.dram_tensor("y", [S, D], F32, kind="ExternalOutput").ap()
        sc = self.scratch
        self.S_qTm = sc("S_qTm", [4, 128, S], BF16)
        self.S_kTm = sc("S_kTm", [4, 128, S], BF16)
        self.S_km = sc("S_km", [4, 128, NT, 128], BF16)
        self.S_vm = sc("S_vm", [4, 128, NT, 256], BF16)
        self.S_som = sc("S_som", [4, 128, NT, 256], F32)
        self.S_gm = sc("S_gm", [128, NT, 16], F32)
        self.S_qkv = sc("S_qkv", [24, 128, S], F32)
        self.S_sz = sc("S_sz", [8, 128, NT, 128], F32)
        self.S_gd = sc("S_gd", [128, NT, 32], F32)
        self.S_sg = sc("S_sg", [32, 128, S], F32)
        self.S_qTd = sc("S_qTd", [8, 128, S], BF16)
        self.S_kTd = sc("S_kTd", [8, 128, S], BF16)
        self.S_kd = sc("S_kd", [8, 128, NT, 128], BF16)
        self.S_vd = sc("S_vd", [8, 128, NT, 128], BF16)
        self.S_ymT = sc("S_ymT", [8, 128, S], BF16)
        self.S_ydT = sc("S_ydT", [8, 128, S], BF16)
        self.S_mixT = sc("S_mixT", [16, 128, S], BF16)

    def common(self, need_prm=True):
        A = self.A
        A.reset()
        self.identb = A.bf16(128)
        self.b_identb = Buf()
        self.load(self.identb, self.cstb_d, self.b_identb)
        self.eps = A.f32(2)
        self.b_eps = Buf()
        self.memset("pool", self.eps[:, 0:1], 1e-6, pwrites=[self.b_eps])
        self.memset("pool", self.eps[:, 1:2], 1.0, pwrites=[self.b_eps])
        self.eps6 = self.eps[:, 0:1]
        self.one1 = self.eps[:, 1:2]

    def load_prm(self, off, n):
        t = self.A.f32(n)
        b = Buf()
        self.load(t, self.prm_d[:, off:off + n], b)
        return t, b

    def load_cst(self, off, n, rows=128):
        t = self.A.f32(n)
        b = Buf()
        self.load(t[0:rows, :], self.cst_d[0:rows, off:off + n], b)
        return t, b

    def phase1_2(self):
        P, A, ps = self.P, self.A, self.psum
        self.common()
        g1, b_g1 = self.load_prm(P_G1, 16)
        self.b_prm = [b_g1]
        xT = A.bf16(16 * S)
        xT3 = xT.rearrange("p (c t) -> p c t", t=S)
        b_xT = Buf()
        mark = A.off
        xin = [A.f32(D) for _ in range(2)]
        b_xin = [Buf(), Buf()]
        ds_x = [self.ds(), self.ds()]
        junk = A.f32(D)
        b_junk = Buf()
        xn = [A.bf16(D) for _ in range(2)]
        b_xn = [Buf(), Buf()]
        ssq = A.f32(16)
        rstd = A.f32(16)
        b_st = [Buf() for _ in range(16)]
        pT = [ps[:, 0:1024].bitcast(BF16), ps[:, 1024:2048].bitcast(BF16)]
        b_pT = [Buf(), Buf()]
        for t in range(NT):
            i = t % 2
            P.dma("sp", ds_x[i], xin[i], self.x[t * 128:(t + 1) * 128, :], writes=[b_xin[i]])
            self.act(junk, xin[i], AF.Square, [b_xin[i]], writes=[b_junk, b_st[t]], accum_out=ssq[:, t:t + 1])
            self.rsqrt_col(rstd[:, t:t + 1], ssq[:, t:t + 1], 1.0 / D, self.eps6, [b_st[t], self.b_eps], b_st[t])
            self.act(xn[i], xin[i], AF.Copy, [b_xin[i], b_st[t]], writes=[b_xn[i]], scale=rstd[:, t:t + 1])
            for dc in range(16):
                self.tr(pT[i][:, dc * 128:(dc + 1) * 128], xn[i][:, dc * 128:(dc + 1) * 128], self.identb,
                        [b_xn[i], self.b_identb], pwrites=[b_pT[i]])
            self.cp("dve", xT3[:, :, t * 128:(t + 1) * 128], pT[i].rearrange("p (c t) -> p c t", t=128),
                    [b_pT[i]], pwrites=[b_xT])
        P.barrier()
        A.off = mark
        self.wring_init(3)
        NS = 4
        stage = [A.f32(1024) for _ in range(NS)]
        b_stage = [Buf() for _ in range(NS)]
        ds_stage = [self.ds(True) for _ in range(NS)]
        preg = [ps[:, i * 1024:(i + 1) * 1024] for i in range(4)]
        b_preg = [Buf() for _ in range(4)]
        units = []
        qs = 128 ** -0.5
        for h in range(4):
            units.append(("fm", h * 128, 128, AF.Copy, 1.0, BF16, lambda hf, h=h: self.S_qTm[h][:, hf * 1024:(hf + 1) * 1024]))
        for h in range(4):
            units.append(("fm", 512 + h * 128, 128, AF.Copy, qs, BF16, lambda hf, h=h: self.S_kTm[h][:, hf * 1024:(hf + 1) * 1024]))
        for h in range(4):
            units.append(("tm", 512 + h * 128, 128, AF.Copy, qs, BF16, lambda hf, h=h: self.S_km[h][:, hf * 8:(hf + 1) * 8, :]))
        for u in range(8):
            units.append(("tm", 1024 + u * 128, 128, AF.Copy, 1.0, BF16,
                          lambda hf, u=u: self.S_vm[u // 2][:, hf * 8:(hf + 1) * 8, (u % 2) * 128:(u % 2) * 128 + 128]))
        for u in range(8):
            units.append(("tm", 2048 + u * 128, 128, AF.Sigmoid, 1.0, F32,
                          lambda hf, u=u: self.S_som[u // 2][:, hf * 8:(hf + 1) * 8, (u % 2) * 128:(u % 2) * 128 + 128]))
        units.append(("tm", 3072, 16, AF.Copy, 1.0, F32, lambda hf: self.S_gm[:, hf * 8:(hf + 1) * 8, :]))
        for u in range(24):
            units.append(("fm", 3088 + u * 128, 128, AF.Copy, 1.0, F32, lambda hf, u=u: self.S_qkv[u][:, hf * 1024:(hf + 1) * 1024]))
        for h in range(8):
            units.append(("tm", 6160 + h * 128, 128, AF.Silu, 1.0, F32, lambda hf, h=h: self.S_sz[h][:, hf * 8:(hf + 1) * 8, :]))
        units.append(("tm", 7184, 32, AF.Copy, 1.0, F32, lambda hf: self.S_gd[:, hf * 8:(hf + 1) * 8, :]))
        for u in range(32):
            units.append(("fm", 7216 + u * 128, 128, AF.Sigmoid, 1.0, F32, lambda hf, u=u: self.S_sg[u][:, hf * 1024:(hf + 1) * 1024]))
        w3 = self.w_in.rearrange("(c p) n -> p c n", p=128)
        ri = 0
        for (kind, c0, n, func, scale, odt, dst) in units:
            wbf, b_w = self.wload(w3[:, :, c0:c0 + n], 16, n, fold=g1)
            for hf in range(2):
                r = ri % 4
                s_ = ri % NS
                ri += 1
                pr = preg[r]
                if kind == "fm":
                    for tb in range(2):
                        t0 = hf * 1024 + tb * 512
                        for dc in range(16):
                            self.mm(pr[0:n, tb * 512:(tb + 1) * 512], wbf[:, dc, :], xT3[:, dc, t0:t0 + 512],
                                    dc == 0, dc == 15, [b_w, b_xT], pwrites=[b_preg[r]])
                    width = 1024
                else:
                    for t in range(8):
                        tg = hf * 8 + t
                        for dc in range(16):
                            self.mm(pr[:, t * n:(t + 1) * n], xT3[:, dc, tg * 128:(tg + 1) * 128], wbf[:, dc, :],
                                    dc == 0, dc == 15, [b_w, b_xT], pwrites=[b_preg[r]])
                    width = 8 * n
                if odt == BF16:
                    sv = stage[s_].bitcast(BF16)[:, 0:width]
                else:
                    sv = stage[s_][:, 0:width]
                self.act(sv, pr[:, 0:width], func, [b_preg[r]], writes=[b_stage[s_]], scale=scale)
                if kind == "fm":
                    P.dma("pool", ds_stage[s_], dst(hf), sv, reads=[b_stage[s_]])
                else:
                    P.dma("pool", ds_stage[s_], dst(hf), sv.rearrange("p (t n) -> p t n", n=n), reads=[b_stage[s_]])

    def phase2b(self):
        P, A, ps = self.P, self.A, self.psum
        self.common()
        cw, b_cw = self.load_prm(P_CW, 120)
        ones, b_ones = self.load_cst(C_ONES, 128)
        NB = 2
        xp = [A.f32(S + 4) for _ in range(NB)]
        b_xp = [Buf() for _ in range(NB)]
        ds_xp = [self.ds() for _ in range(NB)]
        for i in range(NB):
            self.memset("pool", xp[i][:, 0:2], 0.0, pwrites=[b_xp[i]])
            self.memset("pool", xp[i][:, S + 2:S + 4], 0.0, pwrites=[b_xp[i]])
        P.barrier()
        acc = [A.f32(S) for _ in range(NB)]
        b_acc = [Buf() for _ in range(NB)]
        sl = [A.f32(S) for _ in range(NB)]
        b_sl = [Buf() for _ in range(NB)]
        sq = [A.f32(S) for _ in range(NB)]
        b_sq = [Buf() for _ in range(NB)]
        rn = [A.f32(S) for _ in range(NB)]
        b_rn = [Buf() for _ in range(NB)]
        fb = [A.bf16(S) for _ in range(NB)]
        b_fb = [Buf() for _ in range(NB)]
        ds_fb = [self.ds(True) for _ in range(NB)]
        tk = [A.bf16(S) for _ in range(NB)]
        b_tk = [Buf() for _ in range(NB)]
        ds_tk = [self.ds(True) for _ in range(NB)]
        pss = ps[:, 0:2048]
        b_pss = Buf()
        ptr = ps[:, 2048:3072].bitcast(BF16)
        b_ptr = Buf()
        for u in range(24):
            i = u % NB
            h = u % 8
            P.dma("sp", ds_xp[i], xp[i][:, 2:S + 2], self.S_qkv[u], pwrites=[b_xp[i]])
            P.op("dve", lambda e, i=i, u=u: e.tensor_scalar_mul(out=acc[i], in0=xp[i][:, 0:S], scalar1=cw[:, u * 5:u * 5 + 1]),
                 [b_xp[i], b_cw], [b_acc[i]])
            for k in range(1, 5):
                self.stt("dve", acc[i], xp[i][:, k:k + S], cw[:, u * 5 + k:u * 5 + k + 1], acc[i], ALU.mult, ALU.add,
                         [b_xp[i], b_cw, b_acc[i]], [b_acc[i]])
            if u < 16:
                self.act(sl[i], acc[i], AF.Silu, [b_acc[i]], [b_sl[i]])
                self.tt("pool", sq[i], sl[i], sl[i], ALU.mult, [b_sl[i]], [b_sq[i]])
                for tb in range(4):
                    self.mm(pss[:, tb * 512:(tb + 1) * 512], ones, sq[i][:, tb * 512:(tb + 1) * 512], True, True,
                            [b_ones, b_sq[i]], pwrites=[b_pss])
                self.act(rn[i], pss, AF.Sqrt, [b_pss, self.b_eps], [b_rn[i]], bias=self.eps6, scale=1.0)
                P.op("dve", lambda e, i=i: e.reciprocal(out=rn[i], in_=rn[i]), [b_rn[i]], [b_rn[i]])
                sc_ = (128 ** -0.5) if u < 8 else 1.0
                self.stt("dve", fb[i], sl[i], sc_, rn[i], ALU.mult, ALU.mult, [b_sl[i], b_rn[i]], [b_fb[i]])
                dstT = self.S_qTd[h] if u < 8 else self.S_kTd[h]
                P.dma("pool", ds_fb[i], dstT, fb[i], reads=[b_fb[i]])
            else:
                self.act(fb[i], acc[i], AF.Silu, [b_acc[i]], [b_fb[i]])
            if u >= 8:
                for t in range(NT):
                    self.tr(ptr[:, t * 128:(t + 1) * 128], fb[i][:, t * 128:(t + 1) * 128], self.identb,
                            [b_fb[i], self.b_identb], pwrites=[b_ptr])
                self.cp("dve", tk[i], ptr, [b_ptr], [b_tk[i]])
                dst = self.S_kd[h] if u < 16 else self.S_vd[h]
                P.dma("pool", ds_tk[i], dst, tk[i].rearrange("p (t c) -> p t c", c=128), reads=[b_tk[i]])

    def gate_prep(self, lg, b_lg, ncol, cstt, b_cst, identf, b_identf):
        P, A, ps = self.P, self.A, self.psum
        half = ncol // 2
        n = NT * ncol
        lg3 = lg.rearrange("p (t c) -> p t c", c=ncol)
        cum = A.f32(n)
        tot = A.f32(n)
        decA = A.f32(n)
        decB = A.f32(n)
        cumT = A.f32(S)
        b = {k: Buf() for k in ("cum", "tot", "decA", "decB", "cumT", "p0", "p1", "p2", "p3", "pT")}
        p0 = ps[:, 0:n].rearrange("p (t c) -> p t c", c=ncol)
        cum3 = cum.rearrange("p (t c) -> p t c", c=ncol)
        c_ = lambda off: cstt[:, off:off + 128]
        p0b = ps[:, 256:256 + n].rearrange("p (t c) -> p t c", c=ncol)
        self.mm(ps[:, 0:n], c_(C_TRIF), lg, True, True, [b_cst, b_lg], pwrites=[b["p0"]])
        self.mm(ps[:, 256:256 + n], c_(C_TRIB), lg, True, True, [b_cst, b_lg], pwrites=[b["p0"]])
        self.cp("dve", cum3[:, :, 0:half], p0[:, :, 0:half], [b["p0"]], pwrites=[b["cum"]])
        self.cp("dve", cum3[:, :, half:ncol], p0b[:, :, half:ncol], [b["p0"]], pwrites=[b["cum"]])
        self.mm(ps[:, 512:512 + n], c_(C_BLK), lg, True, True, [b_cst, b_lg], writes=[b["p1"]])
        self.cp("dve", tot, ps[:, 512:512 + n], [b["p1"]], [b["tot"]])
        self.mm(ps[:, 1024:1024 + n], c_(C_HA), lg, True, True, [b_cst, b_lg], writes=[b["p2"]])
        self.act(decA, ps[:, 1024:1024 + n], AF.Exp, [b["p2"]], [b["decA"]])
        self.mm(ps[:, 1536:1536 + n], c_(C_HB), lg, True, True, [b_cst, b_lg], writes=[b["p3"]])
        self.act(decB, ps[:, 1536:1536 + n], AF.Exp, [b["p3"]], [b["decB"]])
        pT = ps[:, 2048:4096]
        for t in range(NT):
            self.tr(pT[0:ncol, t * 128:(t + 1) * 128], cum3[:, t, :], identf, [b["cum"], b_identf], pwrites=[b["pT"]])
        self.cp("dve", cumT[0:ncol, :], pT[0:ncol, :], [b["pT"]], [b["cumT"]])
        return dict(cum=cum, cum3=cum3, tot=tot, decA=decA, decB=decB, cumT=cumT, b=b)

    def phase3(self):
        P, A, ps = self.P, self.A, self.psum
        self.common()
        cstt, b_cst = self.load_cst(0, C_SEL8)
        sel, b_sel = self.load_cst(C_SEL8, 1024, rows=8)
        identf = cstt[:, C_IDENT:C_IDENT + 128]
        ibfb, b_ibfb = self.load_prm(P_IB, 16)
        ngm, b_ngm = self.load_prm(P_NGM, 1024)
        gm = A.f32(NT * 16)
        b_gm = Buf()
        self.load(gm, self.S_gm.rearrange("p t c -> p (t c)"), b_gm)
        gm3 = gm.rearrange("p (t c) -> p t c", c=16)
        n8 = NT * 8
        ig = A.f32(n8)
        lf = A.f32(n8)
        t1 = A.f32(n8)
        t2 = A.f32(n8)
        wst = A.f32(n8)
        b_ig, b_lf, b_t1, b_t2, b_wst = Buf(), Buf(), Buf(), Buf(), Buf()
        v3 = lambda a: a.rearrange("p (t c) -> p t c", c=8)
        bc8 = lambda a: a.unsqueeze(1).to_broadcast([128, NT, 8])
        self.tt("dve", v3(ig), gm3[:, :, 0:8], bc8(ibfb[:, 0:8]), ALU.add, [b_gm, b_ibfb], [b_ig])
        self.tt("dve", v3(t1), gm3[:, :, 8:16], bc8(ibfb[:, 8:16]), ALU.add, [b_gm, b_ibfb], [b_t1])
        self.act(t2, t1, AF.Abs, [b_t1], [b_t2])
        self.act(t2, t2, AF.Exp, [b_t2], [b_t2], scale=-1.0)
        self.act(t2, t2, AF.Ln, [b_t2, self.b_eps], [b_t2], bias=self.one1, scale=1.0)
        P.op("dve", lambda e: e.tensor_scalar_min(out=t1, in0=t1, scalar1=0.0), [b_t1], [b_t1])
        self.tt("dve", lf, t1, t2, ALU.subtract, [b_t1, b_t2], [b_lf])
        G = self.gate_prep(lf, b_lf, 8, cstt, b_cst, identf, b_cst)
        gb = G["b"]
        self.tt("dve", t1, G["tot"], G["cum"], ALU.subtract, [gb["tot"], gb["cum"]], [b_t1])
        self.tt("dve", t1, t1, ig, ALU.add, [b_t1, b_ig], [b_t1])
        self.act(wst, t1, AF.Exp, [b_t1], [b_wst])
        wst3, ig3 = v3(wst), v3(ig)
        decA3, decB3 = v3(G["decA"]), v3(G["decB"])
        cum3 = G["cum3"]
        P.barrier()
        qT = A.bf16(S); kT = A.bf16(S); ktok = A.bf16(S); vext = A.bf16(NT * 264)
        so = A.f32(NT * 256); hsum = A.f32(NT * 256); ymT = A.bf16(2 * S)
        E = A.f32(S); qg = A.bf16(S); WT = A.f32(S); Dm = A.f32(S); kw = A.bf16(S)
        Cf = A.f32(264)
        Cb = [A.bf16(264) for _ in range(3)]
        PT = [A.bf16(128) for _ in range(2)]
        den = A.f32(4)
        ytmp = [A.f32(256) for _ in range(2)]
        ybf = [A.bf16(256) for _ in range(2)]
        fin = A.f32(8)
        vext3 = vext.rearrange("p (t c) -> p t c", c=264)
        so3 = so.rearrange("p (t c) -> p t c", c=256)
        hs3 = hsum.rearrange("p (t c) -> p t c", c=256)
        ktok3 = ktok.rearrange("p (t c) -> p t c", c=128)
        kw3 = kw.rearrange("p (t c) -> p t c", c=128)
        WT3 = WT.rearrange("p (t c) -> p t c", c=128)
        Dm3 = Dm.rearrange("p (t c) -> p t c", c=128)
        ymT3 = ymT.rearrange("p (h t) -> p h t", t=S)
        B = {k: Buf() for k in ("qT", "kT", "ktok", "vext", "so", "ymT", "E", "qg", "WT", "Dm", "kw", "Cf", "bc",
                                "ones")}
        b_hs = [Buf() for _ in range(NT)]
        b_Cb = [Buf() for _ in range(3)]
        b_PT = [Buf(), Buf()]
        b_den = [Buf(), Buf()]
        b_yt = [Buf(), Buf()]
        b_yb = [Buf(), Buf()]
        b_fin = [Buf(), Buf()]
        dsl = {k: self.ds(k == "ymT") for k in ("qT", "kT", "ktok", "vext", "so", "ymT")}
        pbc = ps[:, 0:2048]
        pST = [ps[:, 2048:2176], ps[:, 2176:2304]]
        b_pST = [Buf(), Buf()]
        ptr = ps[:, 2304:2432].bitcast(BF16)
        b_ptr = Buf()
        pnum = [ps[:, 2560:2560 + 257], ps[:, 3584:3584 + 257]]
        b_pnum = [Buf(), Buf()]
        pC = ps[:, 3072:3072 + 257]
        b_pC = Buf()
        self.memset("pool", vext3[:, :, 256:257], 1.0, pwrites=[B["ones"]])
        P.barrier()
        for h in range(4):
            P.dma("sp", dsl["qT"], qT, self.S_qTm[h], writes=[B["qT"]])
            P.dma("sp", dsl["kT"], kT, self.S_kTm[h], writes=[B["kT"]])
            P.dma("sp", dsl["ktok"], ktok3, self.S_km[h], writes=[B["ktok"]])
            P.dma("sp", dsl["vext"], vext3[:, :, 0:256], self.S_vm[h], writes=[B["vext"]])
            P.dma("sp", dsl["so"], so3, self.S_som[h], writes=[B["so"]])
            for dr in range(2):
                c = dr * 4 + h
                mn = cstt[:, C_MNF:C_MNF + 128] if dr == 0 else cstt[:, C_MNB:C_MNB + 128]
                for tb in range(4):
                    self.mm(pbc[:, tb * 512:(tb + 1) * 512], sel[0:8, c * 128:(c + 1) * 128],
                            G["cumT"][0:8, tb * 512:(tb + 1) * 512], True, True, [b_sel, gb["cumT"]], pwrites=[B["bc"]])
                self.act(E, pbc, AF.Exp, [B["bc"]], [B["E"]])
                self.tt("dve", qg, qT, E, ALU.mult, [B["qT"], B["E"]], [B["qg"]])
                self.tt("dve", Dm3, pbc.rearrange("p (t c) -> p t c", c=128),
                        cum3[:, :, c:c + 1].to_broadcast([128, NT, 128]), ALU.subtract, [B["bc"], gb["cum"]], [B["Dm"]])
                self.tt("dve", Dm3, Dm3, mn.unsqueeze(1).to_broadcast([128, NT, 128]), ALU.min, [B["Dm"], b_cst], [B["Dm"]])
                self.tt("pool", Dm3, Dm3, ig3[:, :, c:c + 1].to_broadcast([128, NT, 128]), ALU.add, [B["Dm"], b_ig], [B["Dm"]])
                self.act(WT, Dm, AF.Exp, [B["Dm"]], [B["WT"]])
                self.tt("dve", kw3, ktok3, wst3[:, :, c:c + 1].to_broadcast([128, NT, 128]), ALU.mult,
                        [B["ktok"], b_wst], [B["kw"]])
                self.memset("pool", Cf, 0.0, writes=[B["Cf"]])
                ci = 0
                self.memset("pool", Cb[0], 0.0, writes=[b_Cb[0]])
                tiles = range(NT) if dr == 0 else range(NT - 1, -1, -1)
                for it, t in enumerate(tiles):
                    j = it % 2
                    tok = slice(t * 128, (t + 1) * 128)
                    self.mm(pST[j], kT[:, tok], qT[:, tok], True, True, [B["kT"], B["qT"]], writes=[b_pST[j]])
                    self.tt("dve", PT[j], pST[j], WT3[:, t, :], ALU.mult, [b_pST[j], B["WT"]], [b_PT[j]])
                    chunks = (0, 1) if dr == 0 else (1, 0)
                    for ck in chunks:
                        rows = slice(ck * 64, ck * 64 + 64)
                        ctok = slice(t * 128 + ck * 64, t * 128 + ck * 64 + 64)
                        self.mm(pnum[j][rows, :], qg[:, ctok], Cb[ci][:, 0:257], True, False,
                                [B["qg"], b_Cb[ci]], pwrites=[b_pnum[j]])
                        self.mm(pC, kw3[rows, t, :], vext3[rows, t, 0:257], True, True,
                                [B["kw"], B["vext"], B["ones"]], writes=[b_pC])
                        dec = (decA3 if ck == 0 else decB3)[:, t, c:c + 1]
                        bdec = gb["decA"] if ck == 0 else gb["decB"]
                        cn = (ci + 1) % 3
                        self.stt("dve", Cb[cn][:, 0:257], Cf[:, 0:257], dec, pC, ALU.mult, ALU.add,
                                 [B["Cf"], bdec, b_pC], [b_Cb[cn]])
                        self.stt("dve", Cf[:, 0:257], Cf[:, 0:257], dec, pC, ALU.mult, ALU.add,
                                 [B["Cf"], bdec, b_pC], [B["Cf"]])
                        ci = cn
                    self.mm(pnum[j], PT[j], vext3[:, t, 0:257], False, True, [b_PT[j], B["vext"], B["ones"]],
                            pwrites=[b_pnum[j]])
                    dn = den[:, 2 * j:2 * j + 1]
                    rc = den[:, 2 * j + 1:2 * j + 2]
                    self.act(dn, pnum[j][:, 256:257], AF.Abs, [b_pnum[j]], [b_den[j]])
                    P.op("dve", lambda e, dn=dn: e.tensor_scalar_max(out=dn, in0=dn, scalar1=1.0), [b_den[j]], [b_den[j]])
                    P.op("dve", lambda e, dn=dn, rc=rc: e.reciprocal(out=rc, in_=dn), [b_den[j]], [b_den[j]])
                    if dr == 0:
                        self.act(hs3[:, t, :], pnum[j][:, 0:256], AF.Copy, [b_pnum[j], b_den[j]], [b_hs[t]], scale=rc)
                    else:
                        self.stt("dve", hs3[:, t, :], pnum[j][:, 0:256], rc, hs3[:, t, :], ALU.mult, ALU.add,
                                 [b_pnum[j], b_den[j], b_hs[t]], [b_hs[t]])
                        fs = fin[:, 2 * j:2 * j + 1]
                        fr = fin[:, 2 * j + 1:2 * j + 2]
                        self.act(ytmp[j], hs3[:, t, :], AF.Square, [b_hs[t]], [b_yt[j], b_fin[j]], accum_out=fs)
                        self.rsqrt_col(fr, fs, 1.0 / 256, self.eps6, [b_fin[j], self.b_eps], b_fin[j])
                        self.stt("dve", ytmp[j], hs3[:, t, :], fr, ngm[:, h * 256:(h + 1) * 256], ALU.mult, ALU.mult,
                                 [b_hs[t], b_fin[j], b_ngm], [b_yt[j]])
                        self.tt("pool", ybf[j], ytmp[j], so3[:, t, :], ALU.mult, [b_yt[j], B["so"]], [b_yb[j]])
                        for hh in range(2):
                            self.tr(ptr[:, hh * 128:(hh + 1) * 128], ybf[j][:, hh * 128:(hh + 1) * 128], self.identb,
                                    [b_yb[j], self.b_identb], pwrites=[b_ptr])
                        self.cp("dve", ymT3[:, :, tok], ptr.rearrange("p (h t) -> p h t", t=128), [b_ptr], pwrites=[B["ymT"]])
            for hh in range(2):
                P.dma("pool", dsl["ymT"], self.S_ymT[h * 2 + hh], ymT3[:, hh, :], reads=[B["ymT"]])

    def phase4(self):
        P, A, ps = self.P, self.A, self.psum
        self.common()
        cstt, b_cst = self.load_cst(0, C_SEL8)
        sel, b_sel = self.load_cst(C_SEL16, 2048, rows=16)
        identf = cstt[:, C_IDENT:C_IDENT + 128]
        adt, b_adt = self.load_prm(P_AL, 32)
        ngd, b_ngd = self.load_prm(P_NGD, 128)
        gd = A.f32(NT * 32)
        b_gd = Buf()
        self.load(gd, self.S_gd.rearrange("p t c -> p (t c)"), b_gd)
        gd3 = gd.rearrange("p (t c) -> p t c", c=32)
        n16 = NT * 16
        t1 = A.f32(n16); t2 = A.f32(n16); g = A.f32(n16); beta = A.f32(n16); nbeta = A.f32(n16)
        bg = A.f32(n16); ekd = A.f32(n16); eal = A.f32(16)
        b_t1, b_t2, b_g, b_beta, b_nb, b_bg, b_ekd, b_eal = [Buf() for _ in range(8)]
        v3 = lambda a: a.rearrange("p (t c) -> p t c", c=16)
        bc16 = lambda a: a.unsqueeze(1).to_broadcast([128, NT, 16])
        self.tt("dve", v3(t1), gd3[:, :, 0:16], bc16(adt[:, 16:32]), ALU.add, [b_gd, b_adt], [b_t1])
        self.act(t2, t1, AF.Abs, [b_t1], [b_t2])
        self.act(t2, t2, AF.Exp, [b_t2], [b_t2], scale=-1.0)
        self.act(t2, t2, AF.Ln, [b_t2, self.b_eps], [b_t2], bias=self.one1, scale=1.0)
        P.op("dve", lambda e: e.tensor_scalar_max(out=t1, in0=t1, scalar1=0.0), [b_t1], [b_t1])
        self.tt("dve", t1, t1, t2, ALU.add, [b_t1, b_t2], [b_t1])
        self.act(eal, adt[:, 0:16], AF.Exp, [b_adt], [b_eal])
        self.stt("dve", v3(g), v3(t1), -1.0, bc16(eal), ALU.mult, ALU.mult, [b_t1, b_eal], [b_g])
        self.act(v3(beta), gd3[:, :, 16:32], AF.Sigmoid, [b_gd], [b_beta])
        P.op("dve", lambda e: e.tensor_scalar_mul(out=nbeta, in0=beta, scalar1=-1.0), [b_beta], [b_nb])
        G = self.gate_prep(g, b_g, 16, cstt, b_cst, identf, b_cst)
        gb = G["b"]
        self.act(t2, G["cum"], AF.Exp, [gb["cum"]], [b_t2])
        self.tt("dve", bg, beta, t2, ALU.mult, [b_beta, b_t2], [b_bg])
        self.tt("dve", t1, G["tot"], G["cum"], ALU.subtract, [gb["tot"], gb["cum"]], [b_t1])
        self.act(ekd, t1, AF.Exp, [b_t1], [b_ekd])
        beta3, nbeta3, bg3, ekd3 = v3(beta), v3(nbeta), v3(bg), v3(ekd)
        decA3, decB3 = v3(G["decA"]), v3(G["decB"])
        cum3 = G["cum3"]
        P.barrier()
        qT = A.bf16(S); kT = A.bf16(S); ktok = A.bf16(S); vtok = A.bf16(S)
        sz = A.f32(S); osum = A.f32(S); ydT = A.bf16(S)
        qg = A.bf16(S); X = A.f32(S); Gam = A.f32(S); GamT = A.f32(S)
        kbg = A.bf16(S); vb = A.bf16(S); kd = A.bf16(S)
        u_all = A.f32(S); wT = A.bf16(S); attnT = A.bf16(S)
        GN = 2
        Xs = [[A.f32(128) for _ in range(2)] for _ in range(GN)]
        Ys = [[A.f32(128) for _ in range(2)] for _ in range(GN)]
        TTs = [[A.f32(128) for _ in range(2)] for _ in range(GN)]
        TTb = [A.bf16(128) for _ in range(GN)]
        Sf = A.f32(128)
        Sb = [A.bf16(128) for _ in range(3)]
        vnew = [A.bf16(128) for _ in range(2)]
        ytmp = [A.f32(128) for _ in range(2)]
        ybf = [A.bf16(128) for _ in range(2)]
        fin = A.f32(4)
        r3 = lambda a: a.rearrange("p (t c) -> p t c", c=128)
        ktok3, vtok3, sz3, os3 = r3(ktok), r3(vtok), r3(sz), r3(osum)
        X3, Gam3, GamT3, kbg3, vb3, kd3, u3, at3 = r3(X), r3(Gam), r3(GamT), r3(kbg), r3(vb), r3(kd), r3(u_all), r3(attnT)
        B = {k: Buf() for k in ("qT", "kT", "ktok", "vtok", "sz", "ydT", "qg", "X", "Gam", "GamT", "kbg", "vb", "kd",
                                "u", "wT", "attnT", "Sf", "bc")}
        b_os = [Buf() for _ in range(NT)]
        b_Xs = [[Buf(), Buf()] for _ in range(GN)]
        b_Ys = [[Buf(), Buf()] for _ in range(GN)]
        b_TTs = [[Buf(), Buf()] for _ in range(GN)]
        b_TTb = [Buf() for _ in range(GN)]
        b_Sb = [Buf() for _ in range(3)]
        b_vn = [Buf(), Buf()]
        b_yt = [Buf(), Buf()]
        b_yb = [Buf(), Buf()]
        b_fin = [Buf(), Buf()]
        dsl = {k: self.ds(k == "ydT") for k in ("qT", "kT", "ktok", "vtok", "sz", "ydT")}
        pbc = ps[:, 0:2048]
        pX = [ps[:, (gi * 3) * 512:(gi * 3) * 512 + 128] for gi in range(GN)]
        pY = [ps[:, (gi * 3 + 1) * 512:(gi * 3 + 1) * 512 + 128] for gi in range(GN)]
        pTT = [ps[:, (gi * 3 + 2) * 512:(gi * 3 + 2) * 512 + 128] for gi in range(GN)]
        b_pX = [Buf() for _ in range(GN)]
        b_pY = [Buf() for _ in range(GN)]
        b_pTT = [Buf() for _ in range(GN)]
        pA = [ps[:, (6 + gi) * 512:(6 + gi) * 512 + 128] for gi in range(GN)]
        b_pA = [Buf() for _ in range(GN)]
        pB = [ps[:, (6 + gi) * 512 + 128:(6 + gi) * 512 + 256] for gi in range(GN)]
        b_pB = [Buf() for _ in range(GN)]
        pCq = [ps[:, (6 + gi) * 512 + 256:(6 + gi) * 512 + 384] for gi in range(GN)]
        b_pCq = [Buf() for _ in range(GN)]
        qg_d = [qg, A.bf16(S)]; kd_d = [kd, A.bf16(S)]; u_d = [u_all, A.f32(S)]; wT_d = [wT, A.bf16(S)]
        at_d = [attnT, A.bf16(S)]
        kd3_d = [r3(a) for a in kd_d]; u3_d = [r3(a) for a in u_d]; at3_d = [r3(a) for a in at_d]
        Bd = [{k: Buf() for k in ("qg", "kd", "u", "wT", "attnT", "Sf")} for _ in range(2)]
        Sf_d = [Sf, A.f32(128)]
        Sb_d = [Sb, [A.bf16(128) for _ in range(3)]]
        b_Sb_d = [b_Sb, [Buf() for _ in range(3)]]
        vnew_d = [vnew, [A.bf16(128) for _ in range(2)]]
        b_vn_d = [b_vn, [Buf(), Buf()]]
        pvn_d = [ps[:, 0:128], ps[:, 512:640]]; b_pvn_d = [Buf(), Buf()]
        pS_d = [ps[:, 128:256], ps[:, 640:768]]; b_pS_d = [Buf(), Buf()]
        po_d = [[ps[:, 1024:1152], ps[:, 1536:1664]], [ps[:, 2048:2176], ps[:, 2560:2688]]]
        b_po_d = [[Buf(), Buf()], [Buf(), Buf()]]
        ptr = ps[:, 3072:3136].bitcast(BF16); b_ptr = Buf()
        for h in range(8):
            P.dma("sp", dsl["qT"], qT, self.S_qTd[h], writes=[B["qT"]])
            P.dma("sp", dsl["kT"], kT, self.S_kTd[h], writes=[B["kT"]])
            P.dma("sp", dsl["ktok"], ktok3, self.S_kd[h], writes=[B["ktok"]])
            P.dma("sp", dsl["vtok"], vtok3, self.S_vd[h], writes=[B["vtok"]])
            P.dma("sp", dsl["sz"], sz3, self.S_sz[h], writes=[B["sz"]])
            for dr in range(2):
                c = dr * 8 + h
                BD = Bd[dr]
                mn = cstt[:, C_MNF:C_MNF + 128] if dr == 0 else cstt[:, C_MNB:C_MNB + 128]
                mp = cstt[:, C_MPF:C_MPF + 128] if dr == 0 else cstt[:, C_MPB:C_MPB + 128]
                for tb in range(4):
                    self.mm(pbc[:, tb * 512:(tb + 1) * 512], sel[0:16, c * 128:(c + 1) * 128],
                            G["cumT"][0:16, tb * 512:(tb + 1) * 512], True, True, [b_sel, gb["cumT"]], pwrites=[B["bc"]])
                self.act(Gam, pbc, AF.Exp, [B["bc"]], [B["Gam"]])
                self.tt("dve", qg_d[dr], qT, Gam, ALU.mult, [B["qT"], B["Gam"]], [BD["qg"]])
                self.tt("dve", X3, pbc.rearrange("p (t c) -> p t c", c=128),
                        cum3[:, :, c:c + 1].to_broadcast([128, NT, 128]), ALU.subtract, [B["bc"], gb["cum"]], [B["X"]])
                self.tt("dve", GamT3, X3, mn.unsqueeze(1).to_broadcast([128, NT, 128]), ALU.min, [B["X"], b_cst], [B["GamT"]])
                self.act(GamT, GamT, AF.Exp, [B["GamT"]], [B["GamT"]])
                self.tt("dve", Gam3, X3, mp.unsqueeze(1).to_broadcast([128, NT, 128]), ALU.max, [B["X"], b_cst, BD["qg"]], [B["Gam"]])
                self.act(Gam, Gam, AF.Exp, [B["Gam"]], [B["Gam"]], scale=-1.0)
                bcc = lambda a3: a3[:, :, c:c + 1].to_broadcast([128, NT, 128])
                self.tt("dve", kbg3, ktok3, bcc(bg3), ALU.mult, [B["ktok"], b_bg], [B["kbg"]])
                self.tt("dve", vb3, vtok3, bcc(beta3), ALU.mult, [B["vtok"], b_beta], [B["vb"]])
                self.tt("dve", kd3_d[dr], ktok3, bcc(ekd3), ALU.mult, [B["ktok"], b_ekd], [BD["kd"]])
                P.barrier()
                for g0 in range(0, NT, GN):
                    for gi in range(GN):
                        t = g0 + gi
                        tok = slice(t * 128, (t + 1) * 128)
                        self.mm(pX[gi], kT[:, tok], kT[:, tok], True, True, [B["kT"]], writes=[b_pX[gi]])
                        self.stt("dve", Xs[gi][0], pX[gi], nbeta3[:, t, c:c + 1], Gam3[:, t, :], ALU.mult, ALU.mult,
                                 [b_pX[gi], b_nb, B["Gam"]], [b_Xs[gi][0]])
                    for gi in range(GN):
                        self.tr(pY[gi], Xs[gi][0], identf, [b_Xs[gi][0], b_cst], writes=[b_pY[gi]])
                    for gi in range(GN):
                        self.cp("dve", Ys[gi][0], pY[gi], [b_pY[gi]], [b_Ys[gi][0]])
                        self.tt("dve", TTs[gi][0], pY[gi], identf, ALU.add, [b_pY[gi], b_cst], [b_TTs[gi][0]])
                    for lv in range(1, 6):
                        a, b_ = (lv - 1) % 2, lv % 2
                        for gi in range(GN):
                            self.mm(pX[gi], Ys[gi][a], Xs[gi][a], True, True, [b_Ys[gi][a], b_Xs[gi][a]], writes=[b_pX[gi]])
                            if lv < 5:
                                self.mm(pY[gi], Xs[gi][a], Ys[gi][a], True, True, [b_Ys[gi][a], b_Xs[gi][a]], writes=[b_pY[gi]])
                        for gi in range(GN):
                            self.act(Xs[gi][b_], pX[gi], AF.Copy, [b_pX[gi]], [b_Xs[gi][b_]])
                            if lv < 5:
                                self.cp("dve", Ys[gi][b_], pY[gi], [b_pY[gi]], [b_Ys[gi][b_]])
                        for gi in range(GN):
                            self.mm(pTT[gi], Xs[gi][b_], TTs[gi][a], True, True, [b_Xs[gi][b_], b_TTs[gi][a]], writes=[b_pTT[gi]])
                        for gi in range(GN):
                            if lv < 5:
                                self.tt("dve", TTs[gi][b_], pTT[gi], TTs[gi][a], ALU.add, [b_pTT[gi], b_TTs[gi][a]], [b_TTs[gi][b_]])
                            else:
                                self.tt("dve", TTb[gi], pTT[gi], TTs[gi][a], ALU.add, [b_pTT[gi], b_TTs[gi][a]], [b_TTb[gi]])
                    for gi in range(GN):
                        t = g0 + gi
                        tok = slice(t * 128, (t + 1) * 128)
                        self.mm(pA[gi], TTb[gi], vb3[:, t, :], True, True, [b_TTb[gi], B["vb"]], writes=[b_pA[gi]])
                        self.act(u3_d[dr][:, t, :], pA[gi], AF.Copy, [b_pA[gi]], pwrites=[BD["u"]])
                        self.mm(pB[gi], kbg3[:, t, :], TTb[gi], True, True, [b_TTb[gi], B["kbg"]], writes=[b_pB[gi]])
                        self.act(wT_d[dr][:, tok], pB[gi], AF.Copy, [b_pB[gi]], pwrites=[BD["wT"]])
                P.barrier()
                for t in range(NT):
                    tok = slice(t * 128, (t + 1) * 128)
                    gi = t % 2
                    pq = ps[:, gi * 512:gi * 512 + 128]
                    self.mm(pq, kT[:, tok], qT[:, tok], True, True, [B["kT"], B["qT"]], writes=[b_pCq[gi]])
                    self.tt("dve", at3_d[dr][:, t, :], pq, GamT3[:, t, :], ALU.mult, [b_pCq[gi], B["GamT"]], pwrites=[BD["attnT"]])
                P.barrier()
            si = [0, 0]
            for dr in range(2):
                self.memset("pool", Sf_d[dr], 0.0, writes=[Bd[dr]["Sf"]])
                self.memset("pool", Sb_d[dr][0], 0.0, writes=[b_Sb_d[dr][0]])
            for it in range(NT):
                j = it % 2
                tl = (it, NT - 1 - it)
                for step in range(2):
                    for dr in range(2):
                        BD = Bd[dr]
                        c = dr * 8 + h
                        t = tl[dr]
                        ck = step if dr == 0 else 1 - step
                        rows = slice(ck * 64, ck * 64 + 64)
                        ctok = slice(t * 128 + ck * 64, t * 128 + ck * 64 + 64)
                        Sbc = Sb_d[dr][si[dr]]
                        bSbc = b_Sb_d[dr][si[dr]]
                        vn_ = vnew_d[dr][j]
                        self.mm(pvn_d[dr][rows, :], wT_d[dr][:, ctok], Sbc, True, True, [BD["wT"], bSbc], pwrites=[b_pvn_d[dr]])
                        self.tt("dve", vn_[rows, :], u3_d[dr][rows, t, :], pvn_d[dr][rows, :], ALU.subtract,
                                [BD["u"], b_pvn_d[dr]], pwrites=[b_vn_d[dr][j]])
                        self.mm(po_d[dr][j][rows, :], qg_d[dr][:, ctok], Sbc, True, False, [BD["qg"], bSbc], pwrites=[b_po_d[dr][j]])
                        self.mm(pS_d[dr], kd3_d[dr][rows, t, :], vn_[rows, :], True, True, [BD["kd"], b_vn_d[dr][j]], writes=[b_pS_d[dr]])
                        dec = (decA3 if ck == 0 else decB3)[:, t, c:c + 1]
                        bdec = gb["decA"] if ck == 0 else gb["decB"]
                        sn = (si[dr] + 1) % 3
                        self.stt("dve", Sb_d[dr][sn], Sf_d[dr], dec, pS_d[dr], ALU.mult, ALU.add,
                                 [BD["Sf"], bdec, b_pS_d[dr]], [b_Sb_d[dr][sn]])
                        self.stt("dve", Sf_d[dr], Sf_d[dr], dec, pS_d[dr], ALU.mult, ALU.add,
                                 [BD["Sf"], bdec, b_pS_d[dr]], [BD["Sf"]])
                        si[dr] = sn
                for dr in range(2):
                    t = tl[dr]
                    tok = slice(t * 128, (t + 1) * 128)
                    self.mm(po_d[dr][j], at3_d[dr][:, t, :], vnew_d[dr][j], False, True, [Bd[dr]["attnT"], b_vn_d[dr][j]],
                            pwrites=[b_po_d[dr][j]])
                    if it < NT // 2:
                        self.act(os3[:, t, :], po_d[dr][j], AF.Copy, [b_po_d[dr][j]], [b_os[t]])
                    else:
                        self.tt("dve", os3[:, t, :], po_d[dr][j], os3[:, t, :], ALU.add, [b_po_d[dr][j], b_os[t]], [b_os[t]])
                        jj = dr
                        fs = fin[:, 2 * jj:2 * jj + 1]
                        fr = fin[:, 2 * jj + 1:2 * jj + 2]
                        self.act(ytmp[jj], os3[:, t, :], AF.Square, [b_os[t]], [b_yt[jj], b_fin[jj]], accum_out=fs)
                        self.rsqrt_col(fr, fs, 1.0 / 128, self.eps6, [b_fin[jj], self.b_eps], b_fin[jj])
                        self.stt("dve", ytmp[jj], os3[:, t, :], fr, ngd, ALU.mult, ALU.mult, [b_os[t], b_fin[jj], b_ngd], [b_yt[jj]])
                        self.tt("pool", ybf[jj], ytmp[jj], sz3[:, t, :], ALU.mult, [b_yt[jj], B["sz"]], [b_yb[jj]])
                        self.tr(ptr, ybf[jj], self.identb, [b_yb[jj], self.b_identb], writes=[b_ptr])
                        self.cp("dve", ydT[:, tok], ptr, [b_ptr], pwrites=[B["ydT"]])
            P.barrier()
            P.dma("pool", dsl["ydT"], self.S_ydT[h], ydT, reads=[B["ydT"]])
            P.barrier()

    def phase5(self):
        P, A, ps = self.P, self.A, self.psum
        self.common()
        self.b_prm = []
        ym = A.bf16(8 * S); yd = A.bf16(8 * S)
        b_ym, b_yd = Buf(), Buf()
        ym3 = ym.rearrange("p (c t) -> p c t", t=S)
        yd3 = yd.rearrange("p (c t) -> p c t", t=S)
        self.load(ym3, self.S_ymT.rearrange("c p t -> p c t"), b_ym)
        self.load(yd3, self.S_ydT.rearrange("c p t -> p c t"), b_yd)
        self.wring_init(4)
        sg = [[A.f32(S) for _ in range(2)] for _ in range(2)]
        b_sg = [[Buf(), Buf()] for _ in range(2)]
        ds_sg = [[self.ds(), self.ds()] for _ in range(2)]
        tmp = [A.f32(S) for _ in range(2)]
        b_tmp = [Buf(), Buf()]
        mixb = [A.bf16(S) for _ in range(2)]
        b_mix = [Buf(), Buf()]
        ds_mix = [self.ds(True), self.ds(True)]
        pm = ps[:, 0:2048]; pd = ps[:, 2048:4096]
        b_pm, b_pd = Buf(), Buf()
        wm3 = self.w_bm.rearrange("(c p) n -> p c n", p=128)
        wd3 = self.w_bd.rearrange("(c p) n -> p c n", p=128)
        for cg in range(16):
            i = cg % 2
            wm, b_wm = self.wload(wm3[:, :, cg * 128:(cg + 1) * 128], 8, 128)
            wd, b_wd = self.wload(wd3[:, :, cg * 128:(cg + 1) * 128], 8, 128, eng="pool")
            P.dma("sp", ds_sg[i][0], sg[i][0], self.S_sg[cg], writes=[b_sg[i][0]])
            P.dma("sp", ds_sg[i][1], sg[i][1], self.S_sg[16 + cg], writes=[b_sg[i][1]])
            for tb in range(4):
                for kc in range(8):
                    self.mm(pm[:, tb * 512:(tb + 1) * 512], wm[:, kc, :], ym3[:, kc, tb * 512:(tb + 1) * 512],
                            kc == 0, kc == 7, [b_wm, b_ym], pwrites=[b_pm])
            for tb in range(4):
                for kc in range(8):
                    self.mm(pd[:, tb * 512:(tb + 1) * 512], wd[:, kc, :], yd3[:, kc, tb * 512:(tb + 1) * 512],
                            kc == 0, kc == 7, [b_wd, b_yd], pwrites=[b_pd])
            self.tt("dve", tmp[i], pm, sg[i][0], ALU.mult, [b_pm, b_sg[i][0]], [b_tmp[i]])
            self.tt("dve", sg[i][1], pd, sg[i][1], ALU.mult, [b_pd, b_sg[i][1]], [b_sg[i][1]])
            self.tt("pool", mixb[i], tmp[i], sg[i][1], ALU.add, [b_tmp[i], b_sg[i][1]], [b_mix[i]])
            P.dma("pool", ds_mix[i], self.S_mixT[cg], mixb[i], reads=[b_mix[i]])

    def phase67(self):
        P, A, ps = self.P, self.A, self.psum
        self.common()
        g2, b_g2 = self.load_prm(P_G2, 16)
        self.b_prm = [b_g2]
        gF, b_gF = self.load_prm(P_GF, 2048)
        identf, b_idf = self.load_cst(C_IDENT, 128)
        self.wring_init(4)
        mix = A.bf16(16 * 512); b_mixb = Buf(); ds_mixl = self.ds()
        mix3 = mix.rearrange("p (c t) -> p c t", t=512)
        h2T3 = mix3
        b_h2T = b_mixb
        x2 = A.f32(4 * D); b_x2 = [Buf() for _ in range(4)]; ds_x2 = [self.ds() for _ in range(4)]
        x23 = x2.rearrange("p (t c) -> p t c", c=D)
        actT = A.bf16(64 * 512); b_actT = Buf()
        actT3 = actT.rearrange("p (c t) -> p c t", t=512)
        junk = A.bf16(D); b_junk = Buf()
        xn1 = A.bf16(D); b_xn1 = Buf()
        st = A.f32(16); b_st = [Buf() for _ in range(4)]
        rl = [A.f32(512) for _ in range(2)]; b_rl = [Buf(), Buf()]
        ds_yo = [self.ds(True) for _ in range(4)]
        pf = [ps[:, m * 512:(m + 1) * 512] for m in range(4)]; b_pf = [Buf() for _ in range(4)]
        pacc = [pf[0], pf[1]]; b_pacc = [b_pf[0], b_pf[1]]
        ptrF = ps[:, 2048:2560]; b_ptrF = Buf()
        pT1 = ps[:, 2560:3584].bitcast(BF16); b_pT1 = Buf()
        wo3 = self.w_out.rearrange("(c p) n -> p c n", p=128)
        w13 = self.w_ff1.rearrange("(c p) n -> p c n", p=128)
        w23 = self.w_ff2.rearrange("(c p) n -> p c n", p=128)
        ai = 0
        wi = 0
        for tb in range(4):
            P.dma("sp", ds_mixl, mix3, self.S_mixT.rearrange("c p t -> p c t")[:, :, tb * 512:(tb + 1) * 512], writes=[b_mixb])
            for tt_ in range(4):
                tg = tb * 4 + tt_
                P.dma("sp", ds_x2[tt_], x23[:, tt_, :], self.x[tg * 128:(tg + 1) * 128, :], writes=[b_x2[tt_]])
            for cu in range(16):
                wbf, b_w = self.wload(wo3[:, :, cu * 128:(cu + 1) * 128], 16, 128, eng=("dve" if cu % 2 == 0 else "pool"))
                a = ai % 2
                ai += 1
                for tt_ in range(4):
                    for kc in range(16):
                        self.mm(pacc[a][:, tt_ * 128:(tt_ + 1) * 128], mix3[:, kc, tt_ * 128:(tt_ + 1) * 128], wbf[:, kc, :],
                                kc == 0, kc == 15, [b_mixb, b_w], pwrites=[b_pacc[a]])
                self.tt("dve", x23[:, :, cu * 128:(cu + 1) * 128], pacc[a][:, 0:512].rearrange("p (t c) -> p t c", c=128),
                        x23[:, :, cu * 128:(cu + 1) * 128], ALU.add, [b_pacc[a]] + b_x2, pwrites=b_x2)
            for tt_ in range(4):
                self.act(junk, x23[:, tt_, :], AF.Square, [b_x2[tt_]], [b_junk, b_st[tt_]], accum_out=st[:, tt_:tt_ + 1])
                self.rsqrt_col(st[:, 4 + tt_:5 + tt_], st[:, tt_:tt_ + 1], 1.0 / D, self.eps6, [b_st[tt_], self.b_eps], b_st[tt_])
                self.act(xn1, x23[:, tt_, :], AF.Copy, [b_x2[tt_], b_st[tt_]], [b_xn1], scale=st[:, 4 + tt_:5 + tt_])
                for dc in range(16):
                    self.tr(pT1[:, dc * 128:(dc + 1) * 128], xn1[:, dc * 128:(dc + 1) * 128], self.identb,
                            [b_xn1, self.b_identb], pwrites=[b_pT1])
                if tt_ == 0:
                    self.cp("dve", h2T3[:, :, 0:128], pT1.rearrange("p (c t) -> p c t", t=128), [b_pT1], writes=[b_h2T])
                else:
                    self.cp("dve", h2T3[:, :, tt_ * 128:(tt_ + 1) * 128], pT1.rearrange("p (c t) -> p c t", t=128),
                            [b_pT1], pwrites=[b_h2T])
            for fg in range(16):
                for kp in range(4):
                    wi += 1
                    wbf, b_w = self.wload(w13[:, kp * 4:(kp + 1) * 4, fg * 512:(fg + 1) * 512], 4, 512,
                                          fold=g2[:, kp * 4:(kp + 1) * 4], eng=("dve" if wi % 2 == 0 else "pool"))
                    for m in range(4):
                        for dl in range(4):
                            dc = kp * 4 + dl
                            self.mm(pf[m], wbf[:, dl, m * 128:(m + 1) * 128], h2T3[:, dc, :], dc == 0, dc == 15,
                                    [b_w, b_h2T], pwrites=[b_pf[m]])
                for m in range(4):
                    fu = fg * 4 + m
                    r = m % 2
                    self.act(rl[r], pf[m], AF.Relu, [b_pf[m]], [b_rl[r]])
                    self.tt("dve" if m % 2 == 1 else "pool", actT3[:, fu, :], rl[r], rl[r], ALU.mult, [b_rl[r]], pwrites=[b_actT])
            for cg in range(4):
                for piece in range(16):
                    wi += 1
                    wbf, b_w = self.wload(w23[:, piece * 4:(piece + 1) * 4, cg * 512:(cg + 1) * 512], 4, 512,
                                          eng=("dve" if wi % 2 == 0 else "pool"))
                    for m in range(4):
                        for kl in range(4):
                            kc = piece * 4 + kl
                            self.mm(pf[m], wbf[:, kl, m * 128:(m + 1) * 128], actT3[:, kc, :], kc == 0, kc == 63,
                                    [b_actT, b_w], pwrites=[b_pf[m]])
                for m in range(4):
                    cu = cg * 4 + m
                    r = m % 2
                    self.act(rl[r], pf[m], AF.Copy, [b_pf[m]], [b_rl[r]])
                    for tt_ in range(4):
                        self.tr(ptrF[:, tt_ * 128:(tt_ + 1) * 128], rl[r][:, tt_ * 128:(tt_ + 1) * 128], identf,
                                [b_rl[r], b_idf], pwrites=[b_ptrF])
                    self.tt("dve", x23[:, :, cu * 128:(cu + 1) * 128], ptrF.rearrange("p (t c) -> p t c", c=128),
                            x23[:, :, cu * 128:(cu + 1) * 128], ALU.add, [b_ptrF] + b_x2, pwrites=b_x2)
            for tt_ in range(4):
                tg = tb * 4 + tt_
                self.act(junk, x23[:, tt_, :], AF.Square, [b_x2[tt_]], [b_junk, b_st[tt_]], accum_out=st[:, 8 + tt_:9 + tt_])
                self.rsqrt_col(st[:, 12 + tt_:13 + tt_], st[:, 8 + tt_:9 + tt_], 1.0 / D, self.eps6, [b_st[tt_], self.b_eps], b_st[tt_])
                self.stt("dve", x23[:, tt_, :], x23[:, tt_, :], st[:, 12 + tt_:13 + tt_], gF, ALU.mult, ALU.mult,
                         [b_x2[tt_], b_st[tt_], b_gF], [b_x2[tt_]])
                P.dma("pool", ds_yo[tt_], self.y[tg * 128:(tg + 1) * 128, :], x23[:, tt_, :], reads=[b_x2[tt_]])


_CACHE = {}


def _build(debug=False, upto=99):
    key = (debug, upto)
    if key not in _CACHE:
        nc = bass.Bass("TRN2", target_bir_lowering=False)
        k = Kern(nc, debug=debug, upto=upto)
        k.build()
        _CACHE[key] = (nc, k)
    return _CACHE[key]


def _in_maps(inp):
    f = np.float32
    x = np.asarray(inp["x"], f)
    shared = {
        "w_in": np.ascontiguousarray(np.asarray(inp["w_in"], f)[0]),
        "w_bm": np.ascontiguousarray(np.asarray(inp["w_branch_m"], f)[0]),
        "w_bd": np.ascontiguousarray(np.asarray(inp["w_branch_d"], f)[0]),
        "w_out": np.ascontiguousarray(np.asarray(inp["w_out"], f)[0]),
        "w_ff1": np.ascontiguousarray(np.asarray(inp["w_ff1"], f)[0]),
        "w_ff2": np.ascontiguousarray(np.asarray(inp["w_ff2"], f)[0]),
        "prm": _params(inp),
        "cst": _consts(),
        "cstb": np.eye(128).astype(ml_dtypes.bfloat16),
    }
    maps = []
    for b in range(8):
        m = dict(shared)
        m["x"] = np.ascontiguousarray(x[b])
        maps.append(m)
    return maps


def kernel(**inputs):
    nc, _ = _build()
    res = run_bass_kernel_spmd(nc, _in_maps(inputs), core_ids=list(range(8)))
    return np.stack([np.asarray(r["y"], np.float32) for r in res.results], axis=0)
```
